# Optimizing a Trainium2 kernel written in Bass

```python
import math
import jax
import jax.numpy as jnp
from jax import lax
import numpy as np

D_MODEL = 2048
BATCH = 16
SEQ = 256
DEPTH = 2
DEC_BATCH = 8
DEC_SEQ = 2048
PAST_LEN = 256

GRID_W = 64
W_RWKV = 512
W_HYENA = 512
W_ATTN = 1024
RWKV_HEAD = 64
N_RWKV_HEADS = W_RWKV // RWKV_HEAD
HEAD_DIM = 64
N_Q_HEADS = W_ATTN // HEAD_DIM
N_KV_HEADS = 4
KV_GROUP = N_Q_HEADS // N_KV_HEADS
KV_WIDTH = N_KV_HEADS * HEAD_DIM
LORA_W = 64
LORA_A = 64
LORA_G = 128
RWKV_SIZES = (W_RWKV, W_RWKV, W_RWKV, LORA_W, LORA_W, LORA_A, LORA_A, LORA_G)
RWKV_SPLITS = tuple(sum(RWKV_SIZES[:i + 1]) for i in range(len(RWKV_SIZES) - 1))
RWKV_COLS = sum(RWKV_SIZES)
HYENA_COLS = 3 * W_HYENA
ATTN_COLS = W_ATTN + 2 * KV_WIDTH
IN_COLS = RWKV_COLS + HYENA_COLS + ATTN_COLS
HYENA_BANDS = 16
HYENA_EMB = 1 + 2 * HYENA_BANDS
HYENA_FFN = 64
D_FF = 5632
Q_BLOCK = 128
ROPE_THETA = 10000.0
ALPHA = (2 * DEPTH) ** 0.25
BETA = (8 * DEPTH) ** -0.25
LN_EPS = 1e-5
QK_EPS = 1e-6
GN_EPS = 64e-5

kernel_name = 'hybrid_rwkv7_hyena_gqa_diffusion_step'


def layer_norm(x, g, b):
    xf = x.astype(jnp.float32)
    mu = jnp.mean(xf, -1, keepdims=True)
    var = jnp.mean(jnp.square(xf - mu), -1, keepdims=True)
    return ((xf - mu) * lax.rsqrt(var + LN_EPS) * g + b).astype(x.dtype)


def rms_norm(x, g):
    xf = x.astype(jnp.float32)
    return (xf * lax.rsqrt(jnp.mean(jnp.square(xf), -1, keepdims=True) + QK_EPS) * g).astype(x.dtype)


def dwconv3(x, w):
    xp = jnp.pad(x, ((0, 0), (1, 1), (0, 0)))
    return xp[:, :-2] * w[0] + xp[:, 1:-1] * w[1] + xp[:, 2:] * w[2]


def axial_rope(n_tokens):
    f32 = jnp.float32
    rows = n_tokens // GRID_W
    row = jnp.repeat(jnp.arange(rows, dtype=f32), GRID_W)
    col = jnp.tile(jnp.arange(GRID_W, dtype=f32), rows)
    n_freq = HEAD_DIM // 4
    inv = ROPE_THETA ** (-jnp.arange(n_freq, dtype=f32) / n_freq)
    ang = jnp.concatenate([row[:, None] * inv, col[:, None] * inv], -1)
    return jnp.cos(ang), jnp.sin(ang)


def apply_rope(x, cos, sin):
    xf = x.astype(jnp.float32)
    half = HEAD_DIM // 2
    x1, x2 = xf[..., :half], xf[..., half:]
    c, s = cos[None, :, None, :], sin[None, :, None, :]
    return jnp.concatenate([x1 * c - x2 * s, x2 * c + x1 * s], -1).astype(x.dtype)


def wkv_scan(s0, r, w, k, v, kk, a, reverse):
    def step(S, inp):
        r_t, w_t, k_t, v_t, kk_t, a_t = inp
        sa = jnp.einsum('bhvk,bhk->bhv', S, -kk_t)
        S = (S * w_t[:, :, None, :] + sa[..., None] * (kk_t * a_t)[:, :, None, :]
             + v_t[..., None] * k_t[:, :, None, :])
        return S, jnp.einsum('bhvk,bhk->bhv', S, r_t)
    xs = tuple(jnp.swapaxes(t, 0, 1) for t in (r, w, k, v, kk, a))
    s_fin, ys = lax.scan(step, s0, xs, reverse=reverse)
    return s_fin, jnp.swapaxes(ys, 0, 1)


def rwkv_mix(p, lp, s0_f, s0_b):
    f32 = jnp.float32
    p = dwconv3(p, lp['rwkv_shift'])
    r, k, v, wd_f, wd_b, ad_f, ad_b, gd = jnp.split(p, RWKV_SPLITS, axis=-1)
    B, L, _ = r.shape

    def heads(t):
        return t.astype(f32).reshape(B, L, N_RWKV_HEADS, RWKV_HEAD)

    g = jax.nn.sigmoid(gd) @ lp['rwkv_g2']
    kk = heads(k * lp['rwkv_kk'])
    kk = kk * lax.rsqrt(jnp.sum(jnp.square(kk), -1, keepdims=True) + 1e-12)
    rh, vh = heads(r), heads(v)
    r_k = lp['rwkv_rk'].astype(f32).reshape(N_RWKV_HEADS, RWKV_HEAD)
    y = 0.0
    bonus = 0.0
    finals = []
    for d, (wd, ad, s0, rev) in enumerate(((wd_f, ad_f, s0_f, False), (wd_b, ad_b, s0_b, True))):
        w_log = -jax.nn.softplus(-(lp['rwkv_w0'][d] + jnp.tanh(wd) @ lp['rwkv_w2'][d])) - 0.5
        decay = jnp.exp(-jnp.exp(heads(w_log)))
        a = jax.nn.sigmoid(lp['rwkv_a0'][d] + ad @ lp['rwkv_a2'][d])
        kd = heads(k * (1.0 + (a - 1.0) * lp['rwkv_ka']))
        s_fin, y_d = wkv_scan(s0.astype(f32), rh, decay, kd, vh, kk, heads(a), rev)
        y = y + y_d
        bonus = bonus + jnp.sum(rh * kd * r_k, -1, keepdims=True) * vh
        finals.append(s_fin)
    mu = jnp.mean(y, -1, keepdims=True)
    var = jnp.mean(jnp.square(y - mu), -1, keepdims=True)
    y = ((y - mu) * lax.rsqrt(var + GN_EPS)).reshape(B, L, W_RWKV) * lp['rwkv_gn_g'] + lp['rwkv_gn_b']
    out = (y + bonus.reshape(B, L, W_RWKV)) * g
    return out.astype(p.dtype), finals[0], finals[1]


def hyena_filters(n, lp):
    f32 = jnp.float32
    t01 = jnp.linspace(0.0, 1.0, n, dtype=f32)[:, None]
    pos = jnp.arange(n, dtype=f32)[:, None]
    bands = jnp.linspace(1e-4, HYENA_BANDS - 1, HYENA_BANDS, dtype=f32)[None, :]
    ang = (2.0 * math.pi / n) * pos * bands
    z = jnp.concatenate([t01, jnp.cos(ang), -jnp.sin(ang)], -1)
    freq = lp['hy_freq'].astype(f32)
    h = jnp.sin(freq[0] * (z @ lp['hy_w1'].astype(f32) + lp['hy_b1']))
    h = jnp.sin(freq[1] * (h @ lp['hy_w2'].astype(f32) + lp['hy_b2']))
    h = (h @ lp['hy_w3'].astype(f32)).reshape(n, 2, W_HYENA)
    h = h * jnp.exp(-t01[:, :, None] * jnp.abs(lp['hy_decay'].astype(f32))[None])
    return h[:, 0], h[:, 1]


def hyena_mix(p, lp):
    f32 = jnp.float32
    p = dwconv3(p, lp['hy_short'])
    x0, x1, v = jnp.split(p, 3, axis=-1)
    n = p.shape[1]
    h_fwd, h_bwd = hyena_filters(n, lp)
    filt_full = jnp.concatenate([h_fwd[:1] + h_bwd[:1], h_fwd[1:],
                                 jnp.zeros((1, W_HYENA), f32), h_bwd[:0:-1]], 0)
    u = (x1 * v).astype(f32)
    y = jnp.fft.irfft(jnp.fft.rfft(u, n=2 * n, axis=1) * jnp.fft.rfft(filt_full, axis=0)[None],
                      n=2 * n, axis=1)[:, :n]
    y = y + u * lp['hy_bias'].astype(f32)
    return (x0.astype(f32) * y).astype(p.dtype)


def block_attention(q, k, v):
    B, Lq = q.shape[0], q.shape[1]
    nb = Lq // Q_BLOCK
    qb = jnp.moveaxis(q.reshape(B, nb, Q_BLOCK, N_KV_HEADS, KV_GROUP, HEAD_DIM), 1, 0)
    scale = HEAD_DIM ** -0.5

    def one(qblk):
        s = jnp.einsum('bqhgd,bkhd->bhgqk', qblk, k, preferred_element_type=jnp.float32) * scale
        pr = jax.nn.softmax(s, axis=-1)
        return jnp.einsum('bhgqk,bkhd->bqhgd', pr.astype(v.dtype), v)

    out = lax.map(one, qb)
    return jnp.moveaxis(out, 0, 1).reshape(B, Lq, N_Q_HEADS * HEAD_DIM)


def attention_mix(p, lp, rope, ctx_k, ctx_v):
    q, k, v = jnp.split(p, [W_ATTN, W_ATTN + KV_WIDTH], axis=-1)
    B, L, _ = q.shape
    q = rms_norm(q.reshape(B, L, N_Q_HEADS, HEAD_DIM), lp['attn_qn'])
    k = rms_norm(k.reshape(B, L, N_KV_HEADS, HEAD_DIM), lp['attn_kn'])
    v = v.reshape(B, L, N_KV_HEADS, HEAD_DIM)
    if rope is None:
        return block_attention(q, k, v), k, v
    cos, sin = rope
    q = apply_rope(q, cos, sin)
    k = apply_rope(k, cos, sin)
    keys = jnp.concatenate([k, ctx_k.astype(k.dtype)], 1)
    vals = jnp.concatenate([v, ctx_v.astype(v.dtype)], 1)
    return block_attention(q, keys, vals), None, None


def trunk_layer(x, cond, lp, rope, ctx_k, ctx_v, s0_f, s0_b):
    mod = jax.nn.silu(cond) @ lp['w_mod'] + lp['b_mod']
    sh1, sc1, g1, sh2, sc2, g2 = jnp.split(mod[:, None, :], 6, axis=-1)
    h = x * (1.0 + sc1) + sh1
    proj = h @ lp['w_in']
    p_r, p_h, p_a = jnp.split(proj, [RWKV_COLS, RWKV_COLS + HYENA_COLS], axis=-1)
    y_r, s_f, s_b = rwkv_mix(p_r, lp, s0_f, s0_b)
    y_h = hyena_mix(p_h, lp)
    y_a, k_c, v_c = attention_mix(p_a, lp, rope, ctx_k, ctx_v)
    mix = jnp.concatenate([y_r, y_h.astype(y_r.dtype), y_a.astype(y_r.dtype)], -1) @ lp['w_out']
    x = layer_norm(ALPHA * x + g1 * mix, lp['ln1_g'], lp['ln1_b'])
    h2 = x * (1.0 + sc2) + sh2
    u = dwconv3(h2 @ lp['ffn_up'], lp['ffn_conv'])
    a, b = jnp.split(u, 2, axis=-1)
    f = (jax.nn.silu(a) * b) @ lp['ffn_down']
    x = layer_norm(ALPHA * x + g2 * f, lp['ln2_g'], lp['ln2_b'])
    return x, k_c, v_c, s_f, s_b


def setup_inputs(seed: int = 0) -> dict:
    key = jax.random.key(seed)
    keys = iter(jax.random.split(key, 48))
    f32 = jnp.float32

    def nrm(shape, scale=1.0):
        return scale * jax.random.normal(next(keys), shape, f32)

    def taps3(side, centre, n, noise):
        base = jnp.array([side, centre, side], f32)[None, :, None]
        return base + nrm((DEPTH, 3, n), noise)

    w0_base = jnp.linspace(-6.0, 1.0, W_RWKV, dtype=f32)
    decay_base = jnp.linspace(abs(math.log(1e-2)) / 1.5, abs(math.log(1e-2)) / 0.3, W_HYENA, dtype=f32)
    return {
        'x_prompt': nrm((BATCH, SEQ, D_MODEL)),
        'x_sample': nrm((DEC_BATCH, DEC_SEQ, D_MODEL)),
        'cache_k': nrm((DEC_BATCH, DEPTH, PAST_LEN, N_KV_HEADS, HEAD_DIM)),
        'cache_v': nrm((DEC_BATCH, DEPTH, PAST_LEN, N_KV_HEADS, HEAD_DIM)),
        'state_rwkv': nrm((DEC_BATCH, DEPTH, 2, N_RWKV_HEADS, RWKV_HEAD, RWKV_HEAD), 0.5),
        'c': nrm((DEC_BATCH, D_MODEL)),
        'c_ctx': nrm((D_MODEL,)),
        'w_mod': nrm((DEPTH, D_MODEL, 6 * D_MODEL), 0.5 * D_MODEL ** -0.5),
        'b_mod': nrm((DEPTH, 6 * D_MODEL), 0.02),
        'w_in': nrm((DEPTH, D_MODEL, IN_COLS), D_MODEL ** -0.5),
        'rwkv_shift': taps3(0.2, 0.6, RWKV_COLS, 0.05),
        'rwkv_w0': w0_base + nrm((DEPTH, 2, W_RWKV), 0.1),
        'rwkv_w2': nrm((DEPTH, 2, LORA_W, W_RWKV), 0.1),
        'rwkv_a0': nrm((DEPTH, 2, W_RWKV), 0.5),
        'rwkv_a2': nrm((DEPTH, 2, LORA_A, W_RWKV), 0.1),
        'rwkv_kk': 0.85 + nrm((DEPTH, W_RWKV), 0.05),
        'rwkv_ka': 1.0 + nrm((DEPTH, W_RWKV), 0.05),
        'rwkv_rk': nrm((DEPTH, W_RWKV), 0.1),
        'rwkv_g2': nrm((DEPTH, LORA_G, W_RWKV), LORA_G ** -0.5),
        'rwkv_gn_g': 1.0 + nrm((DEPTH, W_RWKV), 0.02),
        'rwkv_gn_b': nrm((DEPTH, W_RWKV), 0.02),
        'hy_short': taps3(0.25, 1.0, HYENA_COLS, 0.1),
        'hy_w1': nrm((DEPTH, HYENA_EMB, HYENA_FFN), HYENA_EMB ** -0.5),
        'hy_b1': nrm((DEPTH, HYENA_FFN), 0.1),
        'hy_freq': 1.0 + nrm((DEPTH, 2, HYENA_FFN), 0.1),
        'hy_w2': nrm((DEPTH, HYENA_FFN, HYENA_FFN), HYENA_FFN ** -0.5),
        'hy_b2': nrm((DEPTH, HYENA_FFN), 0.1),
        'hy_w3': nrm((DEPTH, HYENA_FFN, 2 * W_HYENA), 0.05 * HYENA_FFN ** -0.5),
        'hy_decay': decay_base + nrm((DEPTH, 2, W_HYENA), 0.1),
        'hy_bias': nrm((DEPTH, W_HYENA), 0.5),
        'attn_qn': 1.0 + nrm((DEPTH, HEAD_DIM), 0.02),
        'attn_kn': 1.0 + nrm((DEPTH, HEAD_DIM), 0.02),
        'w_out': nrm((DEPTH, D_MODEL, D_MODEL), BETA * D_MODEL ** -0.5),
        'ln1_g': 1.0 + nrm((DEPTH, D_MODEL), 0.02),
        'ln1_b': nrm((DEPTH, D_MODEL), 0.02),
        'ln2_g': 1.0 + nrm((DEPTH, D_MODEL), 0.02),
        'ln2_b': nrm((DEPTH, D_MODEL), 0.02),
        'ffn_up': nrm((DEPTH, D_MODEL, 2 * D_FF), D_MODEL ** -0.5),
        'ffn_conv': taps3(0.2, 1.0, 2 * D_FF, 0.05),
        'ffn_down': nrm((DEPTH, D_FF, D_MODEL), BETA * D_FF ** -0.5),
    }


def reference(x_prompt, x_sample, cache_k, cache_v, state_rwkv, c, c_ctx,
              w_mod, b_mod, w_in, rwkv_shift, rwkv_w0, rwkv_w2, rwkv_a0, rwkv_a2,
              rwkv_kk, rwkv_ka, rwkv_rk, rwkv_g2, rwkv_gn_g, rwkv_gn_b,
              hy_short, hy_w1, hy_b1, hy_freq, hy_w2, hy_b2, hy_w3, hy_decay, hy_bias,
              attn_qn, attn_kn, w_out, ln1_g, ln1_b, ln2_g, ln2_b,
              ffn_up, ffn_conv, ffn_down):
    n_ctx_req = x_prompt.shape[0]
    zero_state = jnp.zeros((n_ctx_req, N_RWKV_HEADS, RWKV_HEAD, RWKV_HEAD), jnp.float32)
    rope = axial_rope(x_sample.shape[1])
    ctx_cond = c_ctx[None, :]
    xp = x_prompt
    xs = x_sample
    new_k, new_v, new_s = [], [], []
    for l in range(DEPTH):
        lp = {
            'w_mod': w_mod[l], 'b_mod': b_mod[l], 'w_in': w_in[l],
            'rwkv_shift': rwkv_shift[l], 'rwkv_w0': rwkv_w0[l], 'rwkv_w2': rwkv_w2[l],
            'rwkv_a0': rwkv_a0[l], 'rwkv_a2': rwkv_a2[l], 'rwkv_kk': rwkv_kk[l],
            'rwkv_ka': rwkv_ka[l], 'rwkv_rk': rwkv_rk[l], 'rwkv_g2': rwkv_g2[l],
            'rwkv_gn_g': rwkv_gn_g[l], 'rwkv_gn_b': rwkv_gn_b[l],
            'hy_short': hy_short[l], 'hy_w1': hy_w1[l], 'hy_b1': hy_b1[l],
            'hy_freq': hy_freq[l], 'hy_w2': hy_w2[l], 'hy_b2': hy_b2[l],
            'hy_w3': hy_w3[l], 'hy_decay': hy_decay[l], 'hy_bias': hy_bias[l],
            'attn_qn': attn_qn[l], 'attn_kn': attn_kn[l], 'w_out': w_out[l],
            'ln1_g': ln1_g[l], 'ln1_b': ln1_b[l], 'ln2_g': ln2_g[l], 'ln2_b': ln2_b[l],
            'ffn_up': ffn_up[l], 'ffn_conv': ffn_conv[l], 'ffn_down': ffn_down[l],
        }
        xp, k_c, v_c, s_f, s_b = trunk_layer(xp, ctx_cond, lp, None, None, None, zero_state, zero_state)
        new_k.append(k_c)
        new_v.append(v_c)
        new_s.append(jnp.stack([s_f, s_b], axis=1))
        xs, _, _, _, _ = trunk_layer(xs, c, lp, rope, cache_k[:, l], cache_v[:, l],
                                     state_rwkv[:, l, 0], state_rwkv[:, l, 1])
    new_cache_k = jnp.stack(new_k, axis=1)
    new_cache_v = jnp.stack(new_v, axis=1)
    new_state_rwkv = jnp.stack(new_s, axis=1)
    return (xp, xs, new_cache_k, new_cache_v, new_state_rwkv)
```

```python
import math
import os
from contextlib import ExitStack
import numpy as np
import ml_dtypes
import concourse.bass as bass
import concourse.mybir as mybir
from concourse.bass_utils import run_bass_kernel_spmd

F32 = mybir.dt.float32
F32R = mybir.dt.float32r
BF16 = mybir.dt.bfloat16
AF = mybir.ActivationFunctionType
ALU = mybir.AluOpType
AX = mybir.AxisListType

D = 2048
DEPTH = 2
NTOK = 2560
LP = 256
LS = 2048
INC = 4992
DFF = 5632
ALPHA = (2 * DEPTH) ** 0.25
MAGIC = 12582912.0
TWO_PI = 2.0 * math.pi


class Buf:
    __slots__ = ("w", "r", "name", "sb", "dsem")

    def __init__(self, name="", sb=False):
        self.w = {}
        self.r = {}
        self.name = name
        self.sb = sb
        self.dsem = None


NDSEM = 96


class Ctx:
    def __init__(self):
        self.nc = bass.Bass("TRN2", target_bir_lowering=False)
        nc = self.nc
        self.es = ExitStack()
        self.eng = {"pe": nc.tensor, "dve": nc.vector, "act": nc.scalar, "pool": nc.gpsimd, "sp": nc.sync}
        self.sem = {}
        self.cnt = {}
        self.seen = {e: {} for e in self.eng}
        for p in ["pe", "dve", "act", "pool"]:
            self.sem[p] = self.es.enter_context(nc.semaphore("s_" + p))
            self.cnt[p] = 0
        self.dfree = []
        for i in range(NDSEM):
            k = ("d", i)
            self.sem[k] = self.es.enter_context(nc.semaphore("s_d%d" % i))
            self.cnt[k] = 0
            self.dfree.append(k)
        self.nuid = 0

    def uid(self, s):
        self.nuid += 1
        return "%s_%d" % (s, self.nuid)

    def release(self, b):
        if b.dsem is not None:
            self.dfree.append(b.dsem)
            b.dsem = None

    def _waits(self, e, r, w):
        need = {}
        for b in r:
            for p, v in b.w.items():
                if need.get(p, 0) < v:
                    need[p] = v
        for b in w:
            for p, v in b.w.items():
                if need.get(p, 0) < v:
                    need[p] = v
            for p, v in b.r.items():
                if need.get(p, 0) < v:
                    need[p] = v
        seen = self.seen[e]
        eng = self.eng[e]
        for p, v in need.items():
            if p == "pe" and e == "pe":
                continue
            if seen.get(p, 0) >= v:
                continue
            eng.wait_ge(self.sem[p], v)
            seen[p] = v

    mute = False

    def op(self, e, fn, r=(), w=()):
        if self.mute:
            return None
        self._waits(e, r, w)
        inst = fn()
        self.cnt[e] += 1
        c = self.cnt[e]
        inst.then_inc(self.sem[e], 1)
        for b in r:
            b.r[e] = c
        for b in w:
            b.w[e] = c
        return inst

    def dma(self, q, out, in_, r=(), w=(), **kw):
        if self.mute:
            return None
        key = None
        for b in w:
            if b.sb:
                key = b
        if key is None:
            for b in r:
                if b.sb:
                    key = b
        if key is None:
            key = w[0]
        if key.dsem is None:
            key.dsem = self.dfree.pop()
        p = key.dsem
        self._waits(q, r, w)
        kw.setdefault('allow_slow_non_contiguous', True)
        inst = self.eng[q].dma_start(out=out, in_=in_, **kw)
        self.cnt[p] += 16
        c = self.cnt[p]
        inst.then_inc(self.sem[p], 16)
        for b in r:
            b.r[p] = c
        for b in w:
            b.w[p] = c
        return inst

    def barrier(self):
        if self.mute:
            return
        for e in ["pe", "dve", "act", "pool", "sp"]:
            seen = self.seen[e]
            for p, v in self.cnt.items():
                if v > 0 and seen.get(p, 0) < v:
                    self.eng[e].wait_ge(self.sem[p], v)
                    seen[p] = v

    def sb(self, st, name, shape, dt=F32):
        t = st.enter_context(self.nc.sbuf_tensor(self.uid(name), list(shape), dt))
        b = Buf(name, sb=True)
        st.callback(self.release, b)
        return t, b

    def ps(self, st, name, shape, dt=F32):
        t = st.enter_context(self.nc.psum_tensor(self.uid(name), list(shape), dt))
        return t, Buf(name)

    def dram(self, name, shape, dt=F32, kind="Internal"):
        return self.nc.dram_tensor(name, list(shape), dt, kind=kind).ap(), Buf(name)


def host_consts():
    c = {}
    c["ident"] = np.eye(128, dtype=np.float32)
    c["onesm"] = np.full((128, 128), 1.0 / D, np.float32)
    b = np.zeros((128, 128), np.float32)
    b[:64, :64] = 1.0 / 64
    b[64:, 64:] = 1.0 / 64
    c["blk64"] = b
    rot = np.zeros((128, 128), np.float32)
    for m in range(128):
        if (m % 64) < 32:
            rot[m + 32, m] = -1.0
        else:
            rot[m - 32, m] = 1.0
    c["rot"] = rot
    rows = LS // 64
    row = np.repeat(np.arange(rows, dtype=np.float32), 64)
    col = np.tile(np.arange(64, dtype=np.float32), rows)
    inv = (10000.0 ** (-np.arange(16, dtype=np.float32) / 16)).astype(np.float32)
    ang = np.concatenate([row[:, None] * inv, col[:, None] * inv], -1).astype(np.float32)
    c["cos2"] = np.ascontiguousarray(np.tile(np.cos(ang).T, (4, 1)).astype(np.float32))
    c["sin2"] = np.ascontiguousarray(np.tile(np.sin(ang).T, (4, 1)).astype(np.float32))
    rho = np.arange(128)
    hh, jj = rho % 8, rho // 8
    same = hh[:, None] == hh[None, :]
    c["tri"] = (same & (jj[:, None] <= jj[None, :])).astype(np.float32)
    c["blk"] = same.astype(np.float32)
    c["selp"] = (hh[:, None] // 2 == np.arange(4)[None, :]).astype(np.float32)
    mbd = np.zeros((128, 2, 64), np.float32)
    mbd[rho, hh % 2, :] = 1.0
    c["mbd"] = mbd.reshape(128, 128)
    mh = np.zeros((128, 8, 64), np.float32)
    mh[rho, hh, :] = 1.0
    c["mh"] = mh.reshape(128, 512)
    cm = np.zeros((128, 4, 128), np.float32)
    cm[:, hh // 2, rho] = 1.0
    c["cm"] = cm.reshape(128, 512)
    ms = (same & (jj[:, None] < jj[None, :])).astype(np.float32)
    mi = (same & (jj[:, None] <= jj[None, :])).astype(np.float32)
    c["msi"] = np.concatenate([ms, mi], 1)
    c["msl"] = np.ascontiguousarray(ms.T)
    c["trib"] = (same & (jj[:, None] >= jj[None, :])).astype(np.float32)
    msb = (same & (jj[:, None] > jj[None, :])).astype(np.float32)
    mib = (same & (jj[:, None] >= jj[None, :])).astype(np.float32)
    c["msib"] = np.concatenate([msb, mib], 1)
    c["mslb"] = np.ascontiguousarray(msb.T)
    for n, tag in ((LP, "p"), (LS, "s")):
        t01 = np.linspace(0.0, 1.0, n, dtype=np.float32)[:, None]
        pos = np.arange(n, dtype=np.float32)[:, None]
        bands = np.linspace(1e-4, 15, 16, dtype=np.float32)[None, :]
        angz = (np.float32(2.0 * math.pi / n) * pos * bands).astype(np.float32)
        z = np.concatenate([t01, np.cos(angz), -np.sin(angz)], -1).astype(np.float32)
        c["zT" + tag] = np.ascontiguousarray(z.T)
        c["nt01" + tag] = np.ascontiguousarray((-t01[:, 0]).reshape(n // 128, 128).T)
        tt = np.arange(n, dtype=np.float64)[:, None]
        ff = np.arange(n, dtype=np.float64)[None, :]
        th = 2.0 * math.pi * tt * ff / (2 * n)
        Cf = np.cos(th)
        Sf = -np.sin(th)
        Sf[:, 0] = (-1.0) ** np.arange(n)
        SfT = (-np.sin(th)).T.copy()
        SfT[0, :] = (-1.0) ** np.arange(n)
        c["Cf" + tag] = Cf.astype(ml_dtypes.bfloat16)
        c["Sf" + tag] = Sf.astype(ml_dtypes.bfloat16)
        c["SfT" + tag] = SfT.astype(ml_dtypes.bfloat16)
        wf = np.full((n,), 1.0 / n, np.float32)
        wf[0] = 1.0 / (2 * n)
        c["wf" + tag] = np.ascontiguousarray(wf.reshape(n // 128, 128).T)
    return c


W_NAMES = ["w_mod", "b_mod", "w_in", "rwkv_shift", "rwkv_w0", "rwkv_w2", "rwkv_a0", "rwkv_a2",
           "rwkv_kk", "rwkv_ka", "rwkv_rk", "rwkv_g2", "rwkv_gn_g", "rwkv_gn_b",
           "hy_short", "hy_w1", "hy_b1", "hy_freq", "hy_w2", "hy_b2", "hy_w3", "hy_decay", "hy_bias",
           "attn_qn", "attn_kn", "w_out", "ln1_g", "ln1_b", "ln2_g", "ln2_b",
           "ffn_up", "ffn_conv", "ffn_down"]

W_SHAPES = {
    "w_mod": (2, 2048, 12288), "b_mod": (2, 12288), "w_in": (2, 2048, 4992), "rwkv_shift": (2, 3, 1920),
    "rwkv_w0": (2, 2, 512), "rwkv_w2": (2, 2, 64, 512), "rwkv_a0": (2, 2, 512), "rwkv_a2": (2, 2, 64, 512),
    "rwkv_kk": (2, 512), "rwkv_ka": (2, 512), "rwkv_rk": (2, 512), "rwkv_g2": (2, 128, 512),
    "rwkv_gn_g": (2, 512), "rwkv_gn_b": (2, 512), "hy_short": (2, 3, 1536), "hy_w1": (2, 33, 64),
    "hy_b1": (2, 64), "hy_freq": (2, 2, 64), "hy_w2": (2, 64, 64), "hy_b2": (2, 64), "hy_w3": (2, 64, 1024),
    "hy_decay": (2, 2, 512), "hy_bias": (2, 512), "attn_qn": (2, 64), "attn_kn": (2, 64),
    "w_out": (2, 2048, 2048), "ln1_g": (2, 2048), "ln1_b": (2, 2048), "ln2_g": (2, 2048), "ln2_b": (2, 2048),
    "ffn_up": (2, 2048, 11264), "ffn_conv": (2, 3, 11264), "ffn_down": (2, 5632, 2048),
}


def is_T(ci):
    return ci < 12 or 19 <= ci < 27 or ci >= 37


class Rot:
    def __init__(self, items):
        self.items = items
        self.i = 0

    def nxt(self):
        it = self.items[self.i % len(self.items)]
        self.i += 1
        return it


def build(dbg=None):
    c = Ctx()
    nc = c.nc
    I = {}

    def ext_in(name, shape, dt=F32):
        I[name] = nc.dram_tensor(name, list(shape), dt, kind="ExternalInput").ap()

    ext_in("xp", (512, D))
    ext_in("xs", (LS, D))
    ext_in("ck", (DEPTH, 256, 256))
    ext_in("cv", (DEPTH, 256, 256))
    ext_in("st", (DEPTH, 2, 8, 64, 64))
    ext_in("cond", (2, D))
    for n in W_NAMES:
        ext_in(n, W_SHAPES[n])
    hc = host_consts()
    for k, v in hc.items():
        ext_in("k_" + k, v.shape, BF16 if v.dtype == ml_dtypes.bfloat16 else F32)
    O = {}
    for name, shape in (("yp", (512, D)), ("ys", (LS, D)), ("nk", (2, DEPTH, 256, 256)),
                        ("nv", (2, DEPTH, 256, 256)), ("ns", (2, DEPTH, 2, 8, 64, 64))):
        O[name] = nc.dram_tensor(name, list(shape), F32, kind="ExternalOutput").ap()
    out_b = Buf("outs")

    es = c.es
    with es:
        PS = [c.ps(es, "bank%d" % i, (128, 512)) for i in range(8)]
        psr = Rot(PS)
        ident, _ = c.sb(es, "ident", (128, 128))
        onesm, _ = c.sb(es, "onesm", (128, 128))
        blk64, _ = c.sb(es, "blk64", (128, 128))
        rotm, _ = c.sb(es, "rotm", (128, 128))
        kb = Buf("consts")
        for t, nm in ((ident, "ident"), (onesm, "onesm"), (blk64, "blk64"), (rotm, "rot")):
            c.dma("sp", t[:], I["k_" + nm][:, :], w=[kb])
        modT = [c.sb(es, "modT%d" % l, (128, 96, 2)) for l in range(DEPTH)]
        if dbg:
            c.mute = True

        xT, _ = c.dram("xT", (D, NTOK))
        xTv = xT.rearrange("(kc p) t -> p kc t", p=128)
        xT_b = [Buf("xT%d" % i) for i in range(5)]
        projT, projT_b = c.dram("projT", (NTOK, INC))
        projF, projF_b = c.dram("projF", (INC, NTOK))
        mixT, mixT_b = c.dram("mixT", (D, NTOK), BF16)
        mixTv = mixT.rearrange("(kc p) t -> p kc t", p=128)
        h2T, h2T_b = c.dram("h2T", (D, NTOK), BF16)
        h2Tv = h2T.rearrange("(kc p) t -> p kc t", p=128)

        wb = {}
        for l in range(DEPTH):
            for nm, rows, cols in (("w_in", D, INC), ("w_out", D, D), ("ffn_up", D, 2 * DFF), ("ffn_down", DFF, D)):
                wb[(nm, l)] = c.dram("wb_%s_%d" % (nm, l), (rows, cols), BF16)

        castq = []

        def cast_weights(l, names, now=False):
            for nm in names:
                ap, b = wb[(nm, l)]
                src = I[nm][l]
                for r0 in range(0, ap.shape[0], 256):
                    th = (lambda ap=ap, src=src, r0=r0, b=b: c.dma("pool", ap[r0:r0 + 256, :], src[r0:r0 + 256, :], w=[b]))
                    if now:
                        th()
                    else:
                        castq.append(th)
        cast_weights(0, ["w_in"], now=True)
        if dbg is None:
            cast_weights(0, ["w_out", "ffn_up", "ffn_down"])
            cast_weights(1, ["w_in"])

        def load_cols(st, name, src2d, nrow):
            tmp, tb = c.sb(st, name + "_r", (nrow, 128))
            if isinstance(src2d, list):
                r0 = 0
                for sp_ in src2d:
                    nr = sp_.shape[0]
                    c.dma("sp", tmp[r0:r0 + nr, :], sp_, w=[tb])
                    r0 += nr
            else:
                c.dma("sp", tmp[:], src2d, w=[tb])
            dst, db = c.sb(st, name, (128, nrow))
            pt, pb = psr.nxt()
            c.op("pe", lambda: nc.tensor.transpose(out=pt[:, 0:nrow], in_=tmp[:], identity=ident[0:nrow, 0:nrow]),
                 r=[tb, kb], w=[pb])
            c.op("dve", lambda: nc.vector.tensor_copy(out=dst[:], in_=pt[:, 0:nrow]), r=[pb], w=[db])
            return dst, db

        with ExitStack() as st:
            cin, cin_b = c.sb(st, "cin", (2, D))
            c.dma("sp", cin[:], I["cond"][:, :], w=[cin_b])
            c.op("act", lambda: nc.scalar.activation(out=cin[:], in_=cin[:], func=AF.Silu), r=[cin_b], w=[cin_b])
            silT, silT_b = c.sb(st, "silT", (128, 16, 2), F32R)
            pt, pb = psr.nxt()
            for kc in range(16):
                c.op("pe", lambda kc=kc: nc.tensor.transpose(out=pt[:, kc * 2:kc * 2 + 2], in_=cin[:, kc * 128:(kc + 1) * 128],
                                                           identity=ident[0:2, 0:2]), r=[cin_b, kb], w=[pb])
            c.op("dve", lambda: nc.vector.tensor_copy(out=silT[:].rearrange("p a b -> p (a b)"), in_=pt[:, 0:32]),
                 r=[pb], w=[silT_b])
            wm = [c.sb(st, "wm", (128, 16, 512)) for _ in range(2)]
            wr = [c.sb(st, "wr", (128, 16, 512), F32R) for _ in range(2)]
            mrows = Rot([c.sb(st, "mrow", (2, 512)) for _ in range(2)])
            brows = Rot([c.sb(st, "brow", (2, 512)) for _ in range(2)])
            modD, modDb = c.dram("modD", (DEPTH, 2, 6 * D))
            nblk = 0
            for l in range(DEPTH):
                wv = I["w_mod"][l].rearrange("(kc p) n -> p kc n", p=128)
                for cb in range(24):
                    W, Wb_ = wm[nblk % 2]
                    Wr, Wrb = wr[nblk % 2]
                    nblk += 1
                    c.dma("sp", W[:], wv[:, :, cb * 512:(cb + 1) * 512], w=[Wb_])
                    c.op("dve", lambda W=W, Wr=Wr: nc.vector.tensor_copy(out=Wr[:, 0:8, :], in_=W[:, 0:8, :]), r=[Wb_], w=[Wrb])
                    c.op("act", lambda W=W, Wr=Wr: nc.scalar.copy(out=Wr[:, 8:16, :], in_=W[:, 8:16, :]), r=[Wb_], w=[Wrb])
                    pm, pmb = psr.nxt()
                    for kc in range(16):
                        c.op("pe", lambda kc=kc, Wr=Wr, pm=pm: nc.tensor.matmul(
                            pm[0:2, :], lhsT=silT[:, kc, :], rhs=Wr[:, kc, :], start=(kc == 0), stop=(kc == 15)),
                            r=[Wrb, silT_b], w=[pmb])
                    brow, browb = brows.nxt()
                    mrow, mrowb = mrows.nxt()
                    c.dma("sp", brow[:], I["b_mod"][l, cb * 512:(cb + 1) * 512].partition_broadcast(2), w=[browb])
                    c.op("dve", lambda pm=pm, mrow=mrow, brow=brow: nc.vector.tensor_tensor(
                        out=mrow[:], in0=pm[0:2, :], in1=brow[:], op=ALU.add), r=[pmb, browb], w=[mrowb])
                    c.dma("pool", modD[l][:, cb * 512:(cb + 1) * 512], mrow[:], r=[mrowb], w=[modDb])
                mt, mtb = modT[l]
                for gi_ in range(2):
                    c.dma("sp", mt[:, :, gi_], modD[l, gi_].rearrange("(a p) -> p a", p=128), r=[modDb], w=[mtb])
                for lo in (16, 64):
                    c.op("dve", lambda lo=lo, mt=mt: nc.vector.tensor_scalar(
                        out=mt[:, lo:lo + 16, :], in0=mt[:, lo:lo + 16, :], scalar1=1.0, scalar2=None, op0=ALU.add),
                        r=[mtb], w=[mtb])
                for lo in (32, 80):
                    c.op("dve", lambda lo=lo, mt=mt: nc.vector.tensor_scalar(
                        out=mt[:, lo:lo + 16, :], in0=mt[:, lo:lo + 16, :], scalar1=1.0 / ALPHA, scalar2=None, op0=ALU.mult),
                        r=[mtb], w=[mtb])
            c.barrier()

        with ExitStack() as st:
            xin = [c.sb(st, "xin", (128, D)) for _ in range(2)]
            xo = [c.sb(st, "xo", (128, 16, 128)) for _ in range(2)]
            for i in range(20):
                src = I["xp"][i * 128:(i + 1) * 128, :] if i < 4 else I["xs"][(i - 4) * 128:(i - 3) * 128, :]
                X, Xb = xin[i % 2]
                Y, Yb = xo[i % 2]
                c.dma("sp", X[:], src, w=[Xb])
                for g in range(4):
                    pt, pb = psr.nxt()
                    for j in range(4):
                        kc = g * 4 + j
                        c.op("pe", lambda kc=kc, j=j, X=X, pt=pt: nc.tensor.transpose(
                            out=pt[:, j * 128:(j + 1) * 128], in_=X[:, kc * 128:(kc + 1) * 128], identity=ident[:]),
                            r=[Xb, kb], w=[pb])
                    e = "dve" if g % 2 == 0 else "act"
                    dst = Y[:, g * 4:(g + 1) * 4, :].rearrange("p a b -> p (a b)")
                    if e == "dve":
                        c.op("dve", lambda dst=dst, pt=pt: nc.vector.tensor_copy(out=dst, in_=pt[:]), r=[pb], w=[Yb])
                    else:
                        c.op("act", lambda dst=dst, pt=pt: nc.scalar.copy(out=dst, in_=pt[:]), r=[pb], w=[Yb])
                c.dma("pool", xTv[:, :, i * 128:(i + 1) * 128], Y[:], r=[Yb], w=[xT_b[i // 4]])
            c.barrier()

        def evac(i, dst, src, rb, wb_):
            if i % 2 == 0:
                c.op("dve", lambda: nc.vector.tensor_copy(out=dst, in_=src), r=rb, w=wb_)
            else:
                c.op("act", lambda: nc.scalar.copy(out=dst, in_=src), r=rb, w=wb_)

        def stage_A(l):
            mt, mtb = modT[l]
            with ExitStack() as st:
                xt, xtb = c.sb(st, "xt", (128, 16, 512))
                hT, hTb = c.sb(st, "hT", (128, 16, 512), BF16)
                wt = [c.sb(st, "wA", (128, 16, 512), BF16) for _ in range(2)]
                og = Rot([c.sb(st, "oA", (128, 512)) for _ in range(10)])
                Wap, Wbuf = wb[("w_in", l)]
                Wv = Wap.rearrange("(kc p) n -> p kc n", p=128)
                nev = 0
                for tt in range(5):
                    gi = 0 if tt == 0 else 1
                    c.dma("sp", xt[:], xTv[:, :, tt * 512:(tt + 1) * 512], r=[xT_b[tt]], w=[xtb])
                    for kc in range(16):
                        sc = mt[:, 16 + kc, gi:gi + 1]
                        sh = mt[:, kc, gi:gi + 1]
                        if kc % 2 == 0:
                            c.op("dve", lambda kc=kc, sc=sc, sh=sh: nc.vector.tensor_scalar(
                                out=hT[:, kc, :], in0=xt[:, kc, :], scalar1=sc, scalar2=sh, op0=ALU.mult, op1=ALU.add),
                                r=[xtb, mtb], w=[hTb])
                        else:
                            c.op("act", lambda kc=kc, sc=sc, sh=sh: nc.scalar.activation(
                                out=hT[:, kc, :], in_=xt[:, kc, :], func=AF.Identity, scale=sc, bias=sh),
                                r=[xtb, mtb], w=[hTb])
                    for blk in range(10):
                        c0 = blk * 512
                        ncol = min(512, INC - c0)
                        W, Wb_ = wt[blk % 2]
                        c.dma("sp", W[:, :, 0:ncol], Wv[:, :, c0:c0 + ncol], r=[Wbuf], w=[Wb_])
                        nch = ncol // 128
                        j = 0
                        while j < nch:
                            ci = c0 // 128 + j
                            if is_T(ci):
                                j2 = j
                                while j2 < nch and is_T(c0 // 128 + j2):
                                    j2 += 1
                                wd = (j2 - j) * 128
                                for ts in range(4):
                                    pt, pb = psr.nxt()
                                    for kc in range(16):
                                        c.op("pe", lambda kc=kc, ts=ts, pt=pt, W=W, j=j, wd=wd: nc.tensor.matmul(
                                            pt[:, 0:wd], lhsT=hT[:, kc, ts * 128:(ts + 1) * 128],
                                            rhs=W[:, kc, j * 128:j * 128 + wd], start=(kc == 0), stop=(kc == 15)),
                                            r=[hTb, Wb_], w=[pb])
                                    o, ob = og.nxt()
                                    evac(nev, o[:, 0:wd], pt[:, 0:wd], [pb], [ob])
                                    nev += 1
                                    t0 = tt * 512 + ts * 128
                                    c.dma("pool", projT[t0:t0 + 128, c0 + j * 128:c0 + j * 128 + wd], o[:, 0:wd],
                                          r=[ob], w=[projT_b])
                                j = j2
                            else:
                                pt, pb = psr.nxt()
                                for kc in range(16):
                                    c.op("pe", lambda kc=kc, pt=pt, W=W, j=j: nc.tensor.matmul(
                                        pt[:], lhsT=W[:, kc, j * 128:(j + 1) * 128], rhs=hT[:, kc, :],
                                        start=(kc == 0), stop=(kc == 15)), r=[hTb, Wb_], w=[pb])
                                o, ob = og.nxt()
                                evac(nev, o[:], pt[:], [pb], [ob])
                                nev += 1
                                c.dma("pool", projF[ci * 128:(ci + 1) * 128, tt * 512:(tt + 1) * 512], o[:],
                                      r=[ob], w=[projF_b])
                                j += 1
                c.barrier()

        SEQS = [(0, LP, False, 0), (LP, LP, False, 1), (2 * LP, LS, True, 0)]

        def stage_attn(l):
            with ExitStack() as st:
                gq, gqb = c.sb(st, "gq", (128, 1))
                gk, gkb = c.sb(st, "gk", (128, 1))
                for h in range(2):
                    c.dma("sp", gq[h * 64:(h + 1) * 64, :], I["attn_qn"][l].rearrange("(a b) -> a b", b=1), w=[gqb])
                    c.dma("sp", gk[h * 64:(h + 1) * 64, :], I["attn_kn"][l].rearrange("(a b) -> a b", b=1), w=[gkb])
                cos2, cb2 = c.sb(st, "cos2", (128, LS))
                sin2, sb2 = c.sb(st, "sin2", (128, LS))
                c.dma("sp", cos2[:], I["k_cos2"][:, :], w=[cb2])
                c.dma("sp", sin2[:], I["k_sin2"][:, :], w=[sb2])
                QT, QTb = c.sb(st, "QT", (128, 8, LS), BF16)
                KT2, KT2b = c.sb(st, "KT2", (128, 4, LS + 256), BF16)
                VA, VAb = c.sb(st, "VA", (128, 18, 4, 128), BF16)
                VB, VBb = c.sb(st, "VB", (128, 18, 4, 128), BF16)
                c.op("pool", lambda: nc.gpsimd.memset(VA[:], 1.0), w=[VAb])
                c.op("pool", lambda: nc.gpsimd.memset(VB[:], 1.0), w=[VBb])
                xin = Rot([c.sb(st, "axin", (128, 512)) for _ in range(2)])
                sq = Rot([c.sb(st, "asq", (128, 512)) for _ in range(2)])
                rs = Rot([c.sb(st, "ars", (128, 512)) for _ in range(2)])
                qn = Rot([c.sb(st, "aqn", (128, 512)) for _ in range(2)])
                t1 = Rot([c.sb(st, "at1", (128, 512)) for _ in range(2)])
                kn = Rot([c.sb(st, "akn", (128, 512), BF16) for _ in range(2)])
                vin = Rot([c.sb(st, "avin", (128, 256)) for _ in range(2)])
                pexp = Rot([c.sb(st, "apexp", (128, 512), BF16) for _ in range(4)])
                rec = Rot([c.sb(st, "arec", (128, 512)) for _ in range(2)])
                ao = Rot([c.sb(st, "aao", (128, 512), BF16) for _ in range(2)])
                ko = Rot([c.sb(st, "ako", (128, 128)) for _ in range(2)])

                def prep(ci, t0, n, ttok, is_s, gcol, gb):
                    X, Xb = xin.nxt()
                    c.dma("sp", X[:, 0:n], projF[ci * 128:(ci + 1) * 128, t0:t0 + n], r=[projF_b], w=[Xb])
                    S, Sb = sq.nxt()
                    c.op("act", lambda: nc.scalar.activation(out=S[:, 0:n], in_=X[:, 0:n], func=AF.Square), r=[Xb], w=[Sb])
                    pt, pb = psr.nxt()
                    c.op("pe", lambda: nc.tensor.matmul(pt[:, 0:n], lhsT=blk64[:], rhs=S[:, 0:n], start=True, stop=True),
                         r=[Sb, kb], w=[pb])
                    R, Rb = rs.nxt()
                    c.op("act", lambda: nc.scalar.activation(out=R[:, 0:n], in_=pt[:, 0:n], func=AF.Sqrt, bias=epsq[:, 0:1]),
                         r=[pb, epsb], w=[Rb])
                    c.op("dve", lambda: nc.vector.reciprocal(out=R[:, 0:n], in_=R[:, 0:n]), r=[Rb], w=[Rb])
                    Qn, Qb = qn.nxt()
                    c.op("dve", lambda: nc.vector.scalar_tensor_tensor(
                        out=Qn[:, 0:n], in0=X[:, 0:n], scalar=gcol[:, 0:1], in1=R[:, 0:n], op0=ALU.mult, op1=ALU.mult),
                        r=[Xb, Rb, gb], w=[Qb])
                    if not is_s:
                        return Qn, Qb
                    pr, prb = psr.nxt()
                    c.op("pe", lambda: nc.tensor.matmul(pr[:, 0:n], lhsT=rotm[:], rhs=Qn[:, 0:n], start=True, stop=True),
                         r=[Qb, kb], w=[prb])
                    T1, T1b = t1.nxt()
                    c.op("dve", lambda: nc.vector.tensor_tensor(out=T1[:, 0:n], in0=pr[:, 0:n], in1=sin2[:, ttok:ttok + n],
                                                                op=ALU.mult), r=[prb, sb2], w=[T1b])
                    c.op("pool", lambda: nc.gpsimd.tensor_tensor(out=Qn[:, 0:n], in0=Qn[:, 0:n], in1=cos2[:, ttok:ttok + n],
                                                                 op=ALU.mult), r=[Qb, cb2], w=[Qb])
                    c.op("dve", lambda: nc.vector.tensor_tensor(out=Qn[:, 0:n], in0=Qn[:, 0:n], in1=T1[:, 0:n], op=ALU.add),
                         r=[Qb, T1b], w=[Qb])
                    return Qn, Qb

                epsq, epsb = c.sb(st, "epsq", (128, 1))
                c.op("dve", lambda: nc.vector.memset(epsq[:], 1e-6), w=[epsb])

                for (s0, L, is_s, pi) in SEQS:
                    nq = min(512, L)
                    nkc = (L + (256 if is_s else 0)) // 128
                    for j in range(2):
                        for t0 in range(0, L, nq):
                            Kn, Kb = prep(35 + j, s0 + t0, nq, t0, is_s, gk, gkb)
                            for h in range(2):
                                g = 2 * j + h
                                lo, hi = h * 64, (h + 1) * 64
                                olo, ohi = (1 - h) * 64, (2 - h) * 64
                                c.op("act", lambda g=g, lo=lo, hi=hi, Kn=Kn, t0=t0: nc.scalar.copy(
                                    out=KT2[lo:hi, g, t0:t0 + nq], in_=Kn[lo:hi, 0:nq]), r=[Kb], w=[KT2b])
                                c.op("dve", lambda g=g, lo=lo, hi=hi, olo=olo, ohi=ohi, Kn=Kn, t0=t0: nc.vector.tensor_copy(
                                    out=KT2[olo:ohi, g, t0:t0 + nq], in_=Kn[lo:hi, 0:nq]), r=[Kb], w=[KT2b])
                            if not is_s:
                                for hh in range(nq // 128):
                                    pt, pb = psr.nxt()
                                    c.op("pe", lambda Kn=Kn, hh=hh, pt=pt: nc.tensor.transpose(
                                        out=pt[:, 0:128], in_=Kn[:, hh * 128:(hh + 1) * 128], identity=ident[:]),
                                        r=[Kb, kb], w=[pb])
                                    Ko, Kob = ko.nxt()
                                    c.op("dve", lambda Ko=Ko, pt=pt: nc.vector.tensor_copy(out=Ko[:], in_=pt[:, 0:128]),
                                         r=[pb], w=[Kob])
                                    r0 = t0 + hh * 128
                                    c.dma("pool", O["nk"][pi, l, r0:r0 + 128, j * 128:(j + 1) * 128], Ko[:],
                                          r=[Kob], w=[out_b])
                    for kc in range(nkc):
                        Vi, Vib = vin.nxt()
                        if kc * 128 < L:
                            r0 = s0 + kc * 128
                            c.dma("sp", Vi[:], projT[r0:r0 + 128, 4736:4992], r=[projT_b], w=[Vib])
                            if not is_s:
                                c.dma("pool", O["nv"][pi, l, kc * 128:(kc + 1) * 128, :], projT[r0:r0 + 128, 4736:4992],
                                      r=[projT_b], w=[out_b])
                        else:
                            r0 = kc * 128 - L
                            c.dma("sp", Vi[:], I["cv"][l, r0:r0 + 128, :], w=[Vib])
                            Ki, Kib = vin.nxt()
                            c.dma("sp", Ki[:], I["ck"][l, r0:r0 + 128, :], w=[Kib])
                            for j in range(2):
                                pt, pb = psr.nxt()
                                c.op("pe", lambda Ki=Ki, j=j, pt=pt: nc.tensor.transpose(
                                    out=pt[:, 0:128], in_=Ki[:, j * 128:(j + 1) * 128], identity=ident[:]),
                                    r=[Kib, kb], w=[pb])
                                for h in range(2):
                                    g = 2 * j + h
                                    lo, hi = h * 64, (h + 1) * 64
                                    olo, ohi = (1 - h) * 64, (2 - h) * 64
                                    c.op("act", lambda g=g, lo=lo, hi=hi, pt=pt, kc=kc: nc.scalar.copy(
                                        out=KT2[lo:hi, g, kc * 128:(kc + 1) * 128], in_=pt[lo:hi, 0:128]), r=[pb], w=[KT2b])
                                    c.op("dve", lambda g=g, lo=lo, hi=hi, olo=olo, ohi=ohi, pt=pt, kc=kc: nc.vector.tensor_copy(
                                        out=KT2[olo:ohi, g, kc * 128:(kc + 1) * 128], in_=pt[lo:hi, 0:128]), r=[pb], w=[KT2b])
                        c.op("dve", lambda kc=kc, Vi=Vi: nc.vector.tensor_copy(
                            out=VA[:, kc, :, 0:64], in_=Vi[:].rearrange("p (g d) -> p g d", g=4)), r=[Vib], w=[VAb])
                        c.op("act", lambda kc=kc, Vi=Vi: nc.scalar.copy(
                            out=VB[:, kc, :, 64:128], in_=Vi[:].rearrange("p (g d) -> p g d", g=4)), r=[Vib], w=[VBb])
                    for i in range(8):
                        for t0 in range(0, L, nq):
                            Qn, Qb = prep(27 + i, s0 + t0, nq, t0, is_s, gq, gqb)
                            c.op("act", lambda Qn=Qn, i=i, t0=t0: nc.scalar.copy(out=QT[:, i, t0:t0 + nq], in_=Qn[:, 0:nq]),
                                 r=[Qb], w=[QTb])
                    for t0 in range(0, L, nq):
                        for i in range(8):
                            g = i // 2
                            bx, bxb = psr.nxt()
                            by, byb = psr.nxt()
                            steps = [(kc, h) for kc in range(nkc) for h in range(2)]
                            scored = {}

                            def score(si):
                                kc, h = steps[si]
                                lo, hi = h * 64, (h + 1) * 64
                                pss, pssb = psr.nxt()
                                while pss is bx or pss is by:
                                    pss, pssb = psr.nxt()
                                c.op("pe", lambda lo=lo, hi=hi, pss=pss, kc=kc: nc.tensor.matmul(
                                    pss[:, 0:nq], lhsT=KT2[lo:hi, g, kc * 128:(kc + 1) * 128], rhs=QT[lo:hi, i, t0:t0 + nq],
                                    start=True, stop=True), r=[KT2b, QTb], w=[pssb])
                                scored[si] = (pss, pssb)
                            DEPTH_SC = 3
                            for si in range(min(DEPTH_SC, len(steps))):
                                score(si)
                            for si, (kc, h) in enumerate(steps):
                                pss, pssb = scored.pop(si)
                                Pe, Peb = pexp.nxt()
                                c.op("act", lambda Pe=Pe, pss=pss: nc.scalar.activation(
                                    out=Pe[:, 0:nq], in_=pss[:, 0:nq], func=AF.Exp, scale=0.125), r=[pssb], w=[Peb])
                                if si + DEPTH_SC < len(steps):
                                    score(si + DEPTH_SC)
                                acc, accb, Vt, Vtb = (bx, bxb, VA, VAb) if h == 0 else (by, byb, VB, VBb)
                                c.op("pe", lambda acc=acc, Vt=Vt, kc=kc, Pe=Pe: nc.tensor.matmul(
                                    acc[:, 0:nq], lhsT=Vt[:, kc, g, :], rhs=Pe[:, 0:nq],
                                    start=(kc == 0), stop=(kc == nkc - 1)), r=[Vtb, Peb], w=[accb])
                            R, Rb = rec.nxt()
                            c.op("dve", lambda R=R, bx=bx: nc.vector.reciprocal(out=R[0:64, 0:nq], in_=bx[64:128, 0:nq]),
                                 r=[bxb], w=[Rb])
                            c.op("dve", lambda R=R, by=by: nc.vector.reciprocal(out=R[64:128, 0:nq], in_=by[0:64, 0:nq]),
                                 r=[byb], w=[Rb])
                            A, Ab = ao.nxt()
                            c.op("dve", lambda A=A, R=R, bx=bx: nc.vector.tensor_tensor(
                                out=A[0:64, 0:nq], in0=bx[0:64, 0:nq], in1=R[0:64, 0:nq], op=ALU.mult), r=[bxb, Rb], w=[Ab])
                            c.op("dve", lambda A=A, R=R, by=by: nc.vector.tensor_tensor(
                                out=A[64:128, 0:nq], in0=by[64:128, 0:nq], in1=R[64:128, 0:nq], op=ALU.mult), r=[byb, Rb], w=[Ab])
                            c.dma("pool", mixT[1024 + i * 128:1024 + (i + 1) * 128, s0 + t0:s0 + t0 + nq], A[:, 0:nq],
                                  r=[Ab], w=[mixT_b])
                c.barrier()

        def stage_hyena(l):
            with ExitStack() as st:
                w1, w1b = c.sb(st, "hw1", (33, 64))
                w2, w2b = c.sb(st, "hw2", (64, 64))
                w3, w3b = c.sb(st, "hw3", (64, 1024))
                c.dma("sp", w1[:], I["hy_w1"][l], w=[w1b])
                c.dma("sp", w2[:], I["hy_w2"][l], w=[w2b])
                c.dma("sp", w3[:], I["hy_w3"][l], w=[w3b])
                pc, pcb = c.sb(st, "hpc", (64, 4))
                c.dma("sp", pc[:, 0:1], I["hy_b1"][l].rearrange("(a b) -> a b", b=1), w=[pcb])
                c.dma("sp", pc[:, 1:2], I["hy_freq"][l, 0].rearrange("(a b) -> a b", b=1), w=[pcb])
                c.dma("sp", pc[:, 2:3], I["hy_b2"][l].rearrange("(a b) -> a b", b=1), w=[pcb])
                c.dma("sp", pc[:, 3:4], I["hy_freq"][l, 1].rearrange("(a b) -> a b", b=1), w=[pcb])
                bf, bfb = c.sb(st, "hbf", (64, 2))
                c.op("dve", lambda: nc.vector.tensor_tensor(out=bf[:, 0:1], in0=pc[:, 0:1], in1=pc[:, 1:2], op=ALU.mult),
                     r=[pcb], w=[bfb])
                c.op("dve", lambda: nc.vector.tensor_tensor(out=bf[:, 1:2], in0=pc[:, 2:3], in1=pc[:, 3:4], op=ALU.mult),
                     r=[pcb], w=[bfb])
                dec, decb = c.sb(st, "hdec", (128, 2, 512))
                for d in range(2):
                    c.dma("sp", dec[:, d, :], I["hy_decay"][l, d].partition_broadcast(128), w=[decb])
                c.op("act", lambda: nc.scalar.activation(out=dec[:], in_=dec[:], func=AF.Abs), r=[decb], w=[decb])
                hbias, hbb = c.sb(st, "hbias", (1, 512))
                c.dma("sp", hbias[:], I["hy_bias"][l].rearrange("(a b) -> a b", a=1), w=[hbb])
                tx0, tx0b = load_cols(st, "tx0", [I["hy_short"][l, j, 0:512].rearrange("(c p) -> c p", p=128) for j in range(3)], 12)
                tapb, tapbb = c.sb(st, "htap", (128, 3, 1024))
                for j in range(3):
                    c.dma("sp", tapb[:, j, :], I["hy_short"][l, j, 512:1536].partition_broadcast(128), w=[tapbb])

                HP = {}

                def sin_layer(ps, n, colf, colbf, dst, dstb, psb):
                    A, Ab = HP['a1'].nxt()
                    Kt, Ktb = HP['k1'].nxt()
                    c.op("dve", lambda: nc.vector.tensor_scalar(out=A[:, 0:n], in0=ps[0:64, 0:n], scalar1=colf, scalar2=colbf,
                                                                op0=ALU.mult, op1=ALU.add), r=[psb, pcb, bfb], w=[Ab])
                    c.op("dve", lambda: nc.vector.tensor_scalar(out=Kt[:, 0:n], in0=A[:, 0:n], scalar1=1.0 / TWO_PI, scalar2=MAGIC,
                                                                op0=ALU.mult, op1=ALU.add), r=[Ab], w=[Ktb])
                    c.op("dve", lambda: nc.vector.tensor_scalar(out=Kt[:, 0:n], in0=Kt[:, 0:n], scalar1=-MAGIC, scalar2=None,
                                                                op0=ALU.add), r=[Ktb], w=[Ktb])
                    c.op("dve", lambda: nc.vector.scalar_tensor_tensor(out=A[:, 0:n], in0=Kt[:, 0:n], scalar=-TWO_PI, in1=A[:, 0:n],
                                                                       op0=ALU.mult, op1=ALU.add), r=[Ktb, Ab], w=[Ab])
                    c.op("act", lambda: nc.scalar.activation(out=dst, in_=A[:, 0:n], func=AF.Sin), r=[Ab], w=[dstb])

                for (n, tag, seqs) in ((LP, "p", SEQS[0:2]), (LS, "s", SEQS[2:3])):
                    ntc = n // 128
                    tw = min(512, n)
                    with ExitStack() as s2:
                        nt01, ntb = c.sb(s2, "hnt01", (128, ntc))
                        c.dma("sp", nt01[:], I["k_nt01" + tag][:, :], w=[ntb])
                        wf, wfb = c.sb(s2, "hwf", (128, ntc))
                        c.dma("sp", wf[:], I["k_wf" + tag][:, :], w=[wfb])
                        HS, HSb = c.sb(s2, "hHS", (128, ntc, 512), BF16)
                        HD, HDb = c.sb(s2, "hHD", (128, ntc, 512), BF16)
                        U = [c.sb(s2, "hU", (128, ntc, 512), BF16) for _ in seqs]
                        YRE = [c.sb(s2, "hYRE", (128, ntc, 512), BF16) for _ in seqs]
                        YIM = [c.sb(s2, "hYIM", (128, ntc, 512), BF16) for _ in seqs]
                        sf0, sf0b = c.sb(s2, "hsf0", (128, ntc, 1), BF16)
                        Sfv = I["k_Sf" + tag].rearrange("(tc p) f -> p tc f", p=128)
                        Cfv = I["k_Cf" + tag].rearrange("(tc p) f -> p tc f", p=128)
                        STv = I["k_SfT" + tag].rearrange("(tc p) f -> p tc f", p=128)
                        c.dma("sp", sf0[:], Sfv[:, :, 0:1], w=[sf0b])
                        hn, hnb = c.sb(s2, "hhn", (1, 512))
                        s3 = ExitStack()
                        HP['a1'] = Rot([c.sb(s3, "ha1", (64, 512)) for _ in range(2)])
                        HP['k1'] = Rot([c.sb(s3, "hk1", (64, 512)) for _ in range(2)])
                        ef = Rot([c.sb(s3, "hef", (128, 512)) for _ in range(4)])
                        zT, zTb = c.sb(s3, "hzT", (33, n))
                        c.dma("sp", zT[:], I["k_zT" + tag][:, :], w=[zTb])
                        h1T, h1b = c.sb(s3, "hh1T", (64, n))
                        h2T_, h2b = c.sb(s3, "hh2T", (64, n))
                        for t0 in range(0, n, tw):
                            pt, pb = psr.nxt()
                            c.op("pe", lambda pt=pt, t0=t0: nc.tensor.matmul(pt[0:64, 0:tw], lhsT=w1[:, :], rhs=zT[:, t0:t0 + tw],
                                                                            start=True, stop=True), r=[w1b, zTb], w=[pb])
                            sin_layer(pt, tw, pc[:, 1:2], bf[:, 0:1], h1T[:, t0:t0 + tw], h1b, pb)
                            pt, pb = psr.nxt()
                            c.op("pe", lambda pt=pt, t0=t0: nc.tensor.matmul(pt[0:64, 0:tw], lhsT=w2[:, :], rhs=h1T[:, t0:t0 + tw],
                                                                            start=True, stop=True), r=[w2b, h1b], w=[pb])
                            sin_layer(pt, tw, pc[:, 3:4], bf[:, 1:2], h2T_[:, t0:t0 + tw], h2b, pb)
                        for tc in range(ntc):
                            hh = []
                            for d in range(2):
                                pt, pb = psr.nxt()
                                c.op("pe", lambda pt=pt, tc=tc, d=d: nc.tensor.matmul(
                                    pt[:], lhsT=h2T_[:, tc * 128:(tc + 1) * 128], rhs=w3[:, d * 512:(d + 1) * 512],
                                    start=True, stop=True), r=[h2b, w3b], w=[pb])
                                E, Eb = ef.nxt()
                                c.op("act", lambda E=E, d=d, tc=tc: nc.scalar.activation(
                                    out=E[:], in_=dec[:, d, :], func=AF.Exp, scale=nt01[:, tc:tc + 1]), r=[decb, ntb], w=[Eb])
                                c.op("dve", lambda E=E, pt=pt: nc.vector.tensor_tensor(out=E[:], in0=pt[:], in1=E[:], op=ALU.mult),
                                     r=[pb, Eb], w=[Eb])
                                hh.append((E, Eb))
                            (Hf, Hfb), (Hb_, Hbb) = hh
                            if tc == 0:
                                c.op("dve", lambda Hf=Hf: nc.vector.tensor_tensor(out=Hf[0:1, :], in0=Hf[0:1, :], in1=hbias[:],
                                                                                  op=ALU.add), r=[Hfb, hbb], w=[Hfb])
                            c.op("dve", lambda Hf=Hf, Hb_=Hb_, tc=tc: nc.vector.tensor_tensor(
                                out=HS[:, tc, :], in0=Hf[:], in1=Hb_[:], op=ALU.add), r=[Hfb, Hbb], w=[HSb])
                            c.op("pool", lambda Hf=Hf, Hb_=Hb_, tc=tc: nc.gpsimd.tensor_tensor(
                                out=HD[:, tc, :], in0=Hf[:], in1=Hb_[:], op=ALU.subtract), r=[Hfb, Hbb], w=[HDb])
                        pn, pnb = psr.nxt()
                        for tc in range(ntc):
                            c.op("pe", lambda tc=tc: nc.tensor.matmul(pn[0:1, :], lhsT=sf0[:, tc, :], rhs=HS[:, tc, :],
                                                                     start=(tc == 0), stop=(tc == ntc - 1)),
                                 r=[sf0b, HSb], w=[pnb])
                        c.op("act", lambda: nc.scalar.activation(out=hn[:], in_=pn[0:1, :], func=AF.Identity,
                                                                 scale=wf[0:1, 0:1]), r=[pnb, wfb], w=[hnb])
                        c.barrier()
                        s3.close()
                        s3 = ExitStack()
                        xs3 = Rot([c.sb(s3, "hx3", (128, 1024)) for _ in range(3)])
                        ua, uab = c.sb(s3, "hua", (128, 1024))
                        ub, ubb = c.sb(s3, "hub", (128, 1024))
                        for si, (s0, L, is_s, pi) in enumerate(seqs):
                            Ut, Utb = U[si]
                            for tc in range(ntc):
                                tl = []
                                for sh in (-1, 0, 1):
                                    X, Xb = xs3.nxt()
                                    lo = tc * 128 + sh
                                    a, b = max(lo, 0), min(lo + 128, L)
                                    if a != lo or b != lo + 128:
                                        c.op("pool", lambda X=X: nc.gpsimd.memset(X[:], 0.0), w=[Xb])
                                    c.dma("sp", X[a - lo:b - lo, :], projT[s0 + a:s0 + b, 2432:3456], r=[projT_b], w=[Xb])
                                    tl.append((X, Xb))
                                (Xm, Xmb), (X0, X0b), (Xp, Xpb) = tl
                                c.op("dve", lambda X0=X0: nc.vector.tensor_tensor(out=ua[:], in0=X0[:], in1=tapb[:, 1, :], op=ALU.mult),
                                     r=[X0b, tapbb], w=[uab])
                                c.op("pool", lambda Xm=Xm: nc.gpsimd.tensor_tensor(out=ub[:], in0=Xm[:], in1=tapb[:, 0, :], op=ALU.mult),
                                     r=[Xmb, tapbb], w=[ubb])
                                c.op("dve", lambda: nc.vector.tensor_tensor(out=ua[:], in0=ua[:], in1=ub[:], op=ALU.add),
                                     r=[uab, ubb], w=[uab])
                                c.op("pool", lambda Xp=Xp: nc.gpsimd.tensor_tensor(out=ub[:], in0=Xp[:], in1=tapb[:, 2, :], op=ALU.mult),
                                     r=[Xpb, tapbb], w=[ubb])
                                c.op("dve", lambda: nc.vector.tensor_tensor(out=ua[:], in0=ua[:], in1=ub[:], op=ALU.add),
                                     r=[uab, ubb], w=[uab])
                                c.op("dve", lambda Ut=Ut, tc=tc: nc.vector.tensor_tensor(
                                    out=Ut[:, tc, :], in0=ua[:, 0:512], in1=ua[:, 512:1024], op=ALU.mult), r=[uab], w=[Utb])
                        c.barrier()
                        s3.close()
                        s3 = ExitStack()
                        cfc = Rot([c.sb(s3, "hcfc", (128, 16, 128), BF16) for _ in range(2)])
                        sfc = Rot([c.sb(s3, "hsfc", (128, 16, 128), BF16) for _ in range(2)])
                        hsp = Rot([c.sb(s3, "hsp", (128, 512)) for _ in range(4)])
                        tmp = Rot([c.sb(s3, "htmp", (128, 512)) for _ in range(4)])
                        for fc in range(ntc):
                            Cc, Ccb = cfc.nxt()
                            Sc, Scb = sfc.nxt()
                            c.dma("sp", Cc[:, 0:ntc, :], Cfv[:, :, fc * 128:(fc + 1) * 128], w=[Ccb])
                            c.dma("sp", Sc[:, 0:ntc, :], Sfv[:, :, fc * 128:(fc + 1) * 128], w=[Scb])
                            Hs = []
                            for (M, Mb, Src, Srcb) in ((Cc, Ccb, HS, HSb), (Sc, Scb, HD, HDb)):
                                pt, pb = psr.nxt()
                                for tc in range(ntc):
                                    c.op("pe", lambda pt=pt, M=M, Src=Src, tc=tc: nc.tensor.matmul(
                                        pt[:], lhsT=M[:, tc, :], rhs=Src[:, tc, :], start=(tc == 0), stop=(tc == ntc - 1)),
                                        r=[Mb, Srcb], w=[pb])
                                Hx, Hxb = hsp.nxt()
                                c.op("act", lambda Hx=Hx, pt=pt, fc=fc: nc.scalar.activation(
                                    out=Hx[:], in_=pt[:], func=AF.Identity, scale=wf[:, fc:fc + 1]), r=[pb, wfb], w=[Hxb])
                                Hs.append((Hx, Hxb))
                            (Hre, Hreb), (Him, Himb) = Hs
                            if fc == 0:
                                c.op("dve", lambda Him=Him: nc.vector.tensor_copy(out=Him[0:1, :], in_=hn[:]), r=[hnb], w=[Himb])
                            for si in range(len(seqs)):
                                Ut, Utb = U[si]
                                Us = []
                                for (M, Mb) in ((Cc, Ccb), (Sc, Scb)):
                                    pt, pb = psr.nxt()
                                    for tc in range(ntc):
                                        c.op("pe", lambda pt=pt, M=M, Ut=Ut, tc=tc: nc.tensor.matmul(
                                            pt[:], lhsT=M[:, tc, :], rhs=Ut[:, tc, :], start=(tc == 0), stop=(tc == ntc - 1)),
                                            r=[Mb, Utb], w=[pb])
                                    Us.append((pt, pb))
                                (Ure, Ureb), (Uim, Uimb) = Us
                                T1, T1b = tmp.nxt()
                                T2, T2b = tmp.nxt()
                                T3, T3b = tmp.nxt()
                                T4, T4b = tmp.nxt()
                                c.op("dve", lambda: nc.vector.tensor_tensor(out=T1[:], in0=Ure[:], in1=Hre[:], op=ALU.mult),
                                     r=[Ureb, Hreb], w=[T1b])
                                c.op("dve", lambda: nc.vector.tensor_tensor(out=T2[:], in0=Uim[:], in1=Him[:], op=ALU.mult),
                                     r=[Uimb, Himb], w=[T2b])
                                c.op("dve", lambda: nc.vector.tensor_tensor(out=T3[:], in0=Ure[:], in1=Him[:], op=ALU.mult),
                                     r=[Ureb, Himb], w=[T3b])
                                c.op("dve", lambda: nc.vector.tensor_tensor(out=T4[:], in0=Uim[:], in1=Hre[:], op=ALU.mult),
                                     r=[Uimb, Hreb], w=[T4b])
                                Yr, Yrb = YRE[si]
                                Yi, Yib = YIM[si]
                                c.op("pool", lambda: nc.gpsimd.tensor_tensor(out=Yr[:, fc, :], in0=T1[:], in1=T2[:], op=ALU.subtract),
                                     r=[T1b, T2b], w=[Yrb])
                                c.op("pool", lambda: nc.gpsimd.tensor_tensor(out=Yi[:, fc, :], in0=T3[:], in1=T4[:], op=ALU.add),
                                     r=[T3b, T4b], w=[Yib])
                                if fc == 0:
                                    c.op("dve", lambda: nc.vector.tensor_copy(out=Yr[0:1, 0, :], in_=T1[0:1, :]), r=[T1b], w=[Yrb])
                                    c.op("dve", lambda: nc.vector.tensor_copy(out=Yi[0:1, 0, :], in_=T2[0:1, :]), r=[T2b], w=[Yib])
                        c.barrier()
                        s3.close()
                        s3 = ExitStack()
                        cft = Rot([c.sb(s3, "hcft", (128, 16, 512), BF16) for _ in range(2)])
                        sft = Rot([c.sb(s3, "hsft", (128, 16, 512), BF16) for _ in range(2)])
                        x0t = Rot([c.sb(s3, "hx0", (128, 514)) for _ in range(2)])
                        x0c = Rot([c.sb(s3, "hx0c", (128, 512)) for _ in range(2)])
                        yo = Rot([c.sb(s3, "hyo", (128, 512), BF16) for _ in range(2)])
                        for si, (s0, L, is_s, pi) in enumerate(seqs):
                            Yr, Yrb = YRE[si]
                            Yi, Yib = YIM[si]
                            for t0 in range(0, n, tw):
                                Ct, Ctb = cft.nxt()
                                St, Stb = sft.nxt()
                                c.dma("sp", Ct[:, 0:ntc, 0:tw], Cfv[:, :, t0:t0 + tw], w=[Ctb])
                                c.dma("sp", St[:, 0:ntc, 0:tw], STv[:, :, t0:t0 + tw], w=[Stb])
                                for cc in range(4):
                                    pt, pb = psr.nxt()
                                    for fc in range(ntc):
                                        c.op("pe", lambda pt=pt, fc=fc, cc=cc, Ct=Ct: nc.tensor.matmul(
                                            pt[:, 0:tw], lhsT=Yr[:, fc, cc * 128:(cc + 1) * 128], rhs=Ct[:, fc, 0:tw],
                                            start=(fc == 0), stop=False), r=[Yrb, Ctb], w=[pb])
                                    for fc in range(ntc):
                                        c.op("pe", lambda pt=pt, fc=fc, cc=cc, St=St: nc.tensor.matmul(
                                            pt[:, 0:tw], lhsT=Yi[:, fc, cc * 128:(cc + 1) * 128], rhs=St[:, fc, 0:tw],
                                            start=False, stop=(fc == ntc - 1)), r=[Yib, Stb], w=[pb])
                                    X, Xb = x0t.nxt()
                                    lo = t0 - 1
                                    a, b = max(lo, 0), min(lo + tw + 2, L)
                                    if a != lo or b != lo + tw + 2:
                                        c.op("pool", lambda X=X: nc.gpsimd.memset(X[:], 0.0), w=[Xb])
                                    ci = 15 + cc
                                    c.dma("sp", X[:, a - lo:b - lo], projF[ci * 128:(ci + 1) * 128, s0 + a:s0 + b],
                                          r=[projF_b], w=[Xb])
                                    Xc, Xcb = x0c.nxt()
                                    c.op("dve", lambda X=X, Xc=Xc, cc=cc: nc.vector.tensor_scalar(
                                        out=Xc[:, 0:tw], in0=X[:, 1:tw + 1], scalar1=tx0[:, 4 + cc:5 + cc], scalar2=None, op0=ALU.mult),
                                        r=[Xb, tx0b], w=[Xcb])
                                    c.op("dve", lambda X=X, Xc=Xc, cc=cc: nc.vector.scalar_tensor_tensor(
                                        out=Xc[:, 0:tw], in0=X[:, 0:tw], scalar=tx0[:, cc:cc + 1], in1=Xc[:, 0:tw],
                                        op0=ALU.mult, op1=ALU.add), r=[Xb, tx0b, Xcb], w=[Xcb])
                                    c.op("dve", lambda X=X, Xc=Xc, cc=cc: nc.vector.scalar_tensor_tensor(
                                        out=Xc[:, 0:tw], in0=X[:, 2:tw + 2], scalar=tx0[:, 8 + cc:9 + cc], in1=Xc[:, 0:tw],
                                        op0=ALU.mult, op1=ALU.add), r=[Xb, tx0b, Xcb], w=[Xcb])
                                    Yo, Yob = yo.nxt()
                                    c.op("dve", lambda Yo=Yo, Xc=Xc, pt=pt: nc.vector.tensor_tensor(
                                        out=Yo[:, 0:tw], in0=pt[:, 0:tw], in1=Xc[:, 0:tw], op=ALU.mult), r=[pb, Xcb], w=[Yob])
                                    c.dma("pool", mixT[512 + cc * 128:512 + (cc + 1) * 128, s0 + t0:s0 + t0 + tw], Yo[:, 0:tw],
                                          r=[Yob], w=[mixT_b])
                        c.barrier()
                        s3.close()
                c.barrier()

        epsl, epslb = c.sb(es, "epsl", (128, 1))
        c.op("dve", lambda: nc.vector.memset(epsl[:], 1e-5 / (ALPHA * ALPHA)), w=[epslb])

        def layer_norm(st, xt, xtb, gT, bT, gb, emit):
            sq, sqb = c.sb(st, "lnsq", (128, 16, 512))
            for kc in range(16):
                c.op("act", lambda kc=kc: nc.scalar.activation(out=sq[:, kc, :], in_=xt[:, kc, :], func=AF.Square),
                     r=[xtb], w=[sqb])
            pm, pmb = psr.nxt()
            pe2, pe2b = psr.nxt()
            for kc in range(16):
                c.op("pe", lambda kc=kc: nc.tensor.matmul(pm[:], lhsT=onesm[:], rhs=xt[:, kc, :], start=(kc == 0), stop=(kc == 15)),
                     r=[xtb, kb], w=[pmb])
            for kc in range(16):
                c.op("pe", lambda kc=kc: nc.tensor.matmul(pe2[:], lhsT=onesm[:], rhs=sq[:, kc, :], start=(kc == 0), stop=(kc == 15)),
                     r=[sqb, kb], w=[pe2b])
            mean, meanb = c.sb(st, "lnmean", (128, 512))
            rstd, rstdb = c.sb(st, "lnrstd", (128, 512))
            c.op("act", lambda: nc.scalar.copy(out=mean[:], in_=pm[:]), r=[pmb], w=[meanb])
            c.op("dve", lambda: nc.vector.tensor_tensor(out=rstd[:], in0=mean[:], in1=mean[:], op=ALU.mult), r=[meanb], w=[rstdb])
            c.op("dve", lambda: nc.vector.tensor_tensor(out=rstd[:], in0=pe2[:], in1=rstd[:], op=ALU.subtract), r=[pe2b, rstdb], w=[rstdb])
            c.op("act", lambda: nc.scalar.activation(out=rstd[:], in_=rstd[:], func=AF.Sqrt, bias=epsl[:, 0:1]), r=[rstdb, epslb], w=[rstdb])
            c.op("dve", lambda: nc.vector.reciprocal(out=rstd[:], in_=rstd[:]), r=[rstdb], w=[rstdb])
            tk = Rot([c.sb(st, "lntk", (128, 512)) for _ in range(3)])
            for kc in range(16):
                T, Tb = tk.nxt()
                c.op("dve", lambda kc=kc, T=T: nc.vector.tensor_tensor(out=T[:], in0=xt[:, kc, :], in1=mean[:], op=ALU.subtract),
                     r=[xtb, meanb], w=[Tb])
                c.op("pool", lambda T=T: nc.gpsimd.tensor_tensor(out=T[:], in0=T[:], in1=rstd[:], op=ALU.mult), r=[Tb, rstdb], w=[Tb])
                c.op("act", lambda kc=kc, T=T: nc.scalar.activation(out=xt[:, kc, :], in_=T[:], func=AF.Identity,
                                                                     scale=gT[:, kc:kc + 1], bias=bT[:, kc:kc + 1]),
                     r=[Tb, gb], w=[xtb])
                emit(kc, T, Tb)

        def stage_C1(l):
            mt, mtb = modT[l]
            with ExitStack() as st:
                gT, gTb = load_cols(st, "l1g", I["ln1_g"][l].rearrange("(a p) -> a p", p=128), 16)
                bT, bTb = load_cols(st, "l1b", I["ln1_b"][l].rearrange("(a p) -> a p", p=128), 16)
                G2, G2b = c.sb(st, "G2", (128, 16, 2))
                B2, B2b = c.sb(st, "B2", (128, 16, 2))
                c.op("dve", lambda: nc.vector.tensor_tensor(out=G2[:], in0=mt[:, 64:80, :],
                                                            in1=gT[:, :].unsqueeze(2).to_broadcast([128, 16, 2]), op=ALU.mult),
                     r=[mtb, gTb], w=[G2b])
                c.op("dve", lambda: nc.vector.tensor_tensor(out=B2[:], in0=mt[:, 64:80, :],
                                                            in1=bT[:, :].unsqueeze(2).to_broadcast([128, 16, 2]), op=ALU.mult),
                     r=[mtb, bTb], w=[B2b])
                c.op("dve", lambda: nc.vector.tensor_tensor(out=B2[:], in0=B2[:], in1=mt[:, 48:64, :], op=ALU.add),
                     r=[mtb, B2b], w=[B2b])
                lb = Buf("lnp")
                lb.w.update(gTb.w); lb.w.update(bTb.w)
                Wap, Wbuf = wb[("w_out", l)]
                Wv = Wap.rearrange("(kc p) n -> p kc n", p=128)
                for tt in range(5):
                    gi = 0 if tt == 0 else 1
                    with ExitStack() as s2:
                        M, Mb = c.sb(s2, "c1M", (128, 16, 512), BF16)
                        xt, xtb = c.sb(s2, "c1x", (128, 16, 512))
                        h2o, h2ob = c.sb(s2, "c1h", (128, 16, 512), BF16)
                        wt = Rot([c.sb(s2, "c1w", (128, 16, 512), BF16) for _ in range(2)])
                        c.dma("sp", M[:], mixTv[:, :, tt * 512:(tt + 1) * 512], r=[mixT_b], w=[Mb])
                        c.dma("sp", xt[:], xTv[:, :, tt * 512:(tt + 1) * 512], r=[xT_b[tt]], w=[xtb])
                        for db in range(4):
                            W, Wb_ = wt.nxt()
                            c.dma("sp", W[:], Wv[:, :, db * 512:(db + 1) * 512], r=[Wbuf], w=[Wb_])
                            for jj in range(4):
                                dc = db * 4 + jj
                                pt, pb = psr.nxt()
                                for kc in range(16):
                                    c.op("pe", lambda kc=kc, pt=pt, W=W, jj=jj: nc.tensor.matmul(
                                        pt[:], lhsT=W[:, kc, jj * 128:(jj + 1) * 128], rhs=M[:, kc, :],
                                        start=(kc == 0), stop=(kc == 15)), r=[Wb_, Mb], w=[pb])
                                c.op("dve", lambda dc=dc, pt=pt: nc.vector.scalar_tensor_tensor(
                                    out=xt[:, dc, :], in0=pt[:], scalar=mt[:, 32 + dc, gi:gi + 1], in1=xt[:, dc, :],
                                    op0=ALU.mult, op1=ALU.add), r=[pb, mtb, xtb], w=[xtb])

                        def emit(kc, T, Tb):
                            c.op("dve", lambda: nc.vector.tensor_scalar(
                                out=h2o[:, kc, :], in0=T[:], scalar1=G2[:, kc, gi:gi + 1], scalar2=B2[:, kc, gi:gi + 1],
                                op0=ALU.mult, op1=ALU.add), r=[Tb, G2b, B2b], w=[h2ob])
                        layer_norm(s2, xt, xtb, gT, bT, lb, emit)
                        c.dma("pool", xTv[:, :, tt * 512:(tt + 1) * 512], xt[:], r=[xtb], w=[xT_b[tt]])
                        c.dma("pool", h2Tv[:, :, tt * 512:(tt + 1) * 512], h2o[:], r=[h2ob], w=[h2T_b])
                        c.barrier()

        def stage_C2(l):
            mt, mtb = modT[l]
            last = (l == DEPTH - 1)
            with ExitStack() as st:
                gT, gTb = load_cols(st, "l2g", I["ln2_g"][l].rearrange("(a p) -> a p", p=128), 16)
                bT, bTb = load_cols(st, "l2b", I["ln2_b"][l].rearrange("(a p) -> a p", p=128), 16)
                lb = Buf("lnp2")
                lb.w.update(gTb.w); lb.w.update(bTb.w)
                taps = []
                for j in range(3):
                    taps.append(load_cols(st, "ftap%d" % j, I["ffn_conv"][l, j].rearrange("(a p) -> a p", p=128), 88))
                tapb = Buf("ftaps")
                for _, b in taps:
                    tapb.w.update(b.w)
                Uap, Ubuf = wb[("ffn_up", l)]
                Uv = Uap.rearrange("(kc p) n -> p kc n", p=128)
                Dap, Dbuf = wb[("ffn_down", l)]
                Dv = Dap.rearrange("(kc p) n -> p kc n", p=128)
                for tt in range(5):
                    gi = 0 if tt == 0 else 1
                    t0 = tt * 512
                    hl = tt >= 2
                    hr = 1 <= tt <= 3
                    segs = [(0, 256), (256, 512)] if tt == 0 else [(0, 512)]
                    with ExitStack() as s2:
                        F, Fb = c.sb(s2, "c2F", (128, 44, 512), BF16)
                        with ExitStack() as s3:
                            H, Hb = c.sb(s3, "c2H", (128, 16, 514), BF16)
                            c.dma("sp", H[:, :, 1:513], h2Tv[:, :, t0:t0 + 512], r=[h2T_b], w=[Hb])
                            if hl:
                                c.dma("sp", H[:, :, 0:1], h2Tv[:, :, t0 - 1:t0], r=[h2T_b], w=[Hb])
                            if hr:
                                c.dma("sp", H[:, :, 513:514], h2Tv[:, :, t0 + 512:t0 + 513], r=[h2T_b], w=[Hb])
                            wa = Rot([c.sb(s3, "c2wa", (128, 16, 512), BF16) for _ in range(2)])
                            wbb = Rot([c.sb(s3, "c2wb", (128, 16, 512), BF16) for _ in range(2)])
                            uu = Rot([c.sb(s3, "c2u", (128, 512)) for _ in range(4)])
                            for jb in range(11):
                                Wa, Wab = wa.nxt()
                                Wb2, Wb2b = wbb.nxt()
                                c.dma("sp", Wa[:], Uv[:, :, jb * 512:(jb + 1) * 512], r=[Ubuf], w=[Wab])
                                c.dma("sp", Wb2[:], Uv[:, :, DFF + jb * 512:DFF + (jb + 1) * 512], r=[Ubuf], w=[Wb2b])
                                for jj in range(4):
                                    j = jb * 4 + jj
                                    us = []
                                    for (W, Wbf, cidx) in ((Wa, Wab, j), (Wb2, Wb2b, 44 + j)):
                                        pt, pb = psr.nxt()
                                        for kc in range(16):
                                            c.op("pe", lambda kc=kc, pt=pt, W=W, jj=jj: nc.tensor.matmul(
                                                pt[:], lhsT=W[:, kc, jj * 128:(jj + 1) * 128], rhs=H[:, kc, 1:513],
                                                start=(kc == 0), stop=(kc == 15)), r=[Wbf, Hb], w=[pb])
                                        ph, phb = (None, None)
                                        if hl or hr:
                                            ph, phb = psr.nxt()
                                            for (flag, col, o) in ((hl, 0, 0), (hr, 513, 1)):
                                                if not flag:
                                                    continue
                                                for kc in range(16):
                                                    c.op("pe", lambda kc=kc, ph=ph, W=W, jj=jj, col=col, o=o: nc.tensor.matmul(
                                                        ph[:, o:o + 1], lhsT=W[:, kc, jj * 128:(jj + 1) * 128], rhs=H[:, kc, col:col + 1],
                                                        start=(kc == 0), stop=(kc == 15)), r=[Wbf, Hb], w=[phb])
                                        Ut, Utb = uu.nxt()
                                        w0 = taps[0][0][:, cidx:cidx + 1]
                                        w1 = taps[1][0][:, cidx:cidx + 1]
                                        w2 = taps[2][0][:, cidx:cidx + 1]
                                        c.op("act", lambda Ut=Ut, pt=pt, w1=w1: nc.scalar.activation(
                                            out=Ut[:], in_=pt[:], func=AF.Identity, scale=w1), r=[pb, tapb], w=[Utb])
                                        for (a, b) in segs:
                                            c.op("dve", lambda Ut=Ut, pt=pt, w0=w0, a=a, b=b: nc.vector.scalar_tensor_tensor(
                                                out=Ut[:, a + 1:b], in0=pt[:, a:b - 1], scalar=w0, in1=Ut[:, a + 1:b],
                                                op0=ALU.mult, op1=ALU.add), r=[pb, tapb, Utb], w=[Utb])
                                            c.op("dve", lambda Ut=Ut, pt=pt, w2=w2, a=a, b=b: nc.vector.scalar_tensor_tensor(
                                                out=Ut[:, a:b - 1], in0=pt[:, a + 1:b], scalar=w2, in1=Ut[:, a:b - 1],
                                                op0=ALU.mult, op1=ALU.add), r=[pb, tapb, Utb], w=[Utb])
                                        if hl:
                                            c.op("dve", lambda Ut=Ut, ph=ph, w0=w0: nc.vector.scalar_tensor_tensor(
                                                out=Ut[:, 0:1], in0=ph[:, 0:1], scalar=w0, in1=Ut[:, 0:1],
                                                op0=ALU.mult, op1=ALU.add), r=[phb, tapb, Utb], w=[Utb])
                                        if hr:
                                            c.op("dve", lambda Ut=Ut, ph=ph, w2=w2: nc.vector.scalar_tensor_tensor(
                                                out=Ut[:, 511:512], in0=ph[:, 1:2], scalar=w2, in1=Ut[:, 511:512],
                                                op0=ALU.mult, op1=ALU.add), r=[phb, tapb, Utb], w=[Utb])
                                        us.append((Ut, Utb))
                                    (Ua, Uab), (Ub_, Ubb) = us
                                    c.op("act", lambda Ua=Ua: nc.scalar.activation(out=Ua[:], in_=Ua[:], func=AF.Silu), r=[Uab], w=[Uab])
                                    c.op("pool", lambda Ua=Ua, Ub_=Ub_, j=j: nc.gpsimd.tensor_tensor(
                                        out=F[:, j, :], in0=Ua[:], in1=Ub_[:], op=ALU.mult), r=[Uab, Ubb], w=[Fb])
                            c.barrier()
                        xt, xtb = c.sb(s2, "c2x", (128, 16, 512))
                        c.dma("sp", xt[:], xTv[:, :, t0:t0 + 512], r=[xT_b[tt]], w=[xtb])
                        with ExitStack() as s3:
                            wd = Rot([c.sb(s3, "c2wd", (128, 44, 256), BF16) for _ in range(2)])
                            for db in range(8):
                                W, Wb_ = wd.nxt()
                                c.dma("sp", W[:], Dv[:, :, db * 256:(db + 1) * 256], r=[Dbuf], w=[Wb_])
                                for jj in range(2):
                                    dc = db * 2 + jj
                                    pt, pb = psr.nxt()
                                    for kc in range(44):
                                        c.op("pe", lambda kc=kc, pt=pt, W=W, jj=jj: nc.tensor.matmul(
                                            pt[:], lhsT=W[:, kc, jj * 128:(jj + 1) * 128], rhs=F[:, kc, :],
                                            start=(kc == 0), stop=(kc == 43)), r=[Wb_, Fb], w=[pb])
                                    c.op("dve", lambda dc=dc, pt=pt: nc.vector.scalar_tensor_tensor(
                                        out=xt[:, dc, :], in0=pt[:], scalar=mt[:, 80 + dc, gi:gi + 1], in1=xt[:, dc, :],
                                        op0=ALU.mult, op1=ALU.add), r=[pb, mtb, xtb], w=[xtb])
                            c.barrier()
                        with ExitStack() as s3:
                            layer_norm(s3, xt, xtb, gT, bT, lb, lambda kc, T, Tb: None)
                            if not last:
                                c.dma("pool", xTv[:, :, t0:t0 + 512], xt[:], r=[xtb], w=[xT_b[tt]])
                            else:
                                orow = Rot([c.sb(s3, "c2o", (128, D)) for _ in range(2)])
                                for ts in range(4):
                                    Or, Orb = orow.nxt()
                                    for g in range(4):
                                        pt, pb = psr.nxt()
                                        for j4 in range(4):
                                            kc = g * 4 + j4
                                            c.op("pe", lambda kc=kc, j4=j4, pt=pt, ts=ts: nc.tensor.transpose(
                                                out=pt[:, j4 * 128:(j4 + 1) * 128], in_=xt[:, kc, ts * 128:(ts + 1) * 128],
                                                identity=ident[:]), r=[xtb, kb], w=[pb])
                                        evac(g, Or[:, g * 512:(g + 1) * 512], pt[:], [pb], [Orb])
                                    tok = t0 + ts * 128
                                    dst = O["yp"][tok:tok + 128, :] if tok < 512 else O["ys"][tok - 512:tok - 384, :]
                                    c.dma("pool", dst, Or[:], r=[Orb], w=[out_b])
                            c.barrier()

        RNAMES = ["kap", "w0", "w1", "b0", "b1", "kd0", "kd1", "r", "bon", "g", "v", "y0", "y1"]
        if dbg == "r2":
            RA = {n: c.dram("rw_" + n, (NTOK, 512), kind=("ExternalOutput" if n in ("y0", "y1") else "ExternalInput"))
                  for n in RNAMES}
        else:
            RA = {n: c.dram("rw_" + n, (NTOK, 512)) for n in RNAMES}
        VT, VTb = c.dram("rw_vT", (512, NTOK))
        YT = [c.dram("rw_yT%d" % d, (512, NTOK)) for d in range(2)]

        def stage_rwkv(l, parts=("r1", "r2", "r3")):
            m0 = c.mute
            c.mute = m0 or ("r1" not in parts)
            with ExitStack() as st:
                tapb, tapbb = c.sb(st, "rtap", (128, 3, 1536))
                for j in range(3):
                    c.dma("sp", tapb[:, j, :], I["rwkv_shift"][l, j, 0:1536].partition_broadcast(128), w=[tapbb])
                bc = {}
                for nm, src in (("kk", I["rwkv_kk"][l]), ("ka", I["rwkv_ka"][l]), ("rk", I["rwkv_rk"][l]),
                                ("w00", I["rwkv_w0"][l, 0]), ("w01", I["rwkv_w0"][l, 1]),
                                ("a00", I["rwkv_a0"][l, 0]), ("a01", I["rwkv_a0"][l, 1])):
                    t, b = c.sb(st, "rb_" + nm, (128, 512))
                    c.dma("sp", t[:], src.partition_broadcast(128), w=[b])
                    bc[nm] = (t, b)
                ltap, ltapb = load_cols(st, "rltap", [I["rwkv_shift"][l, j, 1536:1920].rearrange("(c p) -> c p", p=128) for j in range(3)], 9)
                w2, w2b = c.sb(st, "rw2", (128, 512))
                a2, a2b = c.sb(st, "ra2", (128, 512))
                g2, g2b = c.sb(st, "rg2", (128, 512))
                c.dma("sp", w2[:], I["rwkv_w2"][l].rearrange("d r n -> (d r) n"), w=[w2b])
                c.dma("sp", a2[:], I["rwkv_a2"][l].rearrange("d r n -> (d r) n"), w=[a2b])
                c.dma("sp", g2[:], I["rwkv_g2"][l], w=[g2b])
                eps12, eps12b = c.sb(st, "reps", (128, 1))
                c.op("dve", lambda: nc.vector.memset(eps12[:], 1e-12), w=[eps12b])
                x3 = [c.sb(st, "rx3", (128, 1536)) for _ in range(3)]
                rkv, rkvb = c.sb(st, "rrkv", (128, 1536))
                tq, tqb = c.sb(st, "rtq", (128, 1536))
                lx = Rot([c.sb(st, "rlx", (128, 130)) for _ in range(2)])
                lt = [c.sb(st, "rlt", (128, 128)) for _ in range(3)]
                wk = Rot([c.sb(st, "rwk", (128, 512)) for _ in range(12)])
                sm = Rot([c.sb(st, "rsm", (128, 8)) for _ in range(8)])
                vo = Rot([c.sb(st, "rvo", (128, 4, 128)) for _ in range(2)])
                for (s0, L, is_s, pi) in SEQS:
                    for q0 in range(0, L, 128):
                        g0 = s0 + q0
                        for k_, sh in enumerate((-1, 0, 1)):
                            X, Xb = x3[k_]
                            lo = q0 + sh
                            a, b = max(lo, 0), min(lo + 128, L)
                            if a != lo or b != lo + 128:
                                c.op("pool", lambda X=X: nc.gpsimd.memset(X[:], 0.0), w=[Xb])
                            c.dma("sp", X[a - lo:b - lo, :], projT[s0 + a:s0 + b, 0:1536], r=[projT_b], w=[Xb])
                        c.op("dve", lambda: nc.vector.tensor_tensor(out=rkv[:], in0=x3[1][0][:], in1=tapb[:, 1, :], op=ALU.mult),
                             r=[x3[1][1], tapbb], w=[rkvb])
                        for k_, j in ((0, 0), (2, 2)):
                            c.op("pool", lambda k_=k_, j=j: nc.gpsimd.tensor_tensor(out=tq[:], in0=x3[k_][0][:], in1=tapb[:, j, :], op=ALU.mult),
                                 r=[x3[k_][1], tapbb], w=[tqb])
                            c.op("dve", lambda: nc.vector.tensor_tensor(out=rkv[:], in0=rkv[:], in1=tq[:], op=ALU.add),
                                 r=[rkvb, tqb], w=[rkvb])
                        R_ = rkv[:, 0:512]
                        K_ = rkv[:, 512:1024]
                        V_ = rkv[:, 1024:1536]
                        for n_, ci in enumerate((12, 13, 14)):
                            X, Xb = lx.nxt()
                            lo = q0 - 1
                            a, b = max(lo, 0), min(lo + 130, L)
                            if a != lo or b != lo + 130:
                                c.op("pool", lambda X=X: nc.gpsimd.memset(X[:], 0.0), w=[Xb])
                            c.dma("sp", X[:, a - lo:b - lo], projF[ci * 128:(ci + 1) * 128, s0 + a:s0 + b], r=[projF_b], w=[Xb])
                            T_, Tb_ = lt[n_]
                            c.op("dve", lambda X=X, T_=T_, n_=n_: nc.vector.tensor_scalar(
                                out=T_[:], in0=X[:, 1:129], scalar1=ltap[:, 3 + n_:4 + n_], scalar2=None, op0=ALU.mult),
                                r=[Xb, ltapb], w=[Tb_])
                            c.op("dve", lambda X=X, T_=T_, n_=n_: nc.vector.scalar_tensor_tensor(
                                out=T_[:], in0=X[:, 0:128], scalar=ltap[:, n_:n_ + 1], in1=T_[:], op0=ALU.mult, op1=ALU.add),
                                r=[Xb, ltapb, Tb_], w=[Tb_])
                            c.op("dve", lambda X=X, T_=T_, n_=n_: nc.vector.scalar_tensor_tensor(
                                out=T_[:], in0=X[:, 2:130], scalar=ltap[:, 6 + n_:7 + n_], in1=T_[:], op0=ALU.mult, op1=ALU.add),
                                r=[Xb, ltapb, Tb_], w=[Tb_])
                            if ci == 12:
                                c.op("act", lambda T_=T_: nc.scalar.activation(out=T_[:], in_=T_[:], func=AF.Tanh), r=[Tb_], w=[Tb_])
                            elif ci == 14:
                                c.op("act", lambda T_=T_: nc.scalar.activation(out=T_[:], in_=T_[:], func=AF.Sigmoid), r=[Tb_], w=[Tb_])

                        def store(nm, T, Tb):
                            c.dma("pool", RA[nm][0][g0:g0 + 128, :], T[:], r=[Tb], w=[RA[nm][1]])

                        KK, KKb = wk.nxt()
                        SQ, SQb = wk.nxt()
                        c.op("dve", lambda: nc.vector.tensor_tensor(out=KK[:], in0=K_, in1=bc["kk"][0][:], op=ALU.mult),
                             r=[rkvb, bc["kk"][1]], w=[KKb])
                        c.op("pool", lambda: nc.gpsimd.tensor_tensor(out=SQ[:], in0=KK[:], in1=KK[:], op=ALU.mult), r=[KKb], w=[SQb])
                        SS, SSb = sm.nxt()
                        c.op("dve", lambda: nc.vector.tensor_reduce(out=SS[:], in_=SQ[:].rearrange("p (h k) -> p h k", h=8),
                                                                    axis=AX.X, op=ALU.add), r=[SQb], w=[SSb])
                        c.op("act", lambda: nc.scalar.activation(out=SS[:], in_=SS[:], func=AF.Sqrt, bias=eps12[:, 0:1]),
                             r=[SSb, eps12b], w=[SSb])
                        c.op("dve", lambda: nc.vector.reciprocal(out=SS[:], in_=SS[:]), r=[SSb], w=[SSb])
                        KAP, KAPb = wk.nxt()
                        c.op("dve", lambda: nc.vector.tensor_tensor(
                            out=KAP[:].rearrange("p (h k) -> p h k", h=8), in0=KK[:].rearrange("p (h k) -> p h k", h=8),
                            in1=SS[:, :].unsqueeze(2).to_broadcast([128, 8, 64]), op=ALU.mult), r=[KKb, SSb], w=[KAPb])
                        KAN, KANb = wk.nxt()
                        c.op("act", lambda: nc.scalar.mul(out=KAN[:], in_=KAP[:], mul=-1.0), r=[KAPb], w=[KANb])
                        store("kap", KAN, KANb)
                        RR, RRb = wk.nxt()
                        c.op("pool", lambda: nc.gpsimd.tensor_tensor(out=RR[:], in0=R_, in1=bc["rk"][0][:], op=ALU.mult),
                             r=[rkvb, bc["rk"][1]], w=[RRb])
                        bss = []
                        for d in range(2):
                            lo_, hi_ = d * 64, (d + 1) * 64
                            pw, pwb = psr.nxt()
                            c.op("pe", lambda pw=pw, lo_=lo_, hi_=hi_: nc.tensor.matmul(
                                pw[:], lhsT=lt[0][0][lo_:hi_, :], rhs=w2[lo_:hi_, :], start=True, stop=True),
                                r=[lt[0][1], w2b], w=[pwb])
                            pa, pab = psr.nxt()
                            c.op("pe", lambda pa=pa, lo_=lo_, hi_=hi_: nc.tensor.matmul(
                                pa[:], lhsT=lt[1][0][lo_:hi_, :], rhs=a2[lo_:hi_, :], start=True, stop=True),
                                r=[lt[1][1], a2b], w=[pab])
                            Wt, Wtb = wk.nxt()
                            c.op("dve", lambda Wt=Wt, pw=pw, d=d: nc.vector.tensor_tensor(out=Wt[:], in0=pw[:], in1=bc["w0%d" % d][0][:], op=ALU.add),
                                 r=[pwb, bc["w0%d" % d][1]], w=[Wtb])
                            c.op("act", lambda Wt=Wt: nc.scalar.activation(out=Wt[:], in_=Wt[:], func=AF.Sigmoid), r=[Wtb], w=[Wtb])
                            c.op("act", lambda Wt=Wt: nc.scalar.mul(out=Wt[:], in_=Wt[:], mul=-math.exp(-0.5)), r=[Wtb], w=[Wtb])
                            store("w%d" % d, Wt, Wtb)
                            At, Atb = wk.nxt()
                            c.op("dve", lambda At=At, pa=pa, d=d: nc.vector.tensor_tensor(out=At[:], in0=pa[:], in1=bc["a0%d" % d][0][:], op=ALU.add),
                                 r=[pab, bc["a0%d" % d][1]], w=[Atb])
                            c.op("act", lambda At=At: nc.scalar.activation(out=At[:], in_=At[:], func=AF.Sigmoid), r=[Atb], w=[Atb])
                            Bt, Btb = wk.nxt()
                            c.op("pool", lambda Bt=Bt, At=At: nc.gpsimd.tensor_tensor(out=Bt[:], in0=KAP[:], in1=At[:], op=ALU.mult),
                                 r=[KAPb, Atb], w=[Btb])
                            store("b%d" % d, Bt, Btb)
                            Kd, Kdb = wk.nxt()
                            c.op("dve", lambda Kd=Kd, At=At: nc.vector.scalar_tensor_tensor(
                                out=Kd[:], in0=At[:], scalar=-1.0, in1=bc["ka"][0][:], op0=ALU.add, op1=ALU.mult),
                                r=[Atb, bc["ka"][1]], w=[Kdb])
                            c.op("dve", lambda Kd=Kd: nc.vector.scalar_tensor_tensor(
                                out=Kd[:], in0=Kd[:], scalar=1.0, in1=K_, op0=ALU.add, op1=ALU.mult), r=[Kdb, rkvb], w=[Kdb])
                            store("kd%d" % d, Kd, Kdb)
                            c.op("pool", lambda At=At, Kd=Kd: nc.gpsimd.tensor_tensor(out=At[:], in0=RR[:], in1=Kd[:], op=ALU.mult),
                                 r=[RRb, Kdb, Atb], w=[Atb])
                            BS, BSb = sm.nxt()
                            c.op("dve", lambda BS=BS, At=At: nc.vector.tensor_reduce(
                                out=BS[:], in_=At[:].rearrange("p (h k) -> p h k", h=8), axis=AX.X, op=ALU.add), r=[Atb], w=[BSb])
                            bss.append((BS, BSb))
                        c.op("dve", lambda: nc.vector.tensor_tensor(out=bss[0][0][:], in0=bss[0][0][:], in1=bss[1][0][:], op=ALU.add),
                             r=[bss[0][1], bss[1][1]], w=[bss[0][1]])
                        BO, BOb = wk.nxt()
                        c.op("dve", lambda: nc.vector.tensor_tensor(
                            out=BO[:].rearrange("p (h k) -> p h k", h=8), in0=V_.rearrange("p (h k) -> p h k", h=8),
                            in1=bss[0][0][:, :].unsqueeze(2).to_broadcast([128, 8, 64]), op=ALU.mult), r=[rkvb, bss[0][1]], w=[BOb])
                        store("bon", BO, BOb)
                        Rc, Rcb = wk.nxt()
                        c.op("act", lambda: nc.scalar.copy(out=Rc[:], in_=R_), r=[rkvb], w=[Rcb])
                        store("r", Rc, Rcb)
                        pg, pgb = psr.nxt()
                        c.op("pe", lambda: nc.tensor.matmul(pg[:], lhsT=lt[2][0][:, :], rhs=g2[:, :], start=True, stop=True),
                             r=[lt[2][1], g2b], w=[pgb])
                        Gt, Gtb = wk.nxt()
                        c.op("act", lambda: nc.scalar.copy(out=Gt[:], in_=pg[:]), r=[pgb], w=[Gtb])
                        store("g", Gt, Gtb)
                        Vc, Vcb = wk.nxt()
                        c.op("act", lambda: nc.scalar.copy(out=Vc[:], in_=V_), r=[rkvb], w=[Vcb])
                        store("v", Vc, Vcb)
                c.barrier()
            c.mute = m0 or ("r2" not in parts)
            G = 4
            with ExitStack() as st:
                KC = {}
                for nm, w_ in (("tri", 128), ("blk", 128), ("selp", 4), ("mbd", 128), ("mh", 512), ("cm", 512),
                               ("msi", 256), ("msl", 128), ("trib", 128), ("msib", 256), ("mslb", 128)):
                    t, b = c.sb(st, "rk_" + nm, (128, w_))
                    c.dma("sp", t[:], I["k_" + nm][:, :], w=[b])
                    KC[nm] = (t, b)
                TRI, TRIb = KC["tri"]; BLK, BLKb = KC["blk"]; SELP, SELPb = KC["selp"]; MBD, MBDb = KC["mbd"]
                MH, MHb = KC["mh"]; CM, CMb = KC["cm"]
                QN = ["lam", "kap", "b", "kd", "r", "v"]
                class PV:
                    def __init__(self, t, off):
                        self.t, self.off = t, off

                    def __getitem__(self, key):
                        rows, cols = key
                        return self.t[rows, self.off + cols.start:self.off + cols.stop]

                class RSet:
                    pass
                BANKS = [(PV(PS[j][0], 0), PS[j][1]) for j in range(8)]
                PREP_RS = []
                for k in range(4):
                    R_ = RSet()
                    R_.w64 = Rot([c.sb(st, "rw64", (128, 64)) for _ in range(9)])
                    R_.w128 = Rot([c.sb(st, "rw128", (128, 128)) for _ in range(5)])
                    R_.w4 = Rot([c.sb(st, "rw4", (128, 4)) for _ in range(1)])
                    R_.r64 = Rot([c.sb(st, "rr64", (128, 64), F32R) for _ in range(1)])
                    R_.r128 = Rot([c.sb(st, "rr128", (128, 128), F32R) for _ in range(11)])
                    R_.r256 = Rot([c.sb(st, "rr256", (128, 256), F32R) for _ in range(3)])
                    R_.r512 = Rot([c.sb(st, "rr512", (128, 512), F32R) for _ in range(2)])
                    PREP_RS.append(R_)
                CHAIN_RS = []
                for k in range(2):
                    R_ = RSet()
                    R_.w256 = Rot([c.sb(st, "rcw256", (128, 256)) for _ in range(6)])
                    R_.w64 = Rot([c.sb(st, "rcw64", (128, 64)) for _ in range(4)])
                    R_.r64 = Rot([c.sb(st, "rcr64", (128, 64), F32R) for _ in range(4)])
                    R_.r256 = Rot([c.sb(st, "rcr256", (128, 256), F32R) for _ in range(2)])
                    R_.accA = (PV(PS[2 * k][0], 0), PS[2 * k][1])
                    R_.accB = (PV(PS[2 * k + 1][0], 0), PS[2 * k + 1][1])
                    R_.pu = (PV(PS[2 * k + 1][0], 256), PS[2 * k + 1][1])
                    CHAIN_RS.append(R_)

                def f(ap):
                    return ap.bitcast(F32)
                snat = Rot([c.sb(st, "rsnat", (64, 4, 128)) for _ in range(2)])

                def bc2(ap64):
                    return ap64.unsqueeze(1).to_broadcast([128, 2, 64])

                def run_streams(streams, nchunk):
                    sst = ExitStack()
                    for sm_ in streams:
                        sm_["ST"] = c.sb(sst, "rST", (128, 4, 64))
                        sm_["ld"] = {q: Rot([c.sb(sst, "rld_" + q, (128, G, 64)) for _ in range(2)]) for q in QN}
                        sm_["yg"] = Rot([c.sb(sst, "ryg", (128, G, 64)) for _ in range(2)])
                        ST, STb = sm_["ST"]
                        d = sm_["d"]
                        if sm_["is_s"]:
                            Sn, Snb = snat.nxt()
                            src = I["st"][l, d].rearrange("h v k -> v h k")
                            c.dma("sp", Sn[:].rearrange("v p x -> v (p x)").rearrange("v (h k) -> v h k", h=8), src, w=[Snb])
                            pt, pb = psr.nxt()
                            for p in range(4):
                                c.op("pe", lambda p=p, pt=pt, Sn=Sn: nc.tensor.transpose(
                                    out=pt[:, p * 64:(p + 1) * 64], in_=Sn[:, p, :], identity=ident[0:64, 0:64]),
                                    r=[Snb, kb], w=[pb])
                            c.op("dve", lambda pt=pt, ST=ST: nc.vector.tensor_copy(out=ST[:].rearrange("q p v -> q (p v)"), in_=pt[:, 0:256]),
                                 r=[pb], w=[STb])
                        else:
                            c.op("dve", lambda ST=ST: nc.vector.memset(ST[:], 0.0), w=[STb])

                    def gap(sm_, arr, g):
                        d = sm_["d"]
                        s0, L = sm_["s0"], sm_["L"]
                        t_ = RA[arr][0].tensor
                        if d == 0:
                            return bass.AP(tensor=t_, offset=(s0 + g * 16 * G) * 512, ap=[[64, 128], [8192, G], [1, 64]])
                        return bass.AP(tensor=t_, offset=(s0 + L - (g + 1) * 16 * G) * 512, ap=[[64, 128], [8192, G], [1, 64]])

                    def prep(sm_, ci, E, RS):
                        d = sm_["d"]
                        g, cg = divmod(ci, G)
                        if cg == 0:
                            sm_["cur"] = {}
                            for q in QN:
                                T_, Tb_ = sm_["ld"][q].nxt()
                                arr = {"lam": "w%d" % d, "kap": "kap", "b": "b%d" % d, "kd": "kd%d" % d, "r": "r", "v": "v"}[q]
                                E.dma("sp", T_[:], gap(sm_, arr, g), r=[RA[arr][1]], w=[Tb_])
                                sm_["cur"][q] = (T_, Tb_)
                            sm_["ycur"] = sm_["yg"].nxt()
                        cur = sm_["cur"]
                        sl = cg if d == 0 else G - 1 - cg
                        TRI, TRIb = KC["tri"] if d == 0 else KC["trib"]
                        MSI, MSIb = KC["msi"] if d == 0 else KC["msib"]
                        MSL, MSLb = KC["msl"] if d == 0 else KC["mslb"]
                        lam, lamb = cur["lam"][0][:, sl, :], cur["lam"][1]
                        P = {}
                        import os
                        STOP = float(os.environ.get("DBG_PREP_STOP", "99"))
                        if STOP <= 0:
                            return P
                        pc, pcb = RS.ph.nxt()
                        E.op("pe", lambda: nc.tensor.matmul(pc[:, 0:64], lhsT=TRI[:], rhs=lam, start=True, stop=True), r=[TRIb, lamb], w=[pcb])
                        E.op("pe", lambda: nc.tensor.matmul(pc[:, 64:128], lhsT=BLK[:], rhs=lam, start=True, stop=True), r=[BLKb, lamb], w=[pcb])
                        LT, LTb = RS.w64.nxt()
                        E.op("act", lambda: nc.scalar.copy(out=LT[:], in_=pc[:, 64:128]), r=[pcb], w=[LTb])
                        dd, ddb = RS.w64.nxt()
                        E.op("dve", lambda: nc.vector.tensor_tensor(out=dd[:], in0=pc[:, 0:64], in1=LT[:], op=ALU.subtract), r=[pcb, LTb], w=[ddb])
                        Eq, Eqb = RS.w64.nxt()
                        Eb, Ebb = RS.w64.nxt()
                        Ea, Eab = RS.w64.nxt()
                        E.op("act", lambda: nc.scalar.activation(out=Eq[:], in_=dd[:], func=AF.Exp), r=[ddb], w=[Eqb])
                        E.op("act", lambda: nc.scalar.activation(out=Eb[:], in_=dd[:], func=AF.Exp, scale=-1.0), r=[ddb], w=[Ebb])
                        E.op("pool", lambda: nc.gpsimd.tensor_tensor(out=Ea[:], in0=dd[:], in1=lam, op=ALU.subtract), r=[ddb, lamb], w=[Eab])
                        E.op("act", lambda: nc.scalar.activation(out=Ea[:], in_=Ea[:], func=AF.Exp), r=[Eab], w=[Eab])
                        if STOP <= 1:
                            return P
                        bds = {}
                        for nm, q, Eexp, Ebuf in (("q", "r", Eq, Eqb), ("a", "kap", Ea, Eab), ("b", "b", Eb, Ebb), ("k", "kd", Eb, Ebb)):
                            X, Xb = RS.w64.nxt()
                            src, srcb = cur[q][0][:, sl, :], cur[q][1]
                            E.op("pool", lambda X=X, src=src, Eexp=Eexp: nc.gpsimd.tensor_tensor(out=X[:], in0=src, in1=Eexp[:], op=ALU.mult),
                                 r=[srcb, Ebuf], w=[Xb])
                            Bd, Bdb = RS.w128.nxt()
                            eng = "dve" if nm in ("q", "b") else "pool"
                            fn = nc.vector.tensor_tensor if eng == "dve" else nc.gpsimd.tensor_tensor
                            E.op(eng, lambda Bd=Bd, X=X, fn=fn: fn(out=Bd[:].rearrange("p (a b) -> p a b", a=2), in0=bc2(X[:]),
                                                                   in1=MBD[:].rearrange("p (a b) -> p a b", a=2), op=ALU.mult),
                                 r=[Xb, MBDb], w=[Bdb])
                            bds[nm] = (Bd, Bdb)
                            if nm in ("b", "k"):
                                Ex, Exb = RS.r512.nxt()
                                E.op("pool", lambda Ex=Ex, X=X: nc.gpsimd.tensor_tensor(
                                    out=Ex[:].rearrange("p (a b) -> p a b", a=8), in0=X[:].unsqueeze(1).to_broadcast([128, 8, 64]),
                                    in1=MH[:].rearrange("p (a b) -> p a b", a=8), op=ALU.mult), r=[Xb, MHb], w=[Exb])
                                P[nm + "e"] = (Ex, Exb)
                        if STOP <= 2:
                            return P
                        Lb, Lbb = RS.w128.nxt()
                        E.op("dve", lambda: nc.vector.tensor_tensor(out=Lb[:].rearrange("p (a b) -> p a b", a=2), in0=bc2(lam),
                                                                    in1=MBD[:].rearrange("p (a b) -> p a b", a=2), op=ALU.mult),
                             r=[lamb, MBDb], w=[Lbb])
                        pp, ppb = RS.ph.nxt()
                        E.op("pe", lambda: nc.tensor.matmul(pp[:, 0:4], lhsT=Lb[:], rhs=SELP[:], start=True, stop=True), r=[Lbb, SELPb], w=[ppb])
                        PT, PTb = RS.w4.nxt()
                        E.op("act", lambda: nc.scalar.activation(out=PT[:], in_=pp[:, 0:4], func=AF.Exp), r=[ppb], w=[PTb])
                        P["PT"] = (PT, PTb)
                        if STOP <= 3:
                            return P
                        AQ, AQb = RS.r256.nxt()
                        BT, BTb = RS.r128.nxt()
                        KT, KTb = RS.r128.nxt()
                        for nm, dst, dstb in (("a", AQ[:, 0:128], AQb), ("q", AQ[:, 128:256], AQb), ("b", BT[:], BTb), ("k", KT[:], KTb)):
                            pt, pb = RS.ph.nxt()
                            Bd, Bdb = bds[nm]
                            E.op("pe", lambda pt=pt, Bd=Bd: nc.tensor.transpose(out=pt[:, 0:128], in_=Bd[:], identity=ident[:]),
                                 r=[Bdb, kb], w=[pb])
                            if STOP > 3.4:
                                E.op("act", lambda pt=pt, dst=dst: nc.scalar.copy(out=dst, in_=pt[:, 0:128]), r=[pb], w=[dstb])
                        P["AQ"] = (AQ, AQb)
                        if STOP <= 4:
                            return P
                        NB, NBb = RS.r256.nxt()
                        NK_, NKb = RS.r256.nxt()
                        for lt_, ltb_, dst, dstb in ((BT, BTb, NB, NBb), (KT, KTb, NK_, NKb)):
                            pt, pb = RS.ph.nxt()
                            E.op("pe", lambda pt=pt, lt_=lt_: nc.tensor.matmul(pt[:, 0:256], lhsT=lt_[:], rhs=AQ[:], start=True, stop=True),
                                 r=[ltb_, AQb], w=[pb])
                            E.op("dve", lambda pt=pt, dst=dst: nc.vector.tensor_tensor(out=dst[:], in0=pt[:, 0:256], in1=MSI[:], op=ALU.mult),
                                 r=[pb, MSIb], w=[dstb])
                        P["NB"] = (NB, NBb)
                        P["NK"] = (NK_, NKb)
                        N1, N1b = RS.r128.nxt()
                        pt, pb = RS.ph.nxt()
                        E.op("pe", lambda pt=pt: nc.tensor.matmul(pt[:, 0:128], lhsT=AQ[:, 0:128], rhs=BT[:], start=True, stop=True),
                             r=[AQb, BTb], w=[pb])
                        E.op("dve", lambda pt=pt: nc.vector.tensor_tensor(out=N1[:], in0=pt[:, 0:128], in1=MSL[:], op=ALU.mult),
                             r=[pb, MSLb], w=[N1b])
                        N1T = NB[:, 0:128]
                        if STOP <= 5:
                            return P

                        def mmev(lhsT, lb_, rhs, rb_, addt=None, addb=None):
                            pt, pb = RS.ph.nxt()
                            E.op("pe", lambda: nc.tensor.matmul(pt[:, 0:128], lhsT=lhsT, rhs=rhs, start=True, stop=True), r=[lb_, rb_], w=[pb])
                            O_, Ob_ = RS.r128.nxt()
                            if addt is None:
                                E.op("act", lambda: nc.scalar.copy(out=O_[:], in_=pt[:, 0:128]), r=[pb], w=[Ob_])
                            else:
                                E.op("dve", lambda: nc.vector.tensor_tensor(out=O_[:], in0=pt[:, 0:128], in1=f(addt), op=ALU.add),
                                     r=[pb, addb], w=[Ob_])
                            return O_, Ob_
                        N2, N2b = mmev(N1T, NBb, N1[:], N1b)
                        N2T, N2Tb = mmev(N1[:], N1b, N1T, NBb)
                        N4, N4b = mmev(N2T[:], N2Tb, N2[:], N2b)
                        N4T, N4Tb = mmev(N2[:], N2b, N2T[:], N2Tb)
                        Z0, Z0b = mmev(N4[:], N4b, N4T[:], N4Tb, ident[:], kb)
                        R1_, R1b = mmev(N4[:], N4b, Z0[:], Z0b, Z0[:], Z0b)
                        R2_, R2b = mmev(N2[:], N2b, R1_[:], R1b, R1_[:], R1b)
                        R3_, R3b = mmev(N1[:], N1b, R2_[:], R2b, R2_[:], R2b)
                        P["R3"] = (R3_, R3b)
                        Vr, Vrb = RS.r64.nxt()
                        E.op("act", lambda: nc.scalar.copy(out=Vr[:], in_=cur["v"][0][:, sl, :]), r=[cur["v"][1]], w=[Vrb])
                        P["v"] = (Vr[:], Vrb)
                        P["cg"] = cg
                        P["sl"] = sl
                        P["g"] = g
                        return P

                    def chain(sm_, ci, P, E, RS):
                        ST, STb = sm_["ST"]
                        PT, PTb = P["PT"]
                        vts, vb = P["v"]
                        NB, NBb = P["NB"]
                        NK_, NKb = P["NK"]
                        AQ, AQb = P["AQ"]
                        SELP, SELPb = KC["selp"]
                        selbc = SELP[:, :].unsqueeze(2).to_broadcast([128, 4, 64])
                        D0, D0b = RS.r256.nxt()
                        D0f, D0fb = RS.w256.nxt()
                        E.op("dve", lambda: nc.vector.tensor_tensor(
                            out=D0[:].rearrange("q (p v) -> q p v", p=4), in0=ST[:], in1=PT[:, :].unsqueeze(2).to_broadcast([128, 4, 64]),
                            op=ALU.mult), r=[STb, PTb], w=[D0b])
                        E.op("pool", lambda: nc.gpsimd.tensor_tensor(
                            out=D0f[:].rearrange("q (p v) -> q p v", p=4), in0=ST[:], in1=PT[:, :].unsqueeze(2).to_broadcast([128, 4, 64]),
                            op=ALU.mult), r=[STb, PTb], w=[D0fb])
                        pw, pwb = RS.accA
                        E.op("pe", lambda: nc.tensor.matmul(pw[:, 0:64], lhsT=NK_[:, 0:128], rhs=vts, start=True, stop=True),
                             r=[NKb, vb], w=[pwb])
                        E.op("pe", lambda: nc.tensor.matmul(pw[:, 256:512], lhsT=AQ[:, 0:128], rhs=D0[:], start=True, stop=True),
                             r=[AQb, D0b], w=[pwb])
                        TA, TAb = RS.w256.nxt()
                        E.op("dve", lambda: nc.vector.tensor_tensor(out=TA[:].rearrange("q (p v) -> q p v", p=4),
                                                                    in0=pw[:, 256:512].rearrange("q (p v) -> q p v", p=4), in1=selbc, op=ALU.mult),
                             r=[pwb, SELPb], w=[TAb])
                        RA_, RAb_ = RS.w64.nxt()
                        E.op("dve", lambda: nc.vector.tensor_reduce(out=RA_[:], in_=TA[:].rearrange("q (p v) -> q v p", p=4), axis=AX.X, op=ALU.add),
                             r=[TAb], w=[RAb_])
                        W0, W0b = RS.r64.nxt()
                        E.op("dve", lambda: nc.vector.tensor_tensor(out=W0[:], in0=pw[:, 0:64], in1=RA_[:], op=ALU.add), r=[pwb, RAb_], w=[W0b])
                        pu, pub = RS.pu
                        R3_, R3b = P["R3"]
                        E.op("pe", lambda: nc.tensor.matmul(pu[:, 0:64], lhsT=R3_[:], rhs=W0[:], start=True, stop=True), r=[R3b, W0b], w=[pub])
                        U, Ub = RS.r64.nxt()
                        E.op("act", lambda: nc.scalar.copy(out=U[:], in_=pu[:, 0:64]), r=[pub], w=[Ub])
                        pst, pstb = RS.accB
                        be, beb = P["be"]
                        ke, keb = P["ke"]
                        for p in range(4):
                            E.op("pe", lambda p=p: nc.tensor.matmul(pst[:, p * 64:(p + 1) * 64], lhsT=be[:, p * 128:(p + 1) * 128], rhs=U[:],
                                                                    start=True, stop=False), r=[beb, Ub], w=[pstb])
                            E.op("pe", lambda p=p: nc.tensor.matmul(pst[:, p * 64:(p + 1) * 64], lhsT=ke[:, p * 128:(p + 1) * 128], rhs=vts,
                                                                    start=False, stop=True), r=[keb, vb], w=[pstb])
                        E.op("dve", lambda: nc.vector.tensor_tensor(out=ST[:].rearrange("q p v -> q (p v)"), in0=pst[:, 0:256], in1=D0f[:],
                                                                    op=ALU.add), r=[pstb, D0fb], w=[STb])
                        py, pyb = RS.accA
                        E.op("pe", lambda: nc.tensor.matmul(py[:, 0:64], lhsT=NB[:, 128:256], rhs=U[:], start=True, stop=False),
                             r=[NBb, Ub], w=[pyb])
                        E.op("pe", lambda: nc.tensor.matmul(py[:, 0:64], lhsT=NK_[:, 128:256], rhs=vts, start=False, stop=True),
                             r=[NKb, vb], w=[pyb])
                        E.op("pe", lambda: nc.tensor.matmul(py[:, 256:512], lhsT=AQ[:, 128:256], rhs=D0[:], start=True, stop=True),
                             r=[AQb, D0b], w=[pyb])
                        TQ, TQb = RS.w256.nxt()
                        E.op("dve", lambda: nc.vector.tensor_tensor(out=TQ[:].rearrange("q (p v) -> q p v", p=4),
                                                                    in0=py[:, 256:512].rearrange("q (p v) -> q p v", p=4), in1=selbc, op=ALU.mult),
                             r=[pyb, SELPb], w=[TQb])
                        RQ_, RQb_ = RS.w64.nxt()
                        E.op("dve", lambda: nc.vector.tensor_reduce(out=RQ_[:], in_=TQ[:].rearrange("q (p v) -> q v p", p=4), axis=AX.X, op=ALU.add),
                             r=[TQb], w=[RQb_])
                        Yg, Ygb = sm_["ycur"]
                        cg = P["cg"]
                        sl = P["sl"]
                        E.op("dve", lambda: nc.vector.tensor_tensor(out=Yg[:, sl, :], in0=py[:, 0:64], in1=RQ_[:], op=ALU.add),
                             r=[pyb, RQb_], w=[Ygb])
                        if cg == G - 1:
                            arr = "y%d" % sm_["d"]
                            E.dma("pool", gap(sm_, arr, P["g"]), Yg[:], r=[Ygb], w=[RA[arr][1]])

                    class TL:
                        def __init__(self):
                            self.items = []

                        def op(self, e, fn, r=(), w=()):
                            self.items.append((0, e, fn, r, w))

                        def dma(self, q, out, in_, r=(), w=()):
                            self.items.append((1, q, (out, in_), r, w))

                    def run_lists(lists):
                        idx = [0] * len(lists)
                        nmax = max([len(L_.items) for L_ in lists] + [1])
                        for step in range(1, nmax + 1):
                            for k, L_ in enumerate(lists):
                                tgt = (len(L_.items) * step + nmax - 1) // nmax
                                while idx[k] < tgt:
                                    kind, e, fn, r, w = L_.items[idx[k]]
                                    idx[k] += 1
                                    if kind == 0:
                                        c.op(e, fn, r=r, w=w)
                                    else:
                                        c.dma(e, fn[0], fn[1], r=r, w=w)

                    look = 1 if len(streams) <= 2 else 0
                    Ps = {}

                    def mk_prep(ci):
                        out = []
                        for k, sm_ in enumerate(streams):
                            E = TL()
                            if look:
                                RS = PREP_RS[k * 2 + (ci % 2)]
                                RS.ph = Rot(BANKS[4 + 2 * k:6 + 2 * k])
                            else:
                                RS = PREP_RS[k]
                                RS.ph = Rot(BANKS[2 * k:2 * k + 2])
                            Ps[(k, ci)] = prep(sm_, ci, E, RS)
                            out.append(E)
                        return out

                    def mk_chain(ci, ks):
                        out = []
                        for k in ks:
                            E = TL()
                            chain(streams[k], ci, Ps.pop((k, ci)), E, CHAIN_RS[k % 2])
                            out.append(E)
                        return out
                    if look:
                        run_lists(mk_prep(0))
                        for ci in range(nchunk):
                            ls = mk_chain(ci, range(len(streams)))
                            if ci + 1 < nchunk:
                                ls = ls + mk_prep(ci + 1)
                            run_lists(ls)
                            if castq and ci % 2 == 0:
                                castq.pop(0)()
                        while castq:
                            castq.pop(0)()
                    else:
                        for ci in range(nchunk):
                            run_lists(mk_prep(ci))
                            for k0 in range(0, len(streams), 2):
                                run_lists(mk_chain(ci, range(k0, min(k0 + 2, len(streams)))))
                    for sm_ in streams:
                        if sm_["is_s"] or os.environ.get("DBG_NOFIN"):
                            continue
                        ST, STb = sm_["ST"]
                        d = sm_["d"]
                        Sn, Snb = snat.nxt()
                        for p in range(4):
                            pt, pb = psr.nxt()
                            c.op("pe", lambda p=p, pt=pt, ST=ST: nc.tensor.transpose(out=pt[0:64, 0:128], in_=ST[:, p, :], identity=ident[:]),
                                 r=[STb, kb], w=[pb])
                            c.op("dve", lambda p=p, pt=pt, Sn=Sn: nc.vector.tensor_copy(out=Sn[:, p, :], in_=pt[0:64, 0:128]), r=[pb], w=[Snb])
                        pi = sm_["pi"]
                        dst = O["ns"][pi, l, d].rearrange("h v k -> v h k")
                        c.dma("pool", dst, Sn[:].rearrange("v p x -> v (p x)").rearrange("v (h k) -> v h k", h=8), r=[Snb], w=[out_b])
                    c.barrier()
                    sst.close()

                import os
                pstreams = [dict(s0=s0, L=L, d=d, pi=pi, is_s=False) for (s0, L, is_s, pi) in SEQS[0:2] for d in range(2)]
                if not os.environ.get("DBG_SKIP_PROMPT"):
                    run_streams(pstreams[0:int(os.environ.get("DBG_NSTREAM", "4"))], LP // 16)
                s0, L, _, _ = SEQS[2]
                if not os.environ.get("DBG_SKIP_SAMPLE"):
                    run_streams([dict(s0=s0, L=L, d=d, pi=0, is_s=True) for d in range(2)], LS // 16)
                c.barrier()
            c.mute = m0 or ("r3" not in parts)
            with ExitStack() as st:
                gng, gngb = c.sb(st, "rgng", (128, 512))
                gnb, gnbb = c.sb(st, "rgnb", (128, 512))
                c.dma("sp", gng[:], I["rwkv_gn_g"][l].partition_broadcast(128), w=[gngb])
                c.dma("sp", gnb[:], I["rwkv_gn_b"][l].partition_broadcast(128), w=[gnbb])
                epsg, epsgb = c.sb(st, "repsg", (128, 1))
                c.op("dve", lambda: nc.vector.memset(epsg[:], 64e-5), w=[epsgb])
                yi = Rot([c.sb(st, "ryi", (128, 4, 128)) for _ in range(4)])
                wk = Rot([c.sb(st, "rwk3", (128, 512)) for _ in range(12)])
                sm = Rot([c.sb(st, "rsm3", (128, 8)) for _ in range(4)])
                mo = Rot([c.sb(st, "rmo", (128, 4, 128), BF16) for _ in range(2)])
                for g0 in range(0, NTOK, 128):
                    py, pyb = wk.nxt()
                    Y1, Y1b = wk.nxt()
                    c.dma("sp", py[:], RA["y0"][0][g0:g0 + 128, :], r=[RA["y0"][1]], w=[pyb])
                    c.dma("sp", Y1[:], RA["y1"][0][g0:g0 + 128, :], r=[RA["y1"][1]], w=[Y1b])
                    c.op("pool", lambda: nc.gpsimd.tensor_tensor(out=py[:], in0=py[:], in1=Y1[:], op=ALU.add), r=[pyb, Y1b], w=[pyb])
                    Yc, Ycb = wk.nxt()
                    MS, MSb = sm.nxt()
                    c.op("dve", lambda: nc.vector.tensor_reduce(out=MS[:], in_=py[:].rearrange("p (h k) -> p h k", h=8), axis=AX.X, op=ALU.add),
                         r=[pyb], w=[MSb])
                    c.op("dve", lambda: nc.vector.tensor_scalar(out=MS[:], in0=MS[:], scalar1=-1.0 / 64, scalar2=None, op0=ALU.mult),
                         r=[MSb], w=[MSb])
                    c.op("dve", lambda: nc.vector.tensor_tensor(
                        out=Yc[:].rearrange("p (h k) -> p h k", h=8), in0=py[:].rearrange("p (h k) -> p h k", h=8),
                        in1=MS[:, :].unsqueeze(2).to_broadcast([128, 8, 64]), op=ALU.add), r=[pyb, MSb], w=[Ycb])
                    SQ, SQb = wk.nxt()
                    c.op("pool", lambda: nc.gpsimd.tensor_tensor(out=SQ[:], in0=Yc[:], in1=Yc[:], op=ALU.mult), r=[Ycb], w=[SQb])
                    VS, VSb = sm.nxt()
                    c.op("dve", lambda: nc.vector.tensor_reduce(out=VS[:], in_=SQ[:].rearrange("p (h k) -> p h k", h=8), axis=AX.X, op=ALU.add),
                         r=[SQb], w=[VSb])
                    c.op("act", lambda: nc.scalar.activation(out=VS[:], in_=VS[:], func=AF.Sqrt, scale=1.0 / 64, bias=epsg[:, 0:1]),
                         r=[VSb, epsgb], w=[VSb])
                    c.op("dve", lambda: nc.vector.reciprocal(out=VS[:], in_=VS[:]), r=[VSb], w=[VSb])
                    c.op("dve", lambda: nc.vector.tensor_tensor(
                        out=Yc[:].rearrange("p (h k) -> p h k", h=8), in0=Yc[:].rearrange("p (h k) -> p h k", h=8),
                        in1=VS[:, :].unsqueeze(2).to_broadcast([128, 8, 64]), op=ALU.mult), r=[Ycb, VSb], w=[Ycb])
                    c.op("pool", lambda: nc.gpsimd.tensor_tensor(out=Yc[:], in0=Yc[:], in1=gng[:], op=ALU.mult), r=[Ycb, gngb], w=[Ycb])
                    c.op("pool", lambda: nc.gpsimd.tensor_tensor(out=Yc[:], in0=Yc[:], in1=gnb[:], op=ALU.add), r=[Ycb, gnbb], w=[Ycb])
                    BO, BOb = wk.nxt()
                    Gt, Gtb = wk.nxt()
                    c.dma("sp", BO[:], RA["bon"][0][g0:g0 + 128, :], r=[RA["bon"][1]], w=[BOb])
                    c.dma("sp", Gt[:], RA["g"][0][g0:g0 + 128, :], r=[RA["g"][1]], w=[Gtb])
                    c.op("dve", lambda: nc.vector.tensor_tensor(out=Yc[:], in0=Yc[:], in1=BO[:], op=ALU.add), r=[Ycb, BOb], w=[Ycb])
                    c.op("dve", lambda: nc.vector.tensor_tensor(out=Yc[:], in0=Yc[:], in1=Gt[:], op=ALU.mult), r=[Ycb, Gtb], w=[Ycb])
                    po, pob = psr.nxt()
                    for cc in range(4):
                        c.op("pe", lambda cc=cc: nc.tensor.transpose(out=po[:, cc * 128:(cc + 1) * 128], in_=Yc[:, cc * 128:(cc + 1) * 128],
                                                                     identity=ident[:]), r=[Ycb, kb], w=[pob])
                    Mo, Mob = mo.nxt()
                    c.op("act", lambda: nc.scalar.copy(out=Mo[:].rearrange("p a b -> p (a b)"), in_=po[:]), r=[pob], w=[Mob])
                    c.dma("pool", mixTv[:, 0:4, g0:g0 + 128], Mo[:], r=[Mob], w=[mixT_b])
                c.barrier()
            c.mute = m0

        if dbg == "r2":
            c.mute = False
            stage_rwkv(0, parts=("r2",))
            c.mute = True
        for l in range(DEPTH):
            stage_A(l)
            if l == 1:
                cast_weights(1, ["w_out", "ffn_up", "ffn_down"])
            stage_attn(l)
            stage_hyena(l)
            stage_rwkv(l)
            stage_C1(l)
            stage_C2(l)
        c.mute = False
        c.barrier()
    return nc, hc


_CACHE = {}


def kernel(**inputs):
    if "nc" not in _CACHE:
        _CACHE["nc"] = build()
    nc, hc = _CACHE["nc"]
    f = lambda a: np.ascontiguousarray(np.asarray(a, dtype=np.float32))
    xp = f(inputs["x_prompt"]); xs = f(inputs["x_sample"])
    ck = f(inputs["cache_k"]); cv = f(inputs["cache_v"]); stt = f(inputs["state_rwkv"])
    cc = f(inputs["c"]); cctx = f(inputs["c_ctx"])
    shared = {n: f(inputs[n]) for n in W_NAMES}
    for k, v in hc.items():
        shared["k_" + k] = v
    in_maps = []
    for i in range(8):
        m = dict(shared)
        m["xp"] = xp[2 * i:2 * i + 2].reshape(512, D)
        m["xs"] = xs[i]
        m["ck"] = ck[i].reshape(DEPTH, 256, 256)
        m["cv"] = cv[i].reshape(DEPTH, 256, 256)
        m["st"] = stt[i]
        m["cond"] = np.stack([cctx, cc[i]], 0)
        in_maps.append(m)
    res = run_bass_kernel_spmd(nc, in_maps, core_ids=list(range(8)))
    R = res.results
    yp = np.concatenate([R[i]["yp"].reshape(2, LP, D) for i in range(8)], 0)
    ys = np.stack([R[i]["ys"] for i in range(8)], 0)
    nk = np.concatenate([R[i]["nk"].reshape(2, DEPTH, 256, 4, 64) for i in range(8)], 0)
    nv = np.concatenate([R[i]["nv"].reshape(2, DEPTH, 256, 4, 64) for i in range(8)], 0)
    ns = np.concatenate([R[i]["ns"] for i in range(8)], 0)
    return (yp.astype(np.float32), ys.astype(np.float32), nk.astype(np.float32), nv.astype(np.float32), ns.astype(np.float32))
```

```python
import math
import os
from contextlib import ExitStack
import numpy as np
import ml_dtypes
import concourse.bass as bass
import concourse.mybir as mybir
from concourse.bass_utils import run_bass_kernel_spmd

F32 = mybir.dt.float32
F32R = mybir.dt.float32r
BF16 = mybir.dt.bfloat16
AF = mybir.ActivationFunctionType
ALU = mybir.AluOpType
AX = mybir.AxisListType

D = 2048
DEPTH = 2
NTOK = 2560
LP = 256
LS = 2048
INC = 4992
DFF = 5632
ALPHA = (2 * DEPTH) ** 0.25
MAGIC = 12582912.0
TWO_PI = 2.0 * math.pi


class Buf:
    __slots__ = ("w", "r", "name", "sb", "dsem")

    def __init__(self, name="", sb=False):
        self.w = {}
        self.r = {}
        self.name = name
        self.sb = sb
        self.dsem = None


NDSEM = 96


class Ctx:
    def __init__(self):
        self.nc = bass.Bass("TRN2", target_bir_lowering=False)
        nc = self.nc
        self.es = ExitStack()
        self.eng = {"pe": nc.tensor, "dve": nc.vector, "act": nc.scalar, "pool": nc.gpsimd, "sp": nc.sync}
        self.sem = {}
        self.cnt = {}
        self.seen = {e: {} for e in self.eng}
        for p in ["pe", "dve", "act", "pool"]:
            self.sem[p] = self.es.enter_context(nc.semaphore("s_" + p))
            self.cnt[p] = 0
        self.dfree = []
        for i in range(NDSEM):
            k = ("d", i)
            self.sem[k] = self.es.enter_context(nc.semaphore("s_d%d" % i))
            self.cnt[k] = 0
            self.dfree.append(k)
        self.nuid = 0

    def uid(self, s):
        self.nuid += 1
        return "%s_%d" % (s, self.nuid)

    def release(self, b):
        if b.dsem is not None:
            self.dfree.append(b.dsem)
            b.dsem = None

    def _waits(self, e, r, w):
        need = {}
        for b in r:
            for p, v in b.w.items():
                if need.get(p, 0) < v:
                    need[p] = v
        for b in w:
            for p, v in b.w.items():
                if need.get(p, 0) < v:
                    need[p] = v
            for p, v in b.r.items():
                if need.get(p, 0) < v:
                    need[p] = v
        seen = self.seen[e]
        eng = self.eng[e]
        for p, v in need.items():
            if p == "pe" and e == "pe":
                continue
            if seen.get(p, 0) >= v:
                continue
            eng.wait_ge(self.sem[p], v)
            seen[p] = v

    mute = False

    def op(self, e, fn, r=(), w=()):
        if self.mute:
            return None
        self._waits(e, r, w)
        inst = fn()
        self.cnt[e] += 1
        c = self.cnt[e]
        inst.then_inc(self.sem[e], 1)
        for b in r:
            b.r[e] = c
        for b in w:
            b.w[e] = c
        return inst

    def dma(self, q, out, in_, r=(), w=(), **kw):
        if self.mute:
            return None
        key = None
        for b in w:
            if b.sb:
                key = b
        if key is None:
            for b in r:
                if b.sb:
                    key = b
        if key is None:
            key = w[0]
        if key.dsem is None:
            key.dsem = self.dfree.pop()
        p = key.dsem
        self._waits(q, r, w)
        kw.setdefault('allow_slow_non_contiguous', True)
        inst = self.eng[q].dma_start(out=out, in_=in_, **kw)
        self.cnt[p] += 16
        c = self.cnt[p]
        inst.then_inc(self.sem[p], 16)
        for b in r:
            b.r[p] = c
        for b in w:
            b.w[p] = c
        return inst

    def barrier(self):
        if self.mute:
            return
        for e in ["pe", "dve", "act", "pool", "sp"]:
            seen = self.seen[e]
            for p, v in self.cnt.items():
                if v > 0 and seen.get(p, 0) < v:
                    self.eng[e].wait_ge(self.sem[p], v)
                    seen[p] = v

    def sb(self, st, name, shape, dt=F32):
        t = st.enter_context(self.nc.sbuf_tensor(self.uid(name), list(shape), dt))
        b = Buf(name, sb=True)
        st.callback(self.release, b)
        return t, b

    def ps(self, st, name, shape, dt=F32):
        t = st.enter_context(self.nc.psum_tensor(self.uid(name), list(shape), dt))
        return t, Buf(name)

    def dram(self, name, shape, dt=F32, kind="Internal"):
        return self.nc.dram_tensor(name, list(shape), dt, kind=kind).ap(), Buf(name)


def host_consts():
    c = {}
    c["ident"] = np.eye(128, dtype=np.float32)
    c["onesm"] = np.full((128, 128), 1.0 / D, np.float32)
    b = np.zeros((128, 128), np.float32)
    b[:64, :64] = 1.0 / 64
    b[64:, 64:] = 1.0 / 64
    c["blk64"] = b
    rot = np.zeros((128, 128), np.float32)
    for m in range(128):
        if (m % 64) < 32:
            rot[m + 32, m] = -1.0
        else:
            rot[m - 32, m] = 1.0
    c["rot"] = rot
    rows = LS // 64
    row = np.repeat(np.arange(rows, dtype=np.float32), 64)
    col = np.tile(np.arange(64, dtype=np.float32), rows)
    inv = (10000.0 ** (-np.arange(16, dtype=np.float32) / 16)).astype(np.float32)
    ang = np.concatenate([row[:, None] * inv, col[:, None] * inv], -1).astype(np.float32)
    c["cos2"] = np.ascontiguousarray(np.tile(np.cos(ang).T, (4, 1)).astype(np.float32))
    c["sin2"] = np.ascontiguousarray(np.tile(np.sin(ang).T, (4, 1)).astype(np.float32))
    rho = np.arange(128)
    hh, jj = rho % 8, rho // 8
    same = hh[:, None] == hh[None, :]
    c["tri"] = (same & (jj[:, None] <= jj[None, :])).astype(np.float32)
    c["blk"] = same.astype(np.float32)
    c["selp"] = (hh[:, None] // 2 == np.arange(4)[None, :]).astype(np.float32)
    mbd = np.zeros((128, 2, 64), np.float32)
    mbd[rho, hh % 2, :] = 1.0
    c["mbd"] = mbd.reshape(128, 128)
    mh = np.zeros((128, 8, 64), np.float32)
    mh[rho, hh, :] = 1.0
    c["mh"] = mh.reshape(128, 512)
    cm = np.zeros((128, 4, 128), np.float32)
    cm[:, hh // 2, rho] = 1.0
    c["cm"] = cm.reshape(128, 512)
    ms = (same & (jj[:, None] < jj[None, :])).astype(np.float32)
    mi = (same & (jj[:, None] <= jj[None, :])).astype(np.float32)
    c["msi"] = np.concatenate([ms, mi], 1)
    c["msl"] = np.ascontiguousarray(ms.T)
    c["trib"] = (same & (jj[:, None] >= jj[None, :])).astype(np.float32)
    msb = (same & (jj[:, None] > jj[None, :])).astype(np.float32)
    mib = (same & (jj[:, None] >= jj[None, :])).astype(np.float32)
    c["msib"] = np.concatenate([msb, mib], 1)
    c["mslb"] = np.ascontiguousarray(msb.T)
    for n, tag in ((LP, "p"), (LS, "s")):
        t01 = np.linspace(0.0, 1.0, n, dtype=np.float32)[:, None]
        pos = np.arange(n, dtype=np.float32)[:, None]
        bands = np.linspace(1e-4, 15, 16, dtype=np.float32)[None, :]
        angz = (np.float32(2.0 * math.pi / n) * pos * bands).astype(np.float32)
        z = np.concatenate([t01, np.cos(angz), -np.sin(angz)], -1).astype(np.float32)
        c["zT" + tag] = np.ascontiguousarray(z.T)
        c["nt01" + tag] = np.ascontiguousarray((-t01[:, 0]).reshape(n // 128, 128).T)
        tt = np.arange(n, dtype=np.float64)[:, None]
        ff = np.arange(n, dtype=np.float64)[None, :]
        th = 2.0 * math.pi * tt * ff / (2 * n)
        Cf = np.cos(th)
        Sf = -np.sin(th)
        Sf[:, 0] = (-1.0) ** np.arange(n)
        SfT = (-np.sin(th)).T.copy()
        SfT[0, :] = (-1.0) ** np.arange(n)
        c["Cf" + tag] = Cf.astype(ml_dtypes.bfloat16)
        c["Sf" + tag] = Sf.astype(ml_dtypes.bfloat16)
        c["SfT" + tag] = SfT.astype(ml_dtypes.bfloat16)
        wf = np.full((n,), 1.0 / n, np.float32)
        wf[0] = 1.0 / (2 * n)
        c["wf" + tag] = np.ascontiguousarray(wf.reshape(n // 128, 128).T)
    return c


W_NAMES = ["w_mod", "b_mod", "w_in", "rwkv_shift", "rwkv_w0", "rwkv_w2", "rwkv_a0", "rwkv_a2",
           "rwkv_kk", "rwkv_ka", "rwkv_rk", "rwkv_g2", "rwkv_gn_g", "rwkv_gn_b",
           "hy_short", "hy_w1", "hy_b1", "hy_freq", "hy_w2", "hy_b2", "hy_w3", "hy_decay", "hy_bias",
           "attn_qn", "attn_kn", "w_out", "ln1_g", "ln1_b", "ln2_g", "ln2_b",
           "ffn_up", "ffn_conv", "ffn_down"]

W_SHAPES = {
    "w_mod": (2, 2048, 12288), "b_mod": (2, 12288), "w_in": (2, 2048, 4992), "rwkv_shift": (2, 3, 1920),
    "rwkv_w0": (2, 2, 512), "rwkv_w2": (2, 2, 64, 512), "rwkv_a0": (2, 2, 512), "rwkv_a2": (2, 2, 64, 512),
    "rwkv_kk": (2, 512), "rwkv_ka": (2, 512), "rwkv_rk": (2, 512), "rwkv_g2": (2, 128, 512),
    "rwkv_gn_g": (2, 512), "rwkv_gn_b": (2, 512), "hy_short": (2, 3, 1536), "hy_w1": (2, 33, 64),
    "hy_b1": (2, 64), "hy_freq": (2, 2, 64), "hy_w2": (2, 64, 64), "hy_b2": (2, 64), "hy_w3": (2, 64, 1024),
    "hy_decay": (2, 2, 512), "hy_bias": (2, 512), "attn_qn": (2, 64), "attn_kn": (2, 64),
    "w_out": (2, 2048, 2048), "ln1_g": (2, 2048), "ln1_b": (2, 2048), "ln2_g": (2, 2048), "ln2_b": (2, 2048),
    "ffn_up": (2, 2048, 11264), "ffn_conv": (2, 3, 11264), "ffn_down": (2, 5632, 2048),
}


def is_T(ci):
    return ci < 12 or 19 <= ci < 27 or ci >= 37


class Rot:
    def __init__(self, items):
        self.items = items
        self.i = 0

    def nxt(self):
        it = self.items[self.i % len(self.items)]
        self.i += 1
        return it


def build(dbg=None):
    c = Ctx()
    nc = c.nc
    I = {}

    def ext_in(name, shape, dt=F32):
        I[name] = nc.dram_tensor(name, list(shape), dt, kind="ExternalInput").ap()

    ext_in("xp", (512, D))
    ext_in("xs", (LS, D))
    ext_in("ck", (DEPTH, 256, 256))
    ext_in("cv", (DEPTH, 256, 256))
    ext_in("st", (DEPTH, 2, 8, 64, 64))
    ext_in("cond", (2, D))
    for n in W_NAMES:
        ext_in(n, W_SHAPES[n])
    hc = host_consts()
    for k, v in hc.items():
        ext_in("k_" + k, v.shape, BF16 if v.dtype == ml_dtypes.bfloat16 else F32)
    O = {}
    for name, shape in (("yp", (512, D)), ("ys", (LS, D)), ("nk", (2, DEPTH, 256, 256)),
                        ("nv", (2, DEPTH, 256, 256)), ("ns", (2, DEPTH, 2, 8, 64, 64))):
        O[name] = nc.dram_tensor(name, list(shape), F32, kind="ExternalOutput").ap()
    out_b = Buf("outs")

    es = c.es
    with es:
        PS = [c.ps(es, "bank%d" % i, (128, 512)) for i in range(8)]
        psr = Rot(PS)
        ident, _ = c.sb(es, "ident", (128, 128))
        onesm, _ = c.sb(es, "onesm", (128, 128))
        blk64, _ = c.sb(es, "blk64", (128, 128))
        rotm, _ = c.sb(es, "rotm", (128, 128))
        kb = Buf("consts")
        for t, nm in ((ident, "ident"), (onesm, "onesm"), (blk64, "blk64"), (rotm, "rot")):
            c.dma("sp", t[:], I["k_" + nm][:, :], w=[kb])
        modT = [c.sb(es, "modT%d" % l, (128, 96, 2)) for l in range(DEPTH)]
        if dbg:
            c.mute = True

        xT, _ = c.dram("xT", (D, NTOK))
        xTv = xT.rearrange("(kc p) t -> p kc t", p=128)
        xT_b = [Buf("xT%d" % i) for i in range(5)]
        projT, projT_b = c.dram("projT", (NTOK, INC))
        projF, projF_b = c.dram("projF", (INC, NTOK))
        mixT, mixT_b = c.dram("mixT", (D, NTOK), BF16)
        mixTv = mixT.rearrange("(kc p) t -> p kc t", p=128)
        h2T, h2T_b = c.dram("h2T", (D, NTOK), BF16)
        h2Tv = h2T.rearrange("(kc p) t -> p kc t", p=128)

        wb = {}
        for l in range(DEPTH):
            for nm, rows, cols in (("w_in", D, INC), ("w_out", D, D), ("ffn_up", D, 2 * DFF), ("ffn_down", DFF, D)):
                wb[(nm, l)] = c.dram("wb_%s_%d" % (nm, l), (rows, cols), BF16)

        castq = []

        def cast_weights(l, names, now=False):
            for nm in names:
                ap, b = wb[(nm, l)]
                src = I[nm][l]
                for r0 in range(0, ap.shape[0], 256):
                    th = (lambda ap=ap, src=src, r0=r0, b=b: c.dma("pool", ap[r0:r0 + 256, :], src[r0:r0 + 256, :], w=[b]))
                    if now:
                        th()
                    else:
                        castq.append(th)
        cast_weights(0, ["w_in"], now=True)
        if dbg is None:
            cast_weights(0, ["w_out", "ffn_up", "ffn_down"])
            cast_weights(1, ["w_in"])

        def load_cols(st, name, src2d, nrow):
            tmp, tb = c.sb(st, name + "_r", (nrow, 128))
            if isinstance(src2d, list):
                r0 = 0
                for sp_ in src2d:
                    nr = sp_.shape[0]
                    c.dma("sp", tmp[r0:r0 + nr, :], sp_, w=[tb])
                    r0 += nr
            else:
                c.dma("sp", tmp[:], src2d, w=[tb])
            dst, db = c.sb(st, name, (128, nrow))
            pt, pb = psr.nxt()
            c.op("pe", lambda: nc.tensor.transpose(out=pt[:, 0:nrow], in_=tmp[:], identity=ident[0:nrow, 0:nrow]),
                 r=[tb, kb], w=[pb])
            c.op("dve", lambda: nc.vector.tensor_copy(out=dst[:], in_=pt[:, 0:nrow]), r=[pb], w=[db])
            return dst, db

        with ExitStack() as st:
            cin, cin_b = c.sb(st, "cin", (2, D))
            c.dma("sp", cin[:], I["cond"][:, :], w=[cin_b])
            c.op("act", lambda: nc.scalar.activation(out=cin[:], in_=cin[:], func=AF.Silu), r=[cin_b], w=[cin_b])
            silT, silT_b = c.sb(st, "silT", (128, 16, 2), F32R)
            pt, pb = psr.nxt()
            for kc in range(16):
                c.op("pe", lambda kc=kc: nc.tensor.transpose(out=pt[:, kc * 2:kc * 2 + 2], in_=cin[:, kc * 128:(kc + 1) * 128],
                                                           identity=ident[0:2, 0:2]), r=[cin_b, kb], w=[pb])
            c.op("dve", lambda: nc.vector.tensor_copy(out=silT[:].rearrange("p a b -> p (a b)"), in_=pt[:, 0:32]),
                 r=[pb], w=[silT_b])
            wm = [c.sb(st, "wm", (128, 16, 512)) for _ in range(2)]
            wr = [c.sb(st, "wr", (128, 16, 512), F32R) for _ in range(2)]
            mrows = Rot([c.sb(st, "mrow", (2, 512)) for _ in range(2)])
            brows = Rot([c.sb(st, "brow", (2, 512)) for _ in range(2)])
            modD, modDb = c.dram("modD", (DEPTH, 2, 6 * D))
            nblk = 0
            for l in range(DEPTH):
                wv = I["w_mod"][l].rearrange("(kc p) n -> p kc n", p=128)
                for cb in range(24):
                    W, Wb_ = wm[nblk % 2]
                    Wr, Wrb = wr[nblk % 2]
                    nblk += 1
                    c.dma("sp", W[:], wv[:, :, cb * 512:(cb + 1) * 512], w=[Wb_])
                    c.op("dve", lambda W=W, Wr=Wr: nc.vector.tensor_copy(out=Wr[:, 0:8, :], in_=W[:, 0:8, :]), r=[Wb_], w=[Wrb])
                    c.op("act", lambda W=W, Wr=Wr: nc.scalar.copy(out=Wr[:, 8:16, :], in_=W[:, 8:16, :]), r=[Wb_], w=[Wrb])
                    pm, pmb = psr.nxt()
                    for kc in range(16):
                        c.op("pe", lambda kc=kc, Wr=Wr, pm=pm: nc.tensor.matmul(
                            pm[0:2, :], lhsT=silT[:, kc, :], rhs=Wr[:, kc, :], start=(kc == 0), stop=(kc == 15)),
                            r=[Wrb, silT_b], w=[pmb])
                    brow, browb = brows.nxt()
                    mrow, mrowb = mrows.nxt()
                    c.dma("sp", brow[:], I["b_mod"][l, cb * 512:(cb + 1) * 512].partition_broadcast(2), w=[browb])
                    c.op("dve", lambda pm=pm, mrow=mrow, brow=brow: nc.vector.tensor_tensor(
                        out=mrow[:], in0=pm[0:2, :], in1=brow[:], op=ALU.add), r=[pmb, browb], w=[mrowb])
                    c.dma("pool", modD[l][:, cb * 512:(cb + 1) * 512], mrow[:], r=[mrowb], w=[modDb])
                mt, mtb = modT[l]
                for gi_ in range(2):
                    c.dma("sp", mt[:, :, gi_], modD[l, gi_].rearrange("(a p) -> p a", p=128), r=[modDb], w=[mtb])
                for lo in (16, 64):
                    c.op("dve", lambda lo=lo, mt=mt: nc.vector.tensor_scalar(
                        out=mt[:, lo:lo + 16, :], in0=mt[:, lo:lo + 16, :], scalar1=1.0, scalar2=None, op0=ALU.add),
                        r=[mtb], w=[mtb])
                for lo in (32, 80):
                    c.op("dve", lambda lo=lo, mt=mt: nc.vector.tensor_scalar(
                        out=mt[:, lo:lo + 16, :], in0=mt[:, lo:lo + 16, :], scalar1=1.0 / ALPHA, scalar2=None, op0=ALU.mult),
                        r=[mtb], w=[mtb])
            c.barrier()

        with ExitStack() as st:
            xin = [c.sb(st, "xin", (128, D)) for _ in range(2)]
            xo = [c.sb(st, "xo", (128, 16, 128)) for _ in range(2)]
            for i in range(20):
                src = I["xp"][i * 128:(i + 1) * 128, :] if i < 4 else I["xs"][(i - 4) * 128:(i - 3) * 128, :]
                X, Xb = xin[i % 2]
                Y, Yb = xo[i % 2]
                c.dma("sp", X[:], src, w=[Xb])
                for g in range(4):
                    pt, pb = psr.nxt()
                    for j in range(4):
                        kc = g * 4 + j
                        c.op("pe", lambda kc=kc, j=j, X=X, pt=pt: nc.tensor.transpose(
                            out=pt[:, j * 128:(j + 1) * 128], in_=X[:, kc * 128:(kc + 1) * 128], identity=ident[:]),
                            r=[Xb, kb], w=[pb])
                    e = "dve" if g % 2 == 0 else "act"
                    dst = Y[:, g * 4:(g + 1) * 4, :].rearrange("p a b -> p (a b)")
                    if e == "dve":
                        c.op("dve", lambda dst=dst, pt=pt: nc.vector.tensor_copy(out=dst, in_=pt[:]), r=[pb], w=[Yb])
                    else:
                        c.op("act", lambda dst=dst, pt=pt: nc.scalar.copy(out=dst, in_=pt[:]), r=[pb], w=[Yb])
                c.dma("pool", xTv[:, :, i * 128:(i + 1) * 128], Y[:], r=[Yb], w=[xT_b[i // 4]])
            c.barrier()

        def evac(i, dst, src, rb, wb_):
            if i % 2 == 0:
                c.op("dve", lambda: nc.vector.tensor_copy(out=dst, in_=src), r=rb, w=wb_)
            else:
                c.op("act", lambda: nc.scalar.copy(out=dst, in_=src), r=rb, w=wb_)

        def stage_A(l):
            mt, mtb = modT[l]
            with ExitStack() as st:
                xt, xtb = c.sb(st, "xt", (128, 16, 512))
                hT, hTb = c.sb(st, "hT", (128, 16, 512), BF16)
                wt = [c.sb(st, "wA", (128, 16, 512), BF16) for _ in range(2)]
                og = Rot([c.sb(st, "oA", (128, 512)) for _ in range(10)])
                Wap, Wbuf = wb[("w_in", l)]
                Wv = Wap.rearrange("(kc p) n -> p kc n", p=128)
                nev = 0
                for tt in range(5):
                    gi = 0 if tt == 0 else 1
                    c.dma("sp", xt[:], xTv[:, :, tt * 512:(tt + 1) * 512], r=[xT_b[tt]], w=[xtb])
                    for kc in range(16):
                        sc = mt[:, 16 + kc, gi:gi + 1]
                        sh = mt[:, kc, gi:gi + 1]
                        if kc % 2 == 0:
                            c.op("dve", lambda kc=kc, sc=sc, sh=sh: nc.vector.tensor_scalar(
                                out=hT[:, kc, :], in0=xt[:, kc, :], scalar1=sc, scalar2=sh, op0=ALU.mult, op1=ALU.add),
                                r=[xtb, mtb], w=[hTb])
                        else:
                            c.op("act", lambda kc=kc, sc=sc, sh=sh: nc.scalar.activation(
                                out=hT[:, kc, :], in_=xt[:, kc, :], func=AF.Identity, scale=sc, bias=sh),
                                r=[xtb, mtb], w=[hTb])
                    for blk in range(10):
                        c0 = blk * 512
                        ncol = min(512, INC - c0)
                        W, Wb_ = wt[blk % 2]
                        c.dma("sp", W[:, :, 0:ncol], Wv[:, :, c0:c0 + ncol], r=[Wbuf], w=[Wb_])
                        nch = ncol // 128
                        j = 0
                        while j < nch:
                            ci = c0 // 128 + j
                            if is_T(ci):
                                j2 = j
                                while j2 < nch and is_T(c0 // 128 + j2):
                                    j2 += 1
                                wd = (j2 - j) * 128
                                for ts in range(4):
                                    pt, pb = psr.nxt()
                                    for kc in range(16):
                                        c.op("pe", lambda kc=kc, ts=ts, pt=pt, W=W, j=j, wd=wd: nc.tensor.matmul(
                                            pt[:, 0:wd], lhsT=hT[:, kc, ts * 128:(ts + 1) * 128],
                                            rhs=W[:, kc, j * 128:j * 128 + wd], start=(kc == 0), stop=(kc == 15)),
                                            r=[hTb, Wb_], w=[pb])
                                    o, ob = og.nxt()
                                    evac(nev, o[:, 0:wd], pt[:, 0:wd], [pb], [ob])
                                    nev += 1
                                    t0 = tt * 512 + ts * 128
                                    c.dma("pool", projT[t0:t0 + 128, c0 + j * 128:c0 + j * 128 + wd], o[:, 0:wd],
                                          r=[ob], w=[projT_b])
                                j = j2
                            else:
                                pt, pb = psr.nxt()
                                for kc in range(16):
                                    c.op("pe", lambda kc=kc, pt=pt, W=W, j=j: nc.tensor.matmul(
                                        pt[:], lhsT=W[:, kc, j * 128:(j + 1) * 128], rhs=hT[:, kc, :],
                                        start=(kc == 0), stop=(kc == 15)), r=[hTb, Wb_], w=[pb])
                                o, ob = og.nxt()
                                evac(nev, o[:], pt[:], [pb], [ob])
                                nev += 1
                                c.dma("pool", projF[ci * 128:(ci + 1) * 128, tt * 512:(tt + 1) * 512], o[:],
                                      r=[ob], w=[projF_b])
                                j += 1
                c.barrier()

        SEQS = [(0, LP, False, 0), (LP, LP, False, 1), (2 * LP, LS, True, 0)]

        def stage_attn(l):
            with ExitStack() as st:
                gq, gqb = c.sb(st, "gq", (128, 1))
                gk, gkb = c.sb(st, "gk", (128, 1))
                for h in range(2):
                    c.dma("sp", gq[h * 64:(h + 1) * 64, :], I["attn_qn"][l].rearrange("(a b) -> a b", b=1), w=[gqb])
                    c.dma("sp", gk[h * 64:(h + 1) * 64, :], I["attn_kn"][l].rearrange("(a b) -> a b", b=1), w=[gkb])
                cos2, cb2 = c.sb(st, "cos2", (128, LS))
                sin2, sb2 = c.sb(st, "sin2", (128, LS))
                c.dma("sp", cos2[:], I["k_cos2"][:, :], w=[cb2])
                c.dma("sp", sin2[:], I["k_sin2"][:, :], w=[sb2])
                QT, QTb = c.sb(st, "QT", (128, 8, LS), BF16)
                KT2, KT2b = c.sb(st, "KT2", (128, 4, LS + 256), BF16)
                VA, VAb = c.sb(st, "VA", (128, 18, 4, 128), BF16)
                VB, VBb = c.sb(st, "VB", (128, 18, 4, 128), BF16)
                c.op("pool", lambda: nc.gpsimd.memset(VA[:], 1.0), w=[VAb])
                c.op("pool", lambda: nc.gpsimd.memset(VB[:], 1.0), w=[VBb])
                xin = Rot([c.sb(st, "axin", (128, 512)) for _ in range(2)])
                sq = Rot([c.sb(st, "asq", (128, 512)) for _ in range(2)])
                rs = Rot([c.sb(st, "ars", (128, 512)) for _ in range(2)])
                qn = Rot([c.sb(st, "aqn", (128, 512)) for _ in range(2)])
                t1 = Rot([c.sb(st, "at1", (128, 512)) for _ in range(2)])
                kn = Rot([c.sb(st, "akn", (128, 512), BF16) for _ in range(2)])
                vin = Rot([c.sb(st, "avin", (128, 256)) for _ in range(2)])
                pexp = Rot([c.sb(st, "apexp", (128, 512), BF16) for _ in range(4)])
                rec = Rot([c.sb(st, "arec", (128, 512)) for _ in range(2)])
                ao = Rot([c.sb(st, "aao", (128, 512), BF16) for _ in range(2)])
                ko = Rot([c.sb(st, "ako", (128, 128)) for _ in range(2)])

                def prep(ci, t0, n, ttok, is_s, gcol, gb):
                    X, Xb = xin.nxt()
                    c.dma("sp", X[:, 0:n], projF[ci * 128:(ci + 1) * 128, t0:t0 + n], r=[projF_b], w=[Xb])
                    S, Sb = sq.nxt()
                    c.op("act", lambda: nc.scalar.activation(out=S[:, 0:n], in_=X[:, 0:n], func=AF.Square), r=[Xb], w=[Sb])
                    pt, pb = psr.nxt()
                    c.op("pe", lambda: nc.tensor.matmul(pt[:, 0:n], lhsT=blk64[:], rhs=S[:, 0:n], start=True, stop=True),
                         r=[Sb, kb], w=[pb])
                    R, Rb = rs.nxt()
                    c.op("act", lambda: nc.scalar.activation(out=R[:, 0:n], in_=pt[:, 0:n], func=AF.Sqrt, bias=epsq[:, 0:1]),
                         r=[pb, epsb], w=[Rb])
                    c.op("dve", lambda: nc.vector.reciprocal(out=R[:, 0:n], in_=R[:, 0:n]), r=[Rb], w=[Rb])
                    Qn, Qb = qn.nxt()
                    c.op("dve", lambda: nc.vector.scalar_tensor_tensor(
                        out=Qn[:, 0:n], in0=X[:, 0:n], scalar=gcol[:, 0:1], in1=R[:, 0:n], op0=ALU.mult, op1=ALU.mult),
                        r=[Xb, Rb, gb], w=[Qb])
                    if not is_s:
                        return Qn, Qb
                    pr, prb = psr.nxt()
                    c.op("pe", lambda: nc.tensor.matmul(pr[:, 0:n], lhsT=rotm[:], rhs=Qn[:, 0:n], start=True, stop=True),
                         r=[Qb, kb], w=[prb])
                    T1, T1b = t1.nxt()
                    c.op("dve", lambda: nc.vector.tensor_tensor(out=T1[:, 0:n], in0=pr[:, 0:n], in1=sin2[:, ttok:ttok + n],
                                                                op=ALU.mult), r=[prb, sb2], w=[T1b])
                    c.op("pool", lambda: nc.gpsimd.tensor_tensor(out=Qn[:, 0:n], in0=Qn[:, 0:n], in1=cos2[:, ttok:ttok + n],
                                                                 op=ALU.mult), r=[Qb, cb2], w=[Qb])
                    c.op("dve", lambda: nc.vector.tensor_tensor(out=Qn[:, 0:n], in0=Qn[:, 0:n], in1=T1[:, 0:n], op=ALU.add),
                         r=[Qb, T1b], w=[Qb])
                    return Qn, Qb

                epsq, epsb = c.sb(st, "epsq", (128, 1))
                c.op("dve", lambda: nc.vector.memset(epsq[:], 1e-6), w=[epsb])

                for (s0, L, is_s, pi) in SEQS:
                    nq = min(512, L)
                    nkc = (L + (256 if is_s else 0)) // 128
                    for j in range(2):
                        for t0 in range(0, L, nq):
                            Kn, Kb = prep(35 + j, s0 + t0, nq, t0, is_s, gk, gkb)
                            for h in range(2):
                                g = 2 * j + h
                                lo, hi = h * 64, (h + 1) * 64
                                olo, ohi = (1 - h) * 64, (2 - h) * 64
                                c.op("act", lambda g=g, lo=lo, hi=hi, Kn=Kn, t0=t0: nc.scalar.copy(
                                    out=KT2[lo:hi, g, t0:t0 + nq], in_=Kn[lo:hi, 0:nq]), r=[Kb], w=[KT2b])
                                c.op("dve", lambda g=g, lo=lo, hi=hi, olo=olo, ohi=ohi, Kn=Kn, t0=t0: nc.vector.tensor_copy(
                                    out=KT2[olo:ohi, g, t0:t0 + nq], in_=Kn[lo:hi, 0:nq]), r=[Kb], w=[KT2b])
                            if not is_s:
                                for hh in range(nq // 128):
                                    pt, pb = psr.nxt()
                                    c.op("pe", lambda Kn=Kn, hh=hh, pt=pt: nc.tensor.transpose(
                                        out=pt[:, 0:128], in_=Kn[:, hh * 128:(hh + 1) * 128], identity=ident[:]),
                                        r=[Kb, kb], w=[pb])
                                    Ko, Kob = ko.nxt()
                                    c.op("dve", lambda Ko=Ko, pt=pt: nc.vector.tensor_copy(out=Ko[:], in_=pt[:, 0:128]),
                                         r=[pb], w=[Kob])
                                    r0 = t0 + hh * 128
                                    c.dma("pool", O["nk"][pi, l, r0:r0 + 128, j * 128:(j + 1) * 128], Ko[:],
                                          r=[Kob], w=[out_b])
                    for kc in range(nkc):
                        Vi, Vib = vin.nxt()
                        if kc * 128 < L:
                            r0 = s0 + kc * 128
                            c.dma("sp", Vi[:], projT[r0:r0 + 128, 4736:4992], r=[projT_b], w=[Vib])
                            if not is_s:
                                c.dma("pool", O["nv"][pi, l, kc * 128:(kc + 1) * 128, :], projT[r0:r0 + 128, 4736:4992],
                                      r=[projT_b], w=[out_b])
                        else:
                            r0 = kc * 128 - L
                            c.dma("sp", Vi[:], I["cv"][l, r0:r0 + 128, :], w=[Vib])
                            Ki, Kib = vin.nxt()
                            c.dma("sp", Ki[:], I["ck"][l, r0:r0 + 128, :], w=[Kib])
                            for j in range(2):
                                pt, pb = psr.nxt()
                                c.op("pe", lambda Ki=Ki, j=j, pt=pt: nc.tensor.transpose(
                                    out=pt[:, 0:128], in_=Ki[:, j * 128:(j + 1) * 128], identity=ident[:]),
                                    r=[Kib, kb], w=[pb])
                                for h in range(2):
                                    g = 2 * j + h
                                    lo, hi = h * 64, (h + 1) * 64
                                    olo, ohi = (1 - h) * 64, (2 - h) * 64
                                    c.op("act", lambda g=g, lo=lo, hi=hi, pt=pt, kc=kc: nc.scalar.copy(
                                        out=KT2[lo:hi, g, kc * 128:(kc + 1) * 128], in_=pt[lo:hi, 0:128]), r=[pb], w=[KT2b])
                                    c.op("dve", lambda g=g, lo=lo, hi=hi, olo=olo, ohi=ohi, pt=pt, kc=kc: nc.vector.tensor_copy(
                                        out=KT2[olo:ohi, g, kc * 128:(kc + 1) * 128], in_=pt[lo:hi, 0:128]), r=[pb], w=[KT2b])
                        c.op("dve", lambda kc=kc, Vi=Vi: nc.vector.tensor_copy(
                            out=VA[:, kc, :, 0:64], in_=Vi[:].rearrange("p (g d) -> p g d", g=4)), r=[Vib], w=[VAb])
                        c.op("act", lambda kc=kc, Vi=Vi: nc.scalar.copy(
                            out=VB[:, kc, :, 64:128], in_=Vi[:].rearrange("p (g d) -> p g d", g=4)), r=[Vib], w=[VBb])
                    for i in range(8):
                        for t0 in range(0, L, nq):
                            Qn, Qb = prep(27 + i, s0 + t0, nq, t0, is_s, gq, gqb)
                            c.op("act", lambda Qn=Qn, i=i, t0=t0: nc.scalar.copy(out=QT[:, i, t0:t0 + nq], in_=Qn[:, 0:nq]),
                                 r=[Qb], w=[QTb])
                    for t0 in range(0, L, nq):
                        for i in range(8):
                            g = i // 2
                            bx, bxb = psr.nxt()
                            by, byb = psr.nxt()
                            steps = [(kc, h) for kc in range(nkc) for h in range(2)]
                            scored = {}

                            def score(si):
                                kc, h = steps[si]
                                lo, hi = h * 64, (h + 1) * 64
                                pss, pssb = psr.nxt()
                                while pss is bx or pss is by:
                                    pss, pssb = psr.nxt()
                                c.op("pe", lambda lo=lo, hi=hi, pss=pss, kc=kc: nc.tensor.matmul(
                                    pss[:, 0:nq], lhsT=KT2[lo:hi, g, kc * 128:(kc + 1) * 128], rhs=QT[lo:hi, i, t0:t0 + nq],
                                    start=True, stop=True), r=[KT2b, QTb], w=[pssb])
                                scored[si] = (pss, pssb)
                            DEPTH_SC = 3
                            for si in range(min(DEPTH_SC, len(steps))):
                                score(si)
                            for si, (kc, h) in enumerate(steps):
                                pss, pssb = scored.pop(si)
                                Pe, Peb = pexp.nxt()
                                c.op("act", lambda Pe=Pe, pss=pss: nc.scalar.activation(
                                    out=Pe[:, 0:nq], in_=pss[:, 0:nq], func=AF.Exp, scale=0.125), r=[pssb], w=[Peb])
                                if si + DEPTH_SC < len(steps):
                                    score(si + DEPTH_SC)
                                acc, accb, Vt, Vtb = (bx, bxb, VA, VAb) if h == 0 else (by, byb, VB, VBb)
                                c.op("pe", lambda acc=acc, Vt=Vt, kc=kc, Pe=Pe: nc.tensor.matmul(
                                    acc[:, 0:nq], lhsT=Vt[:, kc, g, :], rhs=Pe[:, 0:nq],
                                    start=(kc == 0), stop=(kc == nkc - 1)), r=[Vtb, Peb], w=[accb])
                            R, Rb = rec.nxt()
                            c.op("dve", lambda R=R, bx=bx: nc.vector.reciprocal(out=R[0:64, 0:nq], in_=bx[64:128, 0:nq]),
                                 r=[bxb], w=[Rb])
                            c.op("dve", lambda R=R, by=by: nc.vector.reciprocal(out=R[64:128, 0:nq], in_=by[0:64, 0:nq]),
                                 r=[byb], w=[Rb])
                            A, Ab = ao.nxt()
                            c.op("dve", lambda A=A, R=R, bx=bx: nc.vector.tensor_tensor(
                                out=A[0:64, 0:nq], in0=bx[0:64, 0:nq], in1=R[0:64, 0:nq], op=ALU.mult), r=[bxb, Rb], w=[Ab])
                            c.op("dve", lambda A=A, R=R, by=by: nc.vector.tensor_tensor(
                                out=A[64:128, 0:nq], in0=by[64:128, 0:nq], in1=R[64:128, 0:nq], op=ALU.mult), r=[byb, Rb], w=[Ab])
                            c.dma("pool", mixT[1024 + i * 128:1024 + (i + 1) * 128, s0 + t0:s0 + t0 + nq], A[:, 0:nq],
                                  r=[Ab], w=[mixT_b])
                c.barrier()

        def stage_hyena(l):
            with ExitStack() as st:
                w1, w1b = c.sb(st, "hw1", (33, 64))
                w2, w2b = c.sb(st, "hw2", (64, 64))
                w3, w3b = c.sb(st, "hw3", (64, 1024))
                c.dma("sp", w1[:], I["hy_w1"][l], w=[w1b])
                c.dma("sp", w2[:], I["hy_w2"][l], w=[w2b])
                c.dma("sp", w3[:], I["hy_w3"][l], w=[w3b])
                pc, pcb = c.sb(st, "hpc", (64, 4))
                c.dma("sp", pc[:, 0:1], I["hy_b1"][l].rearrange("(a b) -> a b", b=1), w=[pcb])
                c.dma("sp", pc[:, 1:2], I["hy_freq"][l, 0].rearrange("(a b) -> a b", b=1), w=[pcb])
                c.dma("sp", pc[:, 2:3], I["hy_b2"][l].rearrange("(a b) -> a b", b=1), w=[pcb])
                c.dma("sp", pc[:, 3:4], I["hy_freq"][l, 1].rearrange("(a b) -> a b", b=1), w=[pcb])
                bf, bfb = c.sb(st, "hbf", (64, 2))
                c.op("dve", lambda: nc.vector.tensor_tensor(out=bf[:, 0:1], in0=pc[:, 0:1], in1=pc[:, 1:2], op=ALU.mult),
                     r=[pcb], w=[bfb])
                c.op("dve", lambda: nc.vector.tensor_tensor(out=bf[:, 1:2], in0=pc[:, 2:3], in1=pc[:, 3:4], op=ALU.mult),
                     r=[pcb], w=[bfb])
                dec, decb = c.sb(st, "hdec", (128, 2, 512))
                for d in range(2):
                    c.dma("sp", dec[:, d, :], I["hy_decay"][l, d].partition_broadcast(128), w=[decb])
                c.op("act", lambda: nc.scalar.activation(out=dec[:], in_=dec[:], func=AF.Abs), r=[decb], w=[decb])
                hbias, hbb = c.sb(st, "hbias", (1, 512))
                c.dma("sp", hbias[:], I["hy_bias"][l].rearrange("(a b) -> a b", a=1), w=[hbb])
                tx0, tx0b = load_cols(st, "tx0", [I["hy_short"][l, j, 0:512].rearrange("(c p) -> c p", p=128) for j in range(3)], 12)
                tapb, tapbb = c.sb(st, "htap", (128, 3, 1024))
                for j in range(3):
                    c.dma("sp", tapb[:, j, :], I["hy_short"][l, j, 512:1536].partition_broadcast(128), w=[tapbb])

                HP = {}

                def sin_layer(ps, n, colf, colbf, dst, dstb, psb):
                    A, Ab = HP['a1'].nxt()
                    Kt, Ktb = HP['k1'].nxt()
                    c.op("dve", lambda: nc.vector.tensor_scalar(out=A[:, 0:n], in0=ps[0:64, 0:n], scalar1=colf, scalar2=colbf,
                                                                op0=ALU.mult, op1=ALU.add), r=[psb, pcb, bfb], w=[Ab])
                    c.op("dve", lambda: nc.vector.tensor_scalar(out=Kt[:, 0:n], in0=A[:, 0:n], scalar1=1.0 / TWO_PI, scalar2=MAGIC,
                                                                op0=ALU.mult, op1=ALU.add), r=[Ab], w=[Ktb])
                    c.op("dve", lambda: nc.vector.tensor_scalar(out=Kt[:, 0:n], in0=Kt[:, 0:n], scalar1=-MAGIC, scalar2=None,
                                                                op0=ALU.add), r=[Ktb], w=[Ktb])
                    c.op("dve", lambda: nc.vector.scalar_tensor_tensor(out=A[:, 0:n], in0=Kt[:, 0:n], scalar=-TWO_PI, in1=A[:, 0:n],
                                                                       op0=ALU.mult, op1=ALU.add), r=[Ktb, Ab], w=[Ab])
                    c.op("act", lambda: nc.scalar.activation(out=dst, in_=A[:, 0:n], func=AF.Sin), r=[Ab], w=[dstb])

                for (n, tag, seqs) in ((LP, "p", SEQS[0:2]), (LS, "s", SEQS[2:3])):
                    ntc = n // 128
                    tw = min(512, n)
                    with ExitStack() as s2:
                        nt01, ntb = c.sb(s2, "hnt01", (128, ntc))
                        c.dma("sp", nt01[:], I["k_nt01" + tag][:, :], w=[ntb])
                        wf, wfb = c.sb(s2, "hwf", (128, ntc))
                        c.dma("sp", wf[:], I["k_wf" + tag][:, :], w=[wfb])
                        HS, HSb = c.sb(s2, "hHS", (128, ntc, 512), BF16)
                        HD, HDb = c.sb(s2, "hHD", (128, ntc, 512), BF16)
                        U = [c.sb(s2, "hU", (128, ntc, 512), BF16) for _ in seqs]
                        YRE = [c.sb(s2, "hYRE", (128, ntc, 512), BF16) for _ in seqs]
                        YIM = [c.sb(s2, "hYIM", (128, ntc, 512), BF16) for _ in seqs]
                        sf0, sf0b = c.sb(s2, "hsf0", (128, ntc, 1), BF16)
                        Sfv = I["k_Sf" + tag].rearrange("(tc p) f -> p tc f", p=128)
                        Cfv = I["k_Cf" + tag].rearrange("(tc p) f -> p tc f", p=128)
                        STv = I["k_SfT" + tag].rearrange("(tc p) f -> p tc f", p=128)
                        c.dma("sp", sf0[:], Sfv[:, :, 0:1], w=[sf0b])
                        hn, hnb = c.sb(s2, "hhn", (1, 512))
                        s3 = ExitStack()
                        HP['a1'] = Rot([c.sb(s3, "ha1", (64, 512)) for _ in range(2)])
                        HP['k1'] = Rot([c.sb(s3, "hk1", (64, 512)) for _ in range(2)])
                        ef = Rot([c.sb(s3, "hef", (128, 512)) for _ in range(4)])
                        zT, zTb = c.sb(s3, "hzT", (33, n))
                        c.dma("sp", zT[:], I["k_zT" + tag][:, :], w=[zTb])
                        h1T, h1b = c.sb(s3, "hh1T", (64, n))
                        h2T_, h2b = c.sb(s3, "hh2T", (64, n))
                        for t0 in range(0, n, tw):
                            pt, pb = psr.nxt()
                            c.op("pe", lambda pt=pt, t0=t0: nc.tensor.matmul(pt[0:64, 0:tw], lhsT=w1[:, :], rhs=zT[:, t0:t0 + tw],
                                                                            start=True, stop=True), r=[w1b, zTb], w=[pb])
                            sin_layer(pt, tw, pc[:, 1:2], bf[:, 0:1], h1T[:, t0:t0 + tw], h1b, pb)
                            pt, pb = psr.nxt()
                            c.op("pe", lambda pt=pt, t0=t0: nc.tensor.matmul(pt[0:64, 0:tw], lhsT=w2[:, :], rhs=h1T[:, t0:t0 + tw],
                                                                            start=True, stop=True), r=[w2b, h1b], w=[pb])
                            sin_layer(pt, tw, pc[:, 3:4], bf[:, 1:2], h2T_[:, t0:t0 + tw], h2b, pb)
                        for tc in range(ntc):
                            hh = []
                            for d in range(2):
                                pt, pb = psr.nxt()
                                c.op("pe", lambda pt=pt, tc=tc, d=d: nc.tensor.matmul(
                                    pt[:], lhsT=h2T_[:, tc * 128:(tc + 1) * 128], rhs=w3[:, d * 512:(d + 1) * 512],
                                    start=True, stop=True), r=[h2b, w3b], w=[pb])
                                E, Eb = ef.nxt()
                                c.op("act", lambda E=E, d=d, tc=tc: nc.scalar.activation(
                                    out=E[:], in_=dec[:, d, :], func=AF.Exp, scale=nt01[:, tc:tc + 1]), r=[decb, ntb], w=[Eb])
                                c.op("dve", lambda E=E, pt=pt: nc.vector.tensor_tensor(out=E[:], in0=pt[:], in1=E[:], op=ALU.mult),
                                     r=[pb, Eb], w=[Eb])
                                hh.append((E, Eb))
                            (Hf, Hfb), (Hb_, Hbb) = hh
                            if tc == 0:
                                c.op("dve", lambda Hf=Hf: nc.vector.tensor_tensor(out=Hf[0:1, :], in0=Hf[0:1, :], in1=hbias[:],
                                                                                  op=ALU.add), r=[Hfb, hbb], w=[Hfb])
                            c.op("dve", lambda Hf=Hf, Hb_=Hb_, tc=tc: nc.vector.tensor_tensor(
                                out=HS[:, tc, :], in0=Hf[:], in1=Hb_[:], op=ALU.add), r=[Hfb, Hbb], w=[HSb])
                            c.op("pool", lambda Hf=Hf, Hb_=Hb_, tc=tc: nc.gpsimd.tensor_tensor(
                                out=HD[:, tc, :], in0=Hf[:], in1=Hb_[:], op=ALU.subtract), r=[Hfb, Hbb], w=[HDb])
                        pn, pnb = psr.nxt()
                        for tc in range(ntc):
                            c.op("pe", lambda tc=tc: nc.tensor.matmul(pn[0:1, :], lhsT=sf0[:, tc, :], rhs=HS[:, tc, :],
                                                                     start=(tc == 0), stop=(tc == ntc - 1)),
                                 r=[sf0b, HSb], w=[pnb])
                        c.op("act", lambda: nc.scalar.activation(out=hn[:], in_=pn[0:1, :], func=AF.Identity,
                                                                 scale=wf[0:1, 0:1]), r=[pnb, wfb], w=[hnb])
                        c.barrier()
                        s3.close()
                        s3 = ExitStack()
                        xs3 = Rot([c.sb(s3, "hx3", (128, 1024)) for _ in range(3)])
                        ua, uab = c.sb(s3, "hua", (128, 1024))
                        ub, ubb = c.sb(s3, "hub", (128, 1024))
                        for si, (s0, L, is_s, pi) in enumerate(seqs):
                            Ut, Utb = U[si]
                            for tc in range(ntc):
                                tl = []
                                for sh in (-1, 0, 1):
                                    X, Xb = xs3.nxt()
                                    lo = tc * 128 + sh
                                    a, b = max(lo, 0), min(lo + 128, L)
                                    if a != lo or b != lo + 128:
                                        c.op("pool", lambda X=X: nc.gpsimd.memset(X[:], 0.0), w=[Xb])
                                    c.dma("sp", X[a - lo:b - lo, :], projT[s0 + a:s0 + b, 2432:3456], r=[projT_b], w=[Xb])
                                    tl.append((X, Xb))
                                (Xm, Xmb), (X0, X0b), (Xp, Xpb) = tl
                                c.op("dve", lambda X0=X0: nc.vector.tensor_tensor(out=ua[:], in0=X0[:], in1=tapb[:, 1, :], op=ALU.mult),
                                     r=[X0b, tapbb], w=[uab])
                                c.op("pool", lambda Xm=Xm: nc.gpsimd.tensor_tensor(out=ub[:], in0=Xm[:], in1=tapb[:, 0, :], op=ALU.mult),
                                     r=[Xmb, tapbb], w=[ubb])
                                c.op("dve", lambda: nc.vector.tensor_tensor(out=ua[:], in0=ua[:], in1=ub[:], op=ALU.add),
                                     r=[uab, ubb], w=[uab])
                                c.op("pool", lambda Xp=Xp: nc.gpsimd.tensor_tensor(out=ub[:], in0=Xp[:], in1=tapb[:, 2, :], op=ALU.mult),
                                     r=[Xpb, tapbb], w=[ubb])
                                c.op("dve", lambda: nc.vector.tensor_tensor(out=ua[:], in0=ua[:], in1=ub[:], op=ALU.add),
                                     r=[uab, ubb], w=[uab])
                                c.op("dve", lambda Ut=Ut, tc=tc: nc.vector.tensor_tensor(
                                    out=Ut[:, tc, :], in0=ua[:, 0:512], in1=ua[:, 512:1024], op=ALU.mult), r=[uab], w=[Utb])
                        c.barrier()
                        s3.close()
                        s3 = ExitStack()
                        cfc = Rot([c.sb(s3, "hcfc", (128, 16, 128), BF16) for _ in range(2)])
                        sfc = Rot([c.sb(s3, "hsfc", (128, 16, 128), BF16) for _ in range(2)])
                        hsp = Rot([c.sb(s3, "hsp", (128, 512)) for _ in range(4)])
                        tmp = Rot([c.sb(s3, "htmp", (128, 512)) for _ in range(4)])
                        for fc in range(ntc):
                            Cc, Ccb = cfc.nxt()
                            Sc, Scb = sfc.nxt()
                            c.dma("sp", Cc[:, 0:ntc, :], Cfv[:, :, fc * 128:(fc + 1) * 128], w=[Ccb])
                            c.dma("sp", Sc[:, 0:ntc, :], Sfv[:, :, fc * 128:(fc + 1) * 128], w=[Scb])
                            Hs = []
                            for (M, Mb, Src, Srcb) in ((Cc, Ccb, HS, HSb), (Sc, Scb, HD, HDb)):
                                pt, pb = psr.nxt()
                                for tc in range(ntc):
                                    c.op("pe", lambda pt=pt, M=M, Src=Src, tc=tc: nc.tensor.matmul(
                                        pt[:], lhsT=M[:, tc, :], rhs=Src[:, tc, :], start=(tc == 0), stop=(tc == ntc - 1)),
                                        r=[Mb, Srcb], w=[pb])
                                Hx, Hxb = hsp.nxt()
                                c.op("act", lambda Hx=Hx, pt=pt, fc=fc: nc.scalar.activation(
                                    out=Hx[:], in_=pt[:], func=AF.Identity, scale=wf[:, fc:fc + 1]), r=[pb, wfb], w=[Hxb])
                                Hs.append((Hx, Hxb))
                            (Hre, Hreb), (Him, Himb) = Hs
                            if fc == 0:
                                c.op("dve", lambda Him=Him: nc.vector.tensor_copy(out=Him[0:1, :], in_=hn[:]), r=[hnb], w=[Himb])
                            for si in range(len(seqs)):
                                Ut, Utb = U[si]
                                Us = []
                                for (M, Mb) in ((Cc, Ccb), (Sc, Scb)):
                                    pt, pb = psr.nxt()
                                    for tc in range(ntc):
                                        c.op("pe", lambda pt=pt, M=M, Ut=Ut, tc=tc: nc.tensor.matmul(
                                            pt[:], lhsT=M[:, tc, :], rhs=Ut[:, tc, :], start=(tc == 0), stop=(tc == ntc - 1)),
                                            r=[Mb, Utb], w=[pb])
                                    Us.append((pt, pb))
                                (Ure, Ureb), (Uim, Uimb) = Us
                                T1, T1b = tmp.nxt()
                                T2, T2b = tmp.nxt()
                                T3, T3b = tmp.nxt()
                                T4, T4b = tmp.nxt()
                                c.op("dve", lambda: nc.vector.tensor_tensor(out=T1[:], in0=Ure[:], in1=Hre[:], op=ALU.mult),
                                     r=[Ureb, Hreb], w=[T1b])
                                c.op("dve", lambda: nc.vector.tensor_tensor(out=T2[:], in0=Uim[:], in1=Him[:], op=ALU.mult),
                                     r=[Uimb, Himb], w=[T2b])
                                c.op("dve", lambda: nc.vector.tensor_tensor(out=T3[:], in0=Ure[:], in1=Him[:], op=ALU.mult),
                                     r=[Ureb, Himb], w=[T3b])
                                c.op("dve", lambda: nc.vector.tensor_tensor(out=T4[:], in0=Uim[:], in1=Hre[:], op=ALU.mult),
                                     r=[Uimb, Hreb], w=[T4b])
                                Yr, Yrb = YRE[si]
                                Yi, Yib = YIM[si]
                                c.op("pool", lambda: nc.gpsimd.tensor_tensor(out=Yr[:, fc, :], in0=T1[:], in1=T2[:], op=ALU.subtract),
                                     r=[T1b, T2b], w=[Yrb])
                                c.op("pool", lambda: nc.gpsimd.tensor_tensor(out=Yi[:, fc, :], in0=T3[:], in1=T4[:], op=ALU.add),
                                     r=[T3b, T4b], w=[Yib])
                                if fc == 0:
                                    c.op("dve", lambda: nc.vector.tensor_copy(out=Yr[0:1, 0, :], in_=T1[0:1, :]), r=[T1b], w=[Yrb])
                                    c.op("dve", lambda: nc.vector.tensor_copy(out=Yi[0:1, 0, :], in_=T2[0:1, :]), r=[T2b], w=[Yib])
                        c.barrier()
                        s3.close()
                        s3 = ExitStack()
                        cft = Rot([c.sb(s3, "hcft", (128, 16, 512), BF16) for _ in range(2)])
                        sft = Rot([c.sb(s3, "hsft", (128, 16, 512), BF16) for _ in range(2)])
                        x0t = Rot([c.sb(s3, "hx0", (128, 514)) for _ in range(2)])
                        x0c = Rot([c.sb(s3, "hx0c", (128, 512)) for _ in range(2)])
                        yo = Rot([c.sb(s3, "hyo", (128, 512), BF16) for _ in range(2)])
                        for si, (s0, L, is_s, pi) in enumerate(seqs):
                            Yr, Yrb = YRE[si]
                            Yi, Yib = YIM[si]
                            for t0 in range(0, n, tw):
                                Ct, Ctb = cft.nxt()
                                St, Stb = sft.nxt()
                                c.dma("sp", Ct[:, 0:ntc, 0:tw], Cfv[:, :, t0:t0 + tw], w=[Ctb])
                                c.dma("sp", St[:, 0:ntc, 0:tw], STv[:, :, t0:t0 + tw], w=[Stb])
                                for cc in range(4):
                                    pt, pb = psr.nxt()
                                    for fc in range(ntc):
                                        c.op("pe", lambda pt=pt, fc=fc, cc=cc, Ct=Ct: nc.tensor.matmul(
                                            pt[:, 0:tw], lhsT=Yr[:, fc, cc * 128:(cc + 1) * 128], rhs=Ct[:, fc, 0:tw],
                                            start=(fc == 0), stop=False), r=[Yrb, Ctb], w=[pb])
                                    for fc in range(ntc):
                                        c.op("pe", lambda pt=pt, fc=fc, cc=cc, St=St: nc.tensor.matmul(
                                            pt[:, 0:tw], lhsT=Yi[:, fc, cc * 128:(cc + 1) * 128], rhs=St[:, fc, 0:tw],
                                            start=False, stop=(fc == ntc - 1)), r=[Yib, Stb], w=[pb])
                                    X, Xb = x0t.nxt()
                                    lo = t0 - 1
                                    a, b = max(lo, 0), min(lo + tw + 2, L)
                                    if a != lo or b != lo + tw + 2:
                                        c.op("pool", lambda X=X: nc.gpsimd.memset(X[:], 0.0), w=[Xb])
                                    ci = 15 + cc
                                    c.dma("sp", X[:, a - lo:b - lo], projF[ci * 128:(ci + 1) * 128, s0 + a:s0 + b],
                                          r=[projF_b], w=[Xb])
                                    Xc, Xcb = x0c.nxt()
                                    c.op("dve", lambda X=X, Xc=Xc, cc=cc: nc.vector.tensor_scalar(
                                        out=Xc[:, 0:tw], in0=X[:, 1:tw + 1], scalar1=tx0[:, 4 + cc:5 + cc], scalar2=None, op0=ALU.mult),
                                        r=[Xb, tx0b], w=[Xcb])
                                    c.op("dve", lambda X=X, Xc=Xc, cc=cc: nc.vector.scalar_tensor_tensor(
                                        out=Xc[:, 0:tw], in0=X[:, 0:tw], scalar=tx0[:, cc:cc + 1], in1=Xc[:, 0:tw],
                                        op0=ALU.mult, op1=ALU.add), r=[Xb, tx0b, Xcb], w=[Xcb])
                                    c.op("dve", lambda X=X, Xc=Xc, cc=cc: nc.vector.scalar_tensor_tensor(
                                        out=Xc[:, 0:tw], in0=X[:, 2:tw + 2], scalar=tx0[:, 8 + cc:9 + cc], in1=Xc[:, 0:tw],
                                        op0=ALU.mult, op1=ALU.add), r=[Xb, tx0b, Xcb], w=[Xcb])
                                    Yo, Yob = yo.nxt()
                                    c.op("dve", lambda Yo=Yo, Xc=Xc, pt=pt: nc.vector.tensor_tensor(
                                        out=Yo[:, 0:tw], in0=pt[:, 0:tw], in1=Xc[:, 0:tw], op=ALU.mult), r=[pb, Xcb], w=[Yob])
                                    c.dma("pool", mixT[512 + cc * 128:512 + (cc + 1) * 128, s0 + t0:s0 + t0 + tw], Yo[:, 0:tw],
                                          r=[Yob], w=[mixT_b])
                        c.barrier()
                        s3.close()
                c.barrier()

        epsl, epslb = c.sb(es, "epsl", (128, 1))
        c.op("dve", lambda: nc.vector.memset(epsl[:], 1e-5 / (ALPHA * ALPHA)), w=[epslb])

        def layer_norm(st, xt, xtb, gT, bT, gb, emit):
            sq, sqb = c.sb(st, "lnsq", (128, 16, 512))
            for kc in range(16):
                c.op("act", lambda kc=kc: nc.scalar.activation(out=sq[:, kc, :], in_=xt[:, kc, :], func=AF.Square),
                     r=[xtb], w=[sqb])
            pm, pmb = psr.nxt()
            pe2, pe2b = psr.nxt()
            for kc in range(16):
                c.op("pe", lambda kc=kc: nc.tensor.matmul(pm[:], lhsT=onesm[:], rhs=xt[:, kc, :], start=(kc == 0), stop=(kc == 15)),
                     r=[xtb, kb], w=[pmb])
            for kc in range(16):
                c.op("pe", lambda kc=kc: nc.tensor.matmul(pe2[:], lhsT=onesm[:], rhs=sq[:, kc, :], start=(kc == 0), stop=(kc == 15)),
                     r=[sqb, kb], w=[pe2b])
            mean, meanb = c.sb(st, "lnmean", (128, 512))
            rstd, rstdb = c.sb(st, "lnrstd", (128, 512))
            c.op("act", lambda: nc.scalar.copy(out=mean[:], in_=pm[:]), r=[pmb], w=[meanb])
            c.op("dve", lambda: nc.vector.tensor_tensor(out=rstd[:], in0=mean[:], in1=mean[:], op=ALU.mult), r=[meanb], w=[rstdb])
            c.op("dve", lambda: nc.vector.tensor_tensor(out=rstd[:], in0=pe2[:], in1=rstd[:], op=ALU.subtract), r=[pe2b, rstdb], w=[rstdb])
            c.op("act", lambda: nc.scalar.activation(out=rstd[:], in_=rstd[:], func=AF.Sqrt, bias=epsl[:, 0:1]), r=[rstdb, epslb], w=[rstdb])
            c.op("dve", lambda: nc.vector.reciprocal(out=rstd[:], in_=rstd[:]), r=[rstdb], w=[rstdb])
            tk = Rot([c.sb(st, "lntk", (128, 512)) for _ in range(3)])
            for kc in range(16):
                T, Tb = tk.nxt()
                c.op("dve", lambda kc=kc, T=T: nc.vector.tensor_tensor(out=T[:], in0=xt[:, kc, :], in1=mean[:], op=ALU.subtract),
                     r=[xtb, meanb], w=[Tb])
                c.op("pool", lambda T=T: nc.gpsimd.tensor_tensor(out=T[:], in0=T[:], in1=rstd[:], op=ALU.mult), r=[Tb, rstdb], w=[Tb])
                c.op("act", lambda kc=kc, T=T: nc.scalar.activation(out=xt[:, kc, :], in_=T[:], func=AF.Identity,
                                                                     scale=gT[:, kc:kc + 1], bias=bT[:, kc:kc + 1]),
                     r=[Tb, gb], w=[xtb])
                emit(kc, T, Tb)

        def stage_C1(l):
            mt, mtb = modT[l]
            with ExitStack() as st:
                gT, gTb = load_cols(st, "l1g", I["ln1_g"][l].rearrange("(a p) -> a p", p=128), 16)
                bT, bTb = load_cols(st, "l1b", I["ln1_b"][l].rearrange("(a p) -> a p", p=128), 16)
                G2, G2b = c.sb(st, "G2", (128, 16, 2))
                B2, B2b = c.sb(st, "B2", (128, 16, 2))
                c.op("dve", lambda: nc.vector.tensor_tensor(out=G2[:], in0=mt[:, 64:80, :],
                                                            in1=gT[:, :].unsqueeze(2).to_broadcast([128, 16, 2]), op=ALU.mult),
                     r=[mtb, gTb], w=[G2b])
                c.op("dve", lambda: nc.vector.tensor_tensor(out=B2[:], in0=mt[:, 64:80, :],
                                                            in1=bT[:, :].unsqueeze(2).to_broadcast([128, 16, 2]), op=ALU.mult),
                     r=[mtb, bTb], w=[B2b])
                c.op("dve", lambda: nc.vector.tensor_tensor(out=B2[:], in0=B2[:], in1=mt[:, 48:64, :], op=ALU.add),
                     r=[mtb, B2b], w=[B2b])
                lb = Buf("lnp")
                lb.w.update(gTb.w); lb.w.update(bTb.w)
                Wap, Wbuf = wb[("w_out", l)]
                Wv = Wap.rearrange("(kc p) n -> p kc n", p=128)
                for tt in range(5):
                    gi = 0 if tt == 0 else 1
                    with ExitStack() as s2:
                        M, Mb = c.sb(s2, "c1M", (128, 16, 512), BF16)
                        xt, xtb = c.sb(s2, "c1x", (128, 16, 512))
                        h2o, h2ob = c.sb(s2, "c1h", (128, 16, 512), BF16)
                        wt = Rot([c.sb(s2, "c1w", (128, 16, 512), BF16) for _ in range(2)])
                        c.dma("sp", M[:], mixTv[:, :, tt * 512:(tt + 1) * 512], r=[mixT_b], w=[Mb])
                        c.dma("sp", xt[:], xTv[:, :, tt * 512:(tt + 1) * 512], r=[xT_b[tt]], w=[xtb])
                        for db in range(4):
                            W, Wb_ = wt.nxt()
                            c.dma("sp", W[:], Wv[:, :, db * 512:(db + 1) * 512], r=[Wbuf], w=[Wb_])
                            for jj in range(4):
                                dc = db * 4 + jj
                                pt, pb = psr.nxt()
                                for kc in range(16):
                                    c.op("pe", lambda kc=kc, pt=pt, W=W, jj=jj: nc.tensor.matmul(
                                        pt[:], lhsT=W[:, kc, jj * 128:(jj + 1) * 128], rhs=M[:, kc, :],
                                        start=(kc == 0), stop=(kc == 15)), r=[Wb_, Mb], w=[pb])
                                c.op("dve", lambda dc=dc, pt=pt: nc.vector.scalar_tensor_tensor(
                                    out=xt[:, dc, :], in0=pt[:], scalar=mt[:, 32 + dc, gi:gi + 1], in1=xt[:, dc, :],
                                    op0=ALU.mult, op1=ALU.add), r=[pb, mtb, xtb], w=[xtb])

                        def emit(kc, T, Tb):
                            c.op("dve", lambda: nc.vector.tensor_scalar(
                                out=h2o[:, kc, :], in0=T[:], scalar1=G2[:, kc, gi:gi + 1], scalar2=B2[:, kc, gi:gi + 1],
                                op0=ALU.mult, op1=ALU.add), r=[Tb, G2b, B2b], w=[h2ob])
                        layer_norm(s2, xt, xtb, gT, bT, lb, emit)
                        c.dma("pool", xTv[:, :, tt * 512:(tt + 1) * 512], xt[:], r=[xtb], w=[xT_b[tt]])
                        c.dma("pool", h2Tv[:, :, tt * 512:(tt + 1) * 512], h2o[:], r=[h2ob], w=[h2T_b])
                        c.barrier()

        def stage_C2(l):
            mt, mtb = modT[l]
            last = (l == DEPTH - 1)
            with ExitStack() as st:
                gT, gTb = load_cols(st, "l2g", I["ln2_g"][l].rearrange("(a p) -> a p", p=128), 16)
                bT, bTb = load_cols(st, "l2b", I["ln2_b"][l].rearrange("(a p) -> a p", p=128), 16)
                lb = Buf("lnp2")
                lb.w.update(gTb.w); lb.w.update(bTb.w)
                taps = []
                for j in range(3):
                    taps.append(load_cols(st, "ftap%d" % j, I["ffn_conv"][l, j].rearrange("(a p) -> a p", p=128), 88))
                tapb = Buf("ftaps")
                for _, b in taps:
                    tapb.w.update(b.w)
                Uap, Ubuf = wb[("ffn_up", l)]
                Uv = Uap.rearrange("(kc p) n -> p kc n", p=128)
                Dap, Dbuf = wb[("ffn_down", l)]
                Dv = Dap.rearrange("(kc p) n -> p kc n", p=128)
                for tt in range(5):
                    gi = 0 if tt == 0 else 1
                    t0 = tt * 512
                    hl = tt >= 2
                    hr = 1 <= tt <= 3
                    segs = [(0, 256), (256, 512)] if tt == 0 else [(0, 512)]
                    with ExitStack() as s2:
                        F, Fb = c.sb(s2, "c2F", (128, 44, 512), BF16)
                        with ExitStack() as s3:
                            H, Hb = c.sb(s3, "c2H", (128, 16, 514), BF16)
                            c.dma("sp", H[:, :, 1:513], h2Tv[:, :, t0:t0 + 512], r=[h2T_b], w=[Hb])
                            if hl:
                                c.dma("sp", H[:, :, 0:1], h2Tv[:, :, t0 - 1:t0], r=[h2T_b], w=[Hb])
                            if hr:
                                c.dma("sp", H[:, :, 513:514], h2Tv[:, :, t0 + 512:t0 + 513], r=[h2T_b], w=[Hb])
                            wa = Rot([c.sb(s3, "c2wa", (128, 16, 512), BF16) for _ in range(2)])
                            wbb = Rot([c.sb(s3, "c2wb", (128, 16, 512), BF16) for _ in range(2)])
                            uu = Rot([c.sb(s3, "c2u", (128, 512)) for _ in range(4)])
                            for jb in range(11):
                                Wa, Wab = wa.nxt()
                                Wb2, Wb2b = wbb.nxt()
                                c.dma("sp", Wa[:], Uv[:, :, jb * 512:(jb + 1) * 512], r=[Ubuf], w=[Wab])
                                c.dma("sp", Wb2[:], Uv[:, :, DFF + jb * 512:DFF + (jb + 1) * 512], r=[Ubuf], w=[Wb2b])
                                for jj in range(4):
                                    j = jb * 4 + jj
                                    us = []
                                    for (W, Wbf, cidx) in ((Wa, Wab, j), (Wb2, Wb2b, 44 + j)):
                                        pt, pb = psr.nxt()
                                        for kc in range(16):
                                            c.op("pe", lambda kc=kc, pt=pt, W=W, jj=jj: nc.tensor.matmul(
                                                pt[:], lhsT=W[:, kc, jj * 128:(jj + 1) * 128], rhs=H[:, kc, 1:513],
                                                start=(kc == 0), stop=(kc == 15)), r=[Wbf, Hb], w=[pb])
                                        ph, phb = (None, None)
                                        if hl or hr:
                                            ph, phb = psr.nxt()
                                            for (flag, col, o) in ((hl, 0, 0), (hr, 513, 1)):
                                                if not flag:
                                                    continue
                                                for kc in range(16):
                                                    c.op("pe", lambda kc=kc, ph=ph, W=W, jj=jj, col=col, o=o: nc.tensor.matmul(
                                                        ph[:, o:o + 1], lhsT=W[:, kc, jj * 128:(jj + 1) * 128], rhs=H[:, kc, col:col + 1],
                                                        start=(kc == 0), stop=(kc == 15)), r=[Wbf, Hb], w=[phb])
                                        Ut, Utb = uu.nxt()
                                        w0 = taps[0][0][:, cidx:cidx + 1]
                                        w1 = taps[1][0][:, cidx:cidx + 1]
                                        w2 = taps[2][0][:, cidx:cidx + 1]
                                        c.op("act", lambda Ut=Ut, pt=pt, w1=w1: nc.scalar.activation(
                                            out=Ut[:], in_=pt[:], func=AF.Identity, scale=w1), r=[pb, tapb], w=[Utb])
                                        for (a, b) in segs:
                                            c.op("dve", lambda Ut=Ut, pt=pt, w0=w0, a=a, b=b: nc.vector.scalar_tensor_tensor(
                                                out=Ut[:, a + 1:b], in0=pt[:, a:b - 1], scalar=w0, in1=Ut[:, a + 1:b],
                                                op0=ALU.mult, op1=ALU.add), r=[pb, tapb, Utb], w=[Utb])
                                            c.op("dve", lambda Ut=Ut, pt=pt, w2=w2, a=a, b=b: nc.vector.scalar_tensor_tensor(
                                                out=Ut[:, a:b - 1], in0=pt[:, a + 1:b], scalar=w2, in1=Ut[:, a:b - 1],
                                                op0=ALU.mult, op1=ALU.add), r=[pb, tapb, Utb], w=[Utb])
                                        if hl:
                                            c.op("dve", lambda Ut=Ut, ph=ph, w0=w0: nc.vector.scalar_tensor_tensor(
                                                out=Ut[:, 0:1], in0=ph[:, 0:1], scalar=w0, in1=Ut[:, 0:1],
                                                op0=ALU.mult, op1=ALU.add), r=[phb, tapb, Utb], w=[Utb])
                                        if hr:
                                            c.op("dve", lambda Ut=Ut, ph=ph, w2=w2: nc.vector.scalar_tensor_tensor(
                                                out=Ut[:, 511:512], in0=ph[:, 1:2], scalar=w2, in1=Ut[:, 511:512],
                                                op0=ALU.mult, op1=ALU.add), r=[phb, tapb, Utb], w=[Utb])
                                        us.append((Ut, Utb))
                                    (Ua, Uab), (Ub_, Ubb) = us
                                    c.op("act", lambda Ua=Ua: nc.scalar.activation(out=Ua[:], in_=Ua[:], func=AF.Silu), r=[Uab], w=[Uab])
                                    c.op("pool", lambda Ua=Ua, Ub_=Ub_, j=j: nc.gpsimd.tensor_tensor(
                                        out=F[:, j, :], in0=Ua[:], in1=Ub_[:], op=ALU.mult), r=[Uab, Ubb], w=[Fb])
                            c.barrier()
                        xt, xtb = c.sb(s2, "c2x", (128, 16, 512))
                        c.dma("sp", xt[:], xTv[:, :, t0:t0 + 512], r=[xT_b[tt]], w=[xtb])
                        with ExitStack() as s3:
                            wd = Rot([c.sb(s3, "c2wd", (128, 44, 256), BF16) for _ in range(2)])
                            for db in range(8):
                                W, Wb_ = wd.nxt()
                                c.dma("sp", W[:], Dv[:, :, db * 256:(db + 1) * 256], r=[Dbuf], w=[Wb_])
                                for jj in range(2):
                                    dc = db * 2 + jj
                                    pt, pb = psr.nxt()
                                    for kc in range(44):
                                        c.op("pe", lambda kc=kc, pt=pt, W=W, jj=jj: nc.tensor.matmul(
                                            pt[:], lhsT=W[:, kc, jj * 128:(jj + 1) * 128], rhs=F[:, kc, :],
                                            start=(kc == 0), stop=(kc == 43)), r=[Wb_, Fb], w=[pb])
                                    c.op("dve", lambda dc=dc, pt=pt: nc.vector.scalar_tensor_tensor(
                                        out=xt[:, dc, :], in0=pt[:], scalar=mt[:, 80 + dc, gi:gi + 1], in1=xt[:, dc, :],
                                        op0=ALU.mult, op1=ALU.add), r=[pb, mtb, xtb], w=[xtb])
                            c.barrier()
                        with ExitStack() as s3:
                            layer_norm(s3, xt, xtb, gT, bT, lb, lambda kc, T, Tb: None)
                            if not last:
                                c.dma("pool", xTv[:, :, t0:t0 + 512], xt[:], r=[xtb], w=[xT_b[tt]])
                            else:
                                orow = Rot([c.sb(s3, "c2o", (128, D)) for _ in range(2)])
                                for ts in range(4):
                                    Or, Orb = orow.nxt()
                                    for g in range(4):
                                        pt, pb = psr.nxt()
                                        for j4 in range(4):
                                            kc = g * 4 + j4
                                            c.op("pe", lambda kc=kc, j4=j4, pt=pt, ts=ts: nc.tensor.transpose(
                                                out=pt[:, j4 * 128:(j4 + 1) * 128], in_=xt[:, kc, ts * 128:(ts + 1) * 128],
                                                identity=ident[:]), r=[xtb, kb], w=[pb])
                                        evac(g, Or[:, g * 512:(g + 1) * 512], pt[:], [pb], [Orb])
                                    tok = t0 + ts * 128
                                    dst = O["yp"][tok:tok + 128, :] if tok < 512 else O["ys"][tok - 512:tok - 384, :]
                                    c.dma("pool", dst, Or[:], r=[Orb], w=[out_b])
                            c.barrier()

        RNAMES = ["kap", "w0", "w1", "b0", "b1", "kd0", "kd1", "r", "bon", "g", "v", "y0", "y1"]
        if dbg == "r2":
            RA = {n: c.dram("rw_" + n, (NTOK, 512), kind=("ExternalOutput" if n in ("y0", "y1") else "ExternalInput"))
                  for n in RNAMES}
        else:
            RA = {n: c.dram("rw_" + n, (NTOK, 512)) for n in RNAMES}
        VT, VTb = c.dram("rw_vT", (512, NTOK))
        YT = [c.dram("rw_yT%d" % d, (512, NTOK)) for d in range(2)]

        def stage_rwkv(l, parts=("r1", "r2", "r3")):
            m0 = c.mute
            c.mute = m0 or ("r1" not in parts)
            with ExitStack() as st:
                tapb, tapbb = c.sb(st, "rtap", (128, 3, 1536))
                for j in range(3):
                    c.dma("sp", tapb[:, j, :], I["rwkv_shift"][l, j, 0:1536].partition_broadcast(128), w=[tapbb])
                bc = {}
                for nm, src in (("kk", I["rwkv_kk"][l]), ("ka", I["rwkv_ka"][l]), ("rk", I["rwkv_rk"][l]),
                                ("w00", I["rwkv_w0"][l, 0]), ("w01", I["rwkv_w0"][l, 1]),
                                ("a00", I["rwkv_a0"][l, 0]), ("a01", I["rwkv_a0"][l, 1])):
                    t, b = c.sb(st, "rb_" + nm, (128, 512))
                    c.dma("sp", t[:], src.partition_broadcast(128), w=[b])
                    bc[nm] = (t, b)
                ltap, ltapb = load_cols(st, "rltap", [I["rwkv_shift"][l, j, 1536:1920].rearrange("(c p) -> c p", p=128) for j in range(3)], 9)
                w2, w2b = c.sb(st, "rw2", (128, 512))
                a2, a2b = c.sb(st, "ra2", (128, 512))
                g2, g2b = c.sb(st, "rg2", (128, 512))
                c.dma("sp", w2[:], I["rwkv_w2"][l].rearrange("d r n -> (d r) n"), w=[w2b])
                c.dma("sp", a2[:], I["rwkv_a2"][l].rearrange("d r n -> (d r) n"), w=[a2b])
                c.dma("sp", g2[:], I["rwkv_g2"][l], w=[g2b])
                eps12, eps12b = c.sb(st, "reps", (128, 1))
                c.op("dve", lambda: nc.vector.memset(eps12[:], 1e-12), w=[eps12b])
                x3 = [c.sb(st, "rx3", (128, 1536)) for _ in range(3)]
                rkv, rkvb = c.sb(st, "rrkv", (128, 1536))
                tq, tqb = c.sb(st, "rtq", (128, 1536))
                lx = Rot([c.sb(st, "rlx", (128, 130)) for _ in range(2)])
                lt = [c.sb(st, "rlt", (128, 128)) for _ in range(3)]
                wk = Rot([c.sb(st, "rwk", (128, 512)) for _ in range(12)])
                sm = Rot([c.sb(st, "rsm", (128, 8)) for _ in range(8)])
                vo = Rot([c.sb(st, "rvo", (128, 4, 128)) for _ in range(2)])
                for (s0, L, is_s, pi) in SEQS:
                    for q0 in range(0, L, 128):
                        g0 = s0 + q0
                        for k_, sh in enumerate((-1, 0, 1)):
                            X, Xb = x3[k_]
                            lo = q0 + sh
                            a, b = max(lo, 0), min(lo + 128, L)
                            if a != lo or b != lo + 128:
                                c.op("pool", lambda X=X: nc.gpsimd.memset(X[:], 0.0), w=[Xb])
                            c.dma("sp", X[a - lo:b - lo, :], projT[s0 + a:s0 + b, 0:1536], r=[projT_b], w=[Xb])
                        c.op("dve", lambda: nc.vector.tensor_tensor(out=rkv[:], in0=x3[1][0][:], in1=tapb[:, 1, :], op=ALU.mult),
                             r=[x3[1][1], tapbb], w=[rkvb])
                        for k_, j in ((0, 0), (2, 2)):
                            c.op("pool", lambda k_=k_, j=j: nc.gpsimd.tensor_tensor(out=tq[:], in0=x3[k_][0][:], in1=tapb[:, j, :], op=ALU.mult),
                                 r=[x3[k_][1], tapbb], w=[tqb])
                            c.op("dve", lambda: nc.vector.tensor_tensor(out=rkv[:], in0=rkv[:], in1=tq[:], op=ALU.add),
                                 r=[rkvb, tqb], w=[rkvb])
                        R_ = rkv[:, 0:512]
                        K_ = rkv[:, 512:1024]
                        V_ = rkv[:, 1024:1536]
                        for n_, ci in enumerate((12, 13, 14)):
                            X, Xb = lx.nxt()
                            lo = q0 - 1
                            a, b = max(lo, 0), min(lo + 130, L)
                            if a != lo or b != lo + 130:
                                c.op("pool", lambda X=X: nc.gpsimd.memset(X[:], 0.0), w=[Xb])
                            c.dma("sp", X[:, a - lo:b - lo], projF[ci * 128:(ci + 1) * 128, s0 + a:s0 + b], r=[projF_b], w=[Xb])
                            T_, Tb_ = lt[n_]
                            c.op("dve", lambda X=X, T_=T_, n_=n_: nc.vector.tensor_scalar(
                                out=T_[:], in0=X[:, 1:129], scalar1=ltap[:, 3 + n_:4 + n_], scalar2=None, op0=ALU.mult),
                                r=[Xb, ltapb], w=[Tb_])
                            c.op("dve", lambda X=X, T_=T_, n_=n_: nc.vector.scalar_tensor_tensor(
                                out=T_[:], in0=X[:, 0:128], scalar=ltap[:, n_:n_ + 1], in1=T_[:], op0=ALU.mult, op1=ALU.add),
                                r=[Xb, ltapb, Tb_], w=[Tb_])
                            c.op("dve", lambda X=X, T_=T_, n_=n_: nc.vector.scalar_tensor_tensor(
                                out=T_[:], in0=X[:, 2:130], scalar=ltap[:, 6 + n_:7 + n_], in1=T_[:], op0=ALU.mult, op1=ALU.add),
                                r=[Xb, ltapb, Tb_], w=[Tb_])
                            if ci == 12:
                                c.op("act", lambda T_=T_: nc.scalar.activation(out=T_[:], in_=T_[:], func=AF.Tanh), r=[Tb_], w=[Tb_])
                            elif ci == 14:
                                c.op("act", lambda T_=T_: nc.scalar.activation(out=T_[:], in_=T_[:], func=AF.Sigmoid), r=[Tb_], w=[Tb_])

                        def store(nm, T, Tb):
                            c.dma("pool", RA[nm][0][g0:g0 + 128, :], T[:], r=[Tb], w=[RA[nm][1]])

                        KK, KKb = wk.nxt()
                        SQ, SQb = wk.nxt()
                        c.op("dve", lambda: nc.vector.tensor_tensor(out=KK[:], in0=K_, in1=bc["kk"][0][:], op=ALU.mult),
                             r=[rkvb, bc["kk"][1]], w=[KKb])
                        c.op("pool", lambda: nc.gpsimd.tensor_tensor(out=SQ[:], in0=KK[:], in1=KK[:], op=ALU.mult), r=[KKb], w=[SQb])
                        SS, SSb = sm.nxt()
                        c.op("dve", lambda: nc.vector.tensor_reduce(out=SS[:], in_=SQ[:].rearrange("p (h k) -> p h k", h=8),
                                                                    axis=AX.X, op=ALU.add), r=[SQb], w=[SSb])
                        c.op("act", lambda: nc.scalar.activation(out=SS[:], in_=SS[:], func=AF.Sqrt, bias=eps12[:, 0:1]),
                             r=[SSb, eps12b], w=[SSb])
                        c.op("dve", lambda: nc.vector.reciprocal(out=SS[:], in_=SS[:]), r=[SSb], w=[SSb])
                        KAP, KAPb = wk.nxt()
                        c.op("dve", lambda: nc.vector.tensor_tensor(
                            out=KAP[:].rearrange("p (h k) -> p h k", h=8), in0=KK[:].rearrange("p (h k) -> p h k", h=8),
                            in1=SS[:, :].unsqueeze(2).to_broadcast([128, 8, 64]), op=ALU.mult), r=[KKb, SSb], w=[KAPb])
                        KAN, KANb = wk.nxt()
                        c.op("act", lambda: nc.scalar.mul(out=KAN[:], in_=KAP[:], mul=-1.0), r=[KAPb], w=[KANb])
                        store("kap", KAN, KANb)
                        RR, RRb = wk.nxt()
                        c.op("pool", lambda: nc.gpsimd.tensor_tensor(out=RR[:], in0=R_, in1=bc["rk"][0][:], op=ALU.mult),
                             r=[rkvb, bc["rk"][1]], w=[RRb])
                        bss = []
                        for d in range(2):
                            lo_, hi_ = d * 64, (d + 1) * 64
                            pw, pwb = psr.nxt()
                            c.op("pe", lambda pw=pw, lo_=lo_, hi_=hi_: nc.tensor.matmul(
                                pw[:], lhsT=lt[0][0][lo_:hi_, :], rhs=w2[lo_:hi_, :], start=True, stop=True),
                                r=[lt[0][1], w2b], w=[pwb])
                            pa, pab = psr.nxt()
                            c.op("pe", lambda pa=pa, lo_=lo_, hi_=hi_: nc.tensor.matmul(
                                pa[:], lhsT=lt[1][0][lo_:hi_, :], rhs=a2[lo_:hi_, :], start=True, stop=True),
                                r=[lt[1][1], a2b], w=[pab])
                            Wt, Wtb = wk.nxt()
                            c.op("dve", lambda Wt=Wt, pw=pw, d=d: nc.vector.tensor_tensor(out=Wt[:], in0=pw[:], in1=bc["w0%d" % d][0][:], op=ALU.add),
                                 r=[pwb, bc["w0%d" % d][1]], w=[Wtb])
                            c.op("act", lambda Wt=Wt: nc.scalar.activation(out=Wt[:], in_=Wt[:], func=AF.Sigmoid), r=[Wtb], w=[Wtb])
                            c.op("act", lambda Wt=Wt: nc.scalar.mul(out=Wt[:], in_=Wt[:], mul=-math.exp(-0.5)), r=[Wtb], w=[Wtb])
                            store("w%d" % d, Wt, Wtb)
                            At, Atb = wk.nxt()
                            c.op("dve", lambda At=At, pa=pa, d=d: nc.vector.tensor_tensor(out=At[:], in0=pa[:], in1=bc["a0%d" % d][0][:], op=ALU.add),
                                 r=[pab, bc["a0%d" % d][1]], w=[Atb])
                            c.op("act", lambda At=At: nc.scalar.activation(out=At[:], in_=At[:], func=AF.Sigmoid), r=[Atb], w=[Atb])
                            Bt, Btb = wk.nxt()
                            c.op("pool", lambda Bt=Bt, At=At: nc.gpsimd.tensor_tensor(out=Bt[:], in0=KAP[:], in1=At[:], op=ALU.mult),
                                 r=[KAPb, Atb], w=[Btb])
                            store("b%d" % d, Bt, Btb)
                            Kd, Kdb = wk.nxt()
                            c.op("dve", lambda Kd=Kd, At=At: nc.vector.scalar_tensor_tensor(
                                out=Kd[:], in0=At[:], scalar=-1.0, in1=bc["ka"][0][:], op0=ALU.add, op1=ALU.mult),
                                r=[Atb, bc["ka"][1]], w=[Kdb])
                            c.op("dve", lambda Kd=Kd: nc.vector.scalar_tensor_tensor(
                                out=Kd[:], in0=Kd[:], scalar=1.0, in1=K_, op0=ALU.add, op1=ALU.mult), r=[Kdb, rkvb], w=[Kdb])
                            store("kd%d" % d, Kd, Kdb)
                            c.op("pool", lambda At=At, Kd=Kd: nc.gpsimd.tensor_tensor(out=At[:], in0=RR[:], in1=Kd[:], op=ALU.mult),
                                 r=[RRb, Kdb, Atb], w=[Atb])
                            BS, BSb = sm.nxt()
                            c.op("dve", lambda BS=BS, At=At: nc.vector.tensor_reduce(
                                out=BS[:], in_=At[:].rearrange("p (h k) -> p h k", h=8), axis=AX.X, op=ALU.add), r=[Atb], w=[BSb])
                            bss.append((BS, BSb))
                        c.op("dve", lambda: nc.vector.tensor_tensor(out=bss[0][0][:], in0=bss[0][0][:], in1=bss[1][0][:], op=ALU.add),
                             r=[bss[0][1], bss[1][1]], w=[bss[0][1]])
                        BO, BOb = wk.nxt()
                        c.op("dve", lambda: nc.vector.tensor_tensor(
                            out=BO[:].rearrange("p (h k) -> p h k", h=8), in0=V_.rearrange("p (h k) -> p h k", h=8),
                            in1=bss[0][0][:, :].unsqueeze(2).to_broadcast([128, 8, 64]), op=ALU.mult), r=[rkvb, bss[0][1]], w=[BOb])
                        store("bon", BO, BOb)
                        Rc, Rcb = wk.nxt()
                        c.op("act", lambda: nc.scalar.copy(out=Rc[:], in_=R_), r=[rkvb], w=[Rcb])
                        store("r", Rc, Rcb)
                        pg, pgb = psr.nxt()
                        c.op("pe", lambda: nc.tensor.matmul(pg[:], lhsT=lt[2][0][:, :], rhs=g2[:, :], start=True, stop=True),
                             r=[lt[2][1], g2b], w=[pgb])
                        Gt, Gtb = wk.nxt()
                        c.op("act", lambda: nc.scalar.copy(out=Gt[:], in_=pg[:]), r=[pgb], w=[Gtb])
                        store("g", Gt, Gtb)
                        Vc, Vcb = wk.nxt()
                        c.op("act", lambda: nc.scalar.copy(out=Vc[:], in_=V_), r=[rkvb], w=[Vcb])
                        store("v", Vc, Vcb)
                c.barrier()
            c.mute = m0 or ("r2" not in parts)
            G = 4
            with ExitStack() as st:
                KC = {}
                for nm, w_ in (("tri", 128), ("blk", 128), ("selp", 4), ("mbd", 128), ("mh", 512), ("cm", 512),
                               ("msi", 256), ("msl", 128), ("trib", 128), ("msib", 256), ("mslb", 128)):
                    t, b = c.sb(st, "rk_" + nm, (128, w_))
                    c.dma("sp", t[:], I["k_" + nm][:, :], w=[b])
                    KC[nm] = (t, b)
                TRI, TRIb = KC["tri"]; BLK, BLKb = KC["blk"]; SELP, SELPb = KC["selp"]; MBD, MBDb = KC["mbd"]
                MH, MHb = KC["mh"]; CM, CMb = KC["cm"]
                QN = ["lam", "kap", "b", "kd", "r", "v"]
                class PV:
                    def __init__(self, t, off):
                        self.t, self.off = t, off

                    def __getitem__(self, key):
                        rows, cols = key
                        return self.t[rows, self.off + cols.start:self.off + cols.stop]

                class RSet:
                    pass
                BANKS = [(PV(PS[j][0], 0), PS[j][1]) for j in range(8)]
                PREP_RS = []
                for k in range(4):
                    R_ = RSet()
                    R_.w64 = Rot([c.sb(st, "rw64", (128, 64)) for _ in range(9)])
                    R_.w128 = Rot([c.sb(st, "rw128", (128, 128)) for _ in range(5)])
                    R_.w4 = Rot([c.sb(st, "rw4", (128, 4)) for _ in range(1)])
                    R_.r64 = Rot([c.sb(st, "rr64", (128, 64), F32R) for _ in range(1)])
                    R_.r128 = Rot([c.sb(st, "rr128", (128, 128), F32R) for _ in range(11)])
                    R_.r256 = Rot([c.sb(st, "rr256", (128, 256), F32R) for _ in range(3)])
                    R_.r512 = Rot([c.sb(st, "rr512", (128, 512), F32R) for _ in range(2)])
                    PREP_RS.append(R_)
                CHAIN_RS = []
                for k in range(2):
                    R_ = RSet()
                    R_.w256 = Rot([c.sb(st, "rcw256", (128, 256)) for _ in range(6)])
                    R_.w64 = Rot([c.sb(st, "rcw64", (128, 64)) for _ in range(4)])
                    R_.r64 = Rot([c.sb(st, "rcr64", (128, 64), F32R) for _ in range(4)])
                    R_.r256 = Rot([c.sb(st, "rcr256", (128, 256), F32R) for _ in range(2)])
                    R_.accA = (PV(PS[2 * k][0], 0), PS[2 * k][1])
                    R_.accB = (PV(PS[2 * k + 1][0], 0), PS[2 * k + 1][1])
                    R_.pu = (PV(PS[2 * k + 1][0], 256), PS[2 * k + 1][1])
                    CHAIN_RS.append(R_)

                def f(ap):
                    return ap.bitcast(F32)
                snat = Rot([c.sb(st, "rsnat", (64, 4, 128)) for _ in range(2)])

                def bc2(ap64):
                    return ap64.unsqueeze(1).to_broadcast([128, 2, 64])

                def run_streams(streams, nchunk):
                    sst = ExitStack()
                    for sm_ in streams:
                        sm_["ST"] = c.sb(sst, "rST", (128, 4, 64))
                        sm_["ld"] = {q: Rot([c.sb(sst, "rld_" + q, (128, G, 64)) for _ in range(2)]) for q in QN}
                        sm_["yg"] = Rot([c.sb(sst, "ryg", (128, G, 64)) for _ in range(2)])
                        ST, STb = sm_["ST"]
                        d = sm_["d"]
                        if sm_["is_s"]:
                            Sn, Snb = snat.nxt()
                            src = I["st"][l, d].rearrange("h v k -> v h k")
                            c.dma("sp", Sn[:].rearrange("v p x -> v (p x)").rearrange("v (h k) -> v h k", h=8), src, w=[Snb])
                            pt, pb = psr.nxt()
                            for p in range(4):
                                c.op("pe", lambda p=p, pt=pt, Sn=Sn: nc.tensor.transpose(
                                    out=pt[:, p * 64:(p + 1) * 64], in_=Sn[:, p, :], identity=ident[0:64, 0:64]),
                                    r=[Snb, kb], w=[pb])
                            c.op("dve", lambda pt=pt, ST=ST: nc.vector.tensor_copy(out=ST[:].rearrange("q p v -> q (p v)"), in_=pt[:, 0:256]),
                                 r=[pb], w=[STb])
                        else:
                            c.op("dve", lambda ST=ST: nc.vector.memset(ST[:], 0.0), w=[STb])

                    def gap(sm_, arr, g):
                        d = sm_["d"]
                        s0, L = sm_["s0"], sm_["L"]
                        t_ = RA[arr][0].tensor
                        if d == 0:
                            return bass.AP(tensor=t_, offset=(s0 + g * 16 * G) * 512, ap=[[64, 128], [8192, G], [1, 64]])
                        return bass.AP(tensor=t_, offset=(s0 + L - (g + 1) * 16 * G) * 512, ap=[[64, 128], [8192, G], [1, 64]])

                    def prep(sm_, ci, E, RS):
                        d = sm_["d"]
                        g, cg = divmod(ci, G)
                        if cg == 0:
                            sm_["cur"] = {}
                            for q in QN:
                                T_, Tb_ = sm_["ld"][q].nxt()
                                arr = {"lam": "w%d" % d, "kap": "kap", "b": "b%d" % d, "kd": "kd%d" % d, "r": "r", "v": "v"}[q]
                                E.dma("sp", T_[:], gap(sm_, arr, g), r=[RA[arr][1]], w=[Tb_])
                                sm_["cur"][q] = (T_, Tb_)
                            sm_["ycur"] = sm_["yg"].nxt()
                        cur = sm_["cur"]
                        sl = cg if d == 0 else G - 1 - cg
                        TRI, TRIb = KC["tri"] if d == 0 else KC["trib"]
                        MSI, MSIb = KC["msi"] if d == 0 else KC["msib"]
                        MSL, MSLb = KC["msl"] if d == 0 else KC["mslb"]
                        lam, lamb = cur["lam"][0][:, sl, :], cur["lam"][1]
                        P = {}
                        import os
                        STOP = float(os.environ.get("DBG_PREP_STOP", "99"))
                        if STOP <= 0:
                            return P
                        pc, pcb = RS.ph.nxt()
                        E.op("pe", lambda: nc.tensor.matmul(pc[:, 0:64], lhsT=TRI[:], rhs=lam, start=True, stop=True), r=[TRIb, lamb], w=[pcb])
                        E.op("pe", lambda: nc.tensor.matmul(pc[:, 64:128], lhsT=BLK[:], rhs=lam, start=True, stop=True), r=[BLKb, lamb], w=[pcb])
                        LT, LTb = RS.w64.nxt()
                        E.op("act", lambda: nc.scalar.copy(out=LT[:], in_=pc[:, 64:128]), r=[pcb], w=[LTb])
                        dd, ddb = RS.w64.nxt()
                        E.op("dve", lambda: nc.vector.tensor_tensor(out=dd[:], in0=pc[:, 0:64], in1=LT[:], op=ALU.subtract), r=[pcb, LTb], w=[ddb])
                        Eq, Eqb = RS.w64.nxt()
                        Eb, Ebb = RS.w64.nxt()
                        Ea, Eab = RS.w64.nxt()
                        E.op("act", lambda: nc.scalar.activation(out=Eq[:], in_=dd[:], func=AF.Exp), r=[ddb], w=[Eqb])
                        E.op("act", lambda: nc.scalar.activation(out=Eb[:], in_=dd[:], func=AF.Exp, scale=-1.0), r=[ddb], w=[Ebb])
                        E.op("pool", lambda: nc.gpsimd.tensor_tensor(out=Ea[:], in0=dd[:], in1=lam, op=ALU.subtract), r=[ddb, lamb], w=[Eab])
                        E.op("act", lambda: nc.scalar.activation(out=Ea[:], in_=Ea[:], func=AF.Exp), r=[Eab], w=[Eab])
                        if STOP <= 1:
                            return P
                        bds = {}
                        for nm, q, Eexp, Ebuf in (("q", "r", Eq, Eqb), ("a", "kap", Ea, Eab), ("b", "b", Eb, Ebb), ("k", "kd", Eb, Ebb)):
                            X, Xb = RS.w64.nxt()
                            src, srcb = cur[q][0][:, sl, :], cur[q][1]
                            E.op("pool", lambda X=X, src=src, Eexp=Eexp: nc.gpsimd.tensor_tensor(out=X[:], in0=src, in1=Eexp[:], op=ALU.mult),
                                 r=[srcb, Ebuf], w=[Xb])
                            Bd, Bdb = RS.w128.nxt()
                            eng = "dve" if nm in ("q", "b") else "pool"
                            fn = nc.vector.tensor_tensor if eng == "dve" else nc.gpsimd.tensor_tensor
                            E.op(eng, lambda Bd=Bd, X=X, fn=fn: fn(out=Bd[:].rearrange("p (a b) -> p a b", a=2), in0=bc2(X[:]),
                                                                   in1=MBD[:].rearrange("p (a b) -> p a b", a=2), op=ALU.mult),
                                 r=[Xb, MBDb], w=[Bdb])
                            bds[nm] = (Bd, Bdb)
                            if nm in ("b", "k"):
                                Ex, Exb = RS.r512.nxt()
                                E.op("pool", lambda Ex=Ex, X=X: nc.gpsimd.tensor_tensor(
                                    out=Ex[:].rearrange("p (a b) -> p a b", a=8), in0=X[:].unsqueeze(1).to_broadcast([128, 8, 64]),
                                    in1=MH[:].rearrange("p (a b) -> p a b", a=8), op=ALU.mult), r=[Xb, MHb], w=[Exb])
                                P[nm + "e"] = (Ex, Exb)
                        if STOP <= 2:
                            return P
                        Lb, Lbb = RS.w128.nxt()
                        E.op("dve", lambda: nc.vector.tensor_tensor(out=Lb[:].rearrange("p (a b) -> p a b", a=2), in0=bc2(lam),
                                                                    in1=MBD[:].rearrange("p (a b) -> p a b", a=2), op=ALU.mult),
                             r=[lamb, MBDb], w=[Lbb])
                        pp, ppb = RS.ph.nxt()
                        E.op("pe", lambda: nc.tensor.matmul(pp[:, 0:4], lhsT=Lb[:], rhs=SELP[:], start=True, stop=True), r=[Lbb, SELPb], w=[ppb])
                        PT, PTb = RS.w4.nxt()
                        E.op("act", lambda: nc.scalar.activation(out=PT[:], in_=pp[:, 0:4], func=AF.Exp), r=[ppb], w=[PTb])
                        P["PT"] = (PT, PTb)
                        if STOP <= 3:
                            return P
                        AQ, AQb = RS.r256.nxt()
                        BT, BTb = RS.r128.nxt()
                        KT, KTb = RS.r128.nxt()
                        for nm, dst, dstb in (("a", AQ[:, 0:128], AQb), ("q", AQ[:, 128:256], AQb), ("b", BT[:], BTb), ("k", KT[:], KTb)):
                            pt, pb = RS.ph.nxt()
                            Bd, Bdb = bds[nm]
                            E.op("pe", lambda pt=pt, Bd=Bd: nc.tensor.transpose(out=pt[:, 0:128], in_=Bd[:], identity=ident[:]),
                                 r=[Bdb, kb], w=[pb])
                            if STOP > 3.4:
                                E.op("act", lambda pt=pt, dst=dst: nc.scalar.copy(out=dst, in_=pt[:, 0:128]), r=[pb], w=[dstb])
                        P["AQ"] = (AQ, AQb)
                        if STOP <= 4:
                            return P
                        NB, NBb = RS.r256.nxt()
                        NK_, NKb = RS.r256.nxt()
                        for lt_, ltb_, dst, dstb in ((BT, BTb, NB, NBb), (KT, KTb, NK_, NKb)):
                            pt, pb = RS.ph.nxt()
                            E.op("pe", lambda pt=pt, lt_=lt_: nc.tensor.matmul(pt[:, 0:256], lhsT=lt_[:], rhs=AQ[:], start=True, stop=True),
                                 r=[ltb_, AQb], w=[pb])
                            E.op("dve", lambda pt=pt, dst=dst: nc.vector.tensor_tensor(out=dst[:], in0=pt[:, 0:256], in1=MSI[:], op=ALU.mult),
                                 r=[pb, MSIb], w=[dstb])
                        P["NB"] = (NB, NBb)
                        P["NK"] = (NK_, NKb)
                        N1, N1b = RS.r128.nxt()
                        pt, pb = RS.ph.nxt()
                        E.op("pe", lambda pt=pt: nc.tensor.matmul(pt[:, 0:128], lhsT=AQ[:, 0:128], rhs=BT[:], start=True, stop=True),
                             r=[AQb, BTb], w=[pb])
                        E.op("dve", lambda pt=pt: nc.vector.tensor_tensor(out=N1[:], in0=pt[:, 0:128], in1=MSL[:], op=ALU.mult),
                             r=[pb, MSLb], w=[N1b])
                        N1T = NB[:, 0:128]
                        if STOP <= 5:
                            return P

                        def mmev(lhsT, lb_, rhs, rb_, addt=None, addb=None):
                            pt, pb = RS.ph.nxt()
                            E.op("pe", lambda: nc.tensor.matmul(pt[:, 0:128], lhsT=lhsT, rhs=rhs, start=True, stop=True), r=[lb_, rb_], w=[pb])
                            O_, Ob_ = RS.r128.nxt()
                            if addt is None:
                                E.op("act", lambda: nc.scalar.copy(out=O_[:], in_=pt[:, 0:128]), r=[pb], w=[Ob_])
                            else:
                                E.op("dve", lambda: nc.vector.tensor_tensor(out=O_[:], in0=pt[:, 0:128], in1=f(addt), op=ALU.add),
                                     r=[pb, addb], w=[Ob_])
                            return O_, Ob_
                        N2, N2b = mmev(N1T, NBb, N1[:], N1b)
                        N2T, N2Tb = mmev(N1[:], N1b, N1T, NBb)
                        N4, N4b = mmev(N2T[:], N2Tb, N2[:], N2b)
                        N4T, N4Tb = mmev(N2[:], N2b, N2T[:], N2Tb)
                        Z0, Z0b = mmev(N4[:], N4b, N4T[:], N4Tb, ident[:], kb)
                        R1_, R1b = mmev(N4[:], N4b, Z0[:], Z0b, Z0[:], Z0b)
                        R2_, R2b = mmev(N2[:], N2b, R1_[:], R1b, R1_[:], R1b)
                        R3_, R3b = mmev(N1[:], N1b, R2_[:], R2b, R2_[:], R2b)
                        P["R3"] = (R3_, R3b)
                        Vr, Vrb = RS.r64.nxt()
                        E.op("act", lambda: nc.scalar.copy(out=Vr[:], in_=cur["v"][0][:, sl, :]), r=[cur["v"][1]], w=[Vrb])
                        P["v"] = (Vr[:], Vrb)
                        P["cg"] = cg
                        P["sl"] = sl
                        P["g"] = g
                        return P

                    def chain(sm_, ci, P, E, RS):
                        ST, STb = sm_["ST"]
                        PT, PTb = P["PT"]
                        vts, vb = P["v"]
                        NB, NBb = P["NB"]
                        NK_, NKb = P["NK"]
                        AQ, AQb = P["AQ"]
                        SELP, SELPb = KC["selp"]
                        selbc = SELP[:, :].unsqueeze(2).to_broadcast([128, 4, 64])
                        D0, D0b = RS.r256.nxt()
                        D0f, D0fb = RS.w256.nxt()
                        E.op("dve", lambda: nc.vector.tensor_tensor(
                            out=D0[:].rearrange("q (p v) -> q p v", p=4), in0=ST[:], in1=PT[:, :].unsqueeze(2).to_broadcast([128, 4, 64]),
                            op=ALU.mult), r=[STb, PTb], w=[D0b])
                        E.op("pool", lambda: nc.gpsimd.tensor_tensor(
                            out=D0f[:].rearrange("q (p v) -> q p v", p=4), in0=ST[:], in1=PT[:, :].unsqueeze(2).to_broadcast([128, 4, 64]),
                            op=ALU.mult), r=[STb, PTb], w=[D0fb])
                        pw, pwb = RS.accA
                        E.op("pe", lambda: nc.tensor.matmul(pw[:, 0:64], lhsT=NK_[:, 0:128], rhs=vts, start=True, stop=True),
                             r=[NKb, vb], w=[pwb])
                        E.op("pe", lambda: nc.tensor.matmul(pw[:, 256:512], lhsT=AQ[:, 0:128], rhs=D0[:], start=True, stop=True),
                             r=[AQb, D0b], w=[pwb])
                        TA, TAb = RS.w256.nxt()
                        E.op("dve", lambda: nc.vector.tensor_tensor(out=TA[:].rearrange("q (p v) -> q p v", p=4),
                                                                    in0=pw[:, 256:512].rearrange("q (p v) -> q p v", p=4), in1=selbc, op=ALU.mult),
                             r=[pwb, SELPb], w=[TAb])
                        RA_, RAb_ = RS.w64.nxt()
                        E.op("dve", lambda: nc.vector.tensor_reduce(out=RA_[:], in_=TA[:].rearrange("q (p v) -> q v p", p=4), axis=AX.X, op=ALU.add),
                             r=[TAb], w=[RAb_])
                        W0, W0b = RS.r64.nxt()
                        E.op("dve", lambda: nc.vector.tensor_tensor(out=W0[:], in0=pw[:, 0:64], in1=RA_[:], op=ALU.add), r=[pwb, RAb_], w=[W0b])
                        pu, pub = RS.pu
                        R3_, R3b = P["R3"]
                        E.op("pe", lambda: nc.tensor.matmul(pu[:, 0:64], lhsT=R3_[:], rhs=W0[:], start=True, stop=True), r=[R3b, W0b], w=[pub])
                        U, Ub = RS.r64.nxt()
                        E.op("act", lambda: nc.scalar.copy(out=U[:], in_=pu[:, 0:64]), r=[pub], w=[Ub])
                        pst, pstb = RS.accB
                        be, beb = P["be"]
                        ke, keb = P["ke"]
                        for p in range(4):
                            E.op("pe", lambda p=p: nc.tensor.matmul(pst[:, p * 64:(p + 1) * 64], lhsT=be[:, p * 128:(p + 1) * 128], rhs=U[:],
                                                                    start=True, stop=False), r=[beb, Ub], w=[pstb])
                            E.op("pe", lambda p=p: nc.tensor.matmul(pst[:, p * 64:(p + 1) * 64], lhsT=ke[:, p * 128:(p + 1) * 128], rhs=vts,
                                                                    start=False, stop=True), r=[keb, vb], w=[pstb])
                        E.op("dve", lambda: nc.vector.tensor_tensor(out=ST[:].rearrange("q p v -> q (p v)"), in0=pst[:, 0:256], in1=D0f[:],
                                                                    op=ALU.add), r=[pstb, D0fb], w=[STb])
                        py, pyb = RS.accA
                        E.op("pe", lambda: nc.tensor.matmul(py[:, 0:64], lhsT=NB[:, 128:256], rhs=U[:], start=True, stop=False),
                             r=[NBb, Ub], w=[pyb])
                        E.op("pe", lambda: nc.tensor.matmul(py[:, 0:64], lhsT=NK_[:, 128:256], rhs=vts, start=False, stop=True),
                             r=[NKb, vb], w=[pyb])
                        E.op("pe", lambda: nc.tensor.matmul(py[:, 256:512], lhsT=AQ[:, 128:256], rhs=D0[:], start=True, stop=True),
                             r=[AQb, D0b], w=[pyb])
                        TQ, TQb = RS.w256.nxt()
                        E.op("dve", lambda: nc.vector.tensor_tensor(out=TQ[:].rearrange("q (p v) -> q p v", p=4),
                                                                    in0=py[:, 256:512].rearrange("q (p v) -> q p v", p=4), in1=selbc, op=ALU.mult),
                             r=[pyb, SELPb], w=[TQb])
                        RQ_, RQb_ = RS.w64.nxt()
                        E.op("dve", lambda: nc.vector.tensor_reduce(out=RQ_[:], in_=TQ[:].rearrange("q (p v) -> q v p", p=4), axis=AX.X, op=ALU.add),
                             r=[TQb], w=[RQb_])
                        Yg, Ygb = sm_["ycur"]
                        cg = P["cg"]
                        sl = P["sl"]
                        E.op("dve", lambda: nc.vector.tensor_tensor(out=Yg[:, sl, :], in0=py[:, 0:64], in1=RQ_[:], op=ALU.add),
                             r=[pyb, RQb_], w=[Ygb])
                        if cg == G - 1:
                            arr = "y%d" % sm_["d"]
                            E.dma("pool", gap(sm_, arr, P["g"]), Yg[:], r=[Ygb], w=[RA[arr][1]])

                    class TL:
                        def __init__(self):
                            self.items = []

                        def op(self, e, fn, r=(), w=()):
                            self.items.append((0, e, fn, r, w))

                        def dma(self, q, out, in_, r=(), w=()):
                            self.items.append((1, q, (out, in_), r, w))

                    def run_lists(lists):
                        idx = [0] * len(lists)
                        nmax = max([len(L_.items) for L_ in lists] + [1])
                        for step in range(1, nmax + 1):
                            for k, L_ in enumerate(lists):
                                tgt = (len(L_.items) * step + nmax - 1) // nmax
                                while idx[k] < tgt:
                                    kind, e, fn, r, w = L_.items[idx[k]]
                                    idx[k] += 1
                                    if kind == 0:
                                        c.op(e, fn, r=r, w=w)
                                    else:
                                        c.dma(e, fn[0], fn[1], r=r, w=w)

                    look = 1 if len(streams) <= 2 else 0
                    Ps = {}

                    def mk_prep(ci):
                        out = []
                        for k, sm_ in enumerate(streams):
                            E = TL()
                            if look:
                                RS = PREP_RS[k * 2 + (ci % 2)]
                                RS.ph = Rot(BANKS[4 + 2 * k:6 + 2 * k])
                            else:
                                RS = PREP_RS[k]
                                RS.ph = Rot(BANKS[2 * k:2 * k + 2])
                            Ps[(k, ci)] = prep(sm_, ci, E, RS)
                            out.append(E)
                        return out

                    def mk_chain(ci, ks):
                        out = []
                        for k in ks:
                            E = TL()
                            chain(streams[k], ci, Ps.pop((k, ci)), E, CHAIN_RS[k % 2])
                            out.append(E)
                        return out
                    if look:
                        run_lists(mk_prep(0))
                        for ci in range(nchunk):
                            ls = mk_chain(ci, range(len(streams)))
                            if ci + 1 < nchunk:
                                ls = ls + mk_prep(ci + 1)
                            run_lists(ls)
                            if castq and ci % 2 == 0:
                                castq.pop(0)()
                        while castq:
                            castq.pop(0)()
                    else:
                        for ci in range(nchunk):
                            run_lists(mk_prep(ci))
                            for k0 in range(0, len(streams), 2):
                                run_lists(mk_chain(ci, range(k0, min(k0 + 2, len(streams)))))
                    for sm_ in streams:
                        if sm_["is_s"] or os.environ.get("DBG_NOFIN"):
                            continue
                        ST, STb = sm_["ST"]
                        d = sm_["d"]
                        Sn, Snb = snat.nxt()
                        for p in range(4):
                            pt, pb = psr.nxt()
                            c.op("pe", lambda p=p, pt=pt, ST=ST: nc.tensor.transpose(out=pt[0:64, 0:128], in_=ST[:, p, :], identity=ident[:]),
                                 r=[STb, kb], w=[pb])
                            c.op("dve", lambda p=p, pt=pt, Sn=Sn: nc.vector.tensor_copy(out=Sn[:, p, :], in_=pt[0:64, 0:128]), r=[pb], w=[Snb])
                        pi = sm_["pi"]
                        dst = O["ns"][pi, l, d].rearrange("h v k -> v h k")
                        c.dma("pool", dst, Sn[:].rearrange("v p x -> v (p x)").rearrange("v (h k) -> v h k", h=8), r=[Snb], w=[out_b])
                    c.barrier()
                    sst.close()

                import os
                pstreams = [dict(s0=s0, L=L, d=d, pi=pi, is_s=False) for (s0, L, is_s, pi) in SEQS[0:2] for d in range(2)]
                if not os.environ.get("DBG_SKIP_PROMPT"):
                    run_streams(pstreams[0:int(os.environ.get("DBG_NSTREAM", "4"))], LP // 16)
                s0, L, _, _ = SEQS[2]
                if not os.environ.get("DBG_SKIP_SAMPLE"):
                    run_streams([dict(s0=s0, L=L, d=d, pi=0, is_s=True) for d in range(2)], LS // 16)
                c.barrier()
            c.mute = m0 or ("r3" not in parts)
            with ExitStack() as st:
                gng, gngb = c.sb(st, "rgng", (128, 512))
                gnb, gnbb = c.sb(st, "rgnb", (128, 512))
                c.dma("sp", gng[:], I["rwkv_gn_g"][l].partition_broadcast(128), w=[gngb])
                c.dma("sp", gnb[:], I["rwkv_gn_b"][l].partition_broadcast(128), w=[gnbb])
                epsg, epsgb = c.sb(st, "repsg", (128, 1))
                c.op("dve", lambda: nc.vector.memset(epsg[:], 64e-5), w=[epsgb])
                yi = Rot([c.sb(st, "ryi", (128, 4, 128)) for _ in range(4)])
                wk = Rot([c.sb(st, "rwk3", (128, 512)) for _ in range(18)])
                sm = Rot([c.sb(st, "rsm3", (128, 8)) for _ in range(4)])
                mo = Rot([c.sb(st, "rmo", (128, 4, 128), BF16) for _ in range(2)])
                for g0 in range(0, NTOK, 128):
                    py, pyb = wk.nxt()
                    Y1, Y1b = wk.nxt()
                    c.dma("sp", py[:], RA["y0"][0][g0:g0 + 128, :], r=[RA["y0"][1]], w=[pyb])
                    c.dma("sp", Y1[:], RA["y1"][0][g0:g0 + 128, :], r=[RA["y1"][1]], w=[Y1b])
                    c.op("pool", lambda: nc.gpsimd.tensor_tensor(out=py[:], in0=py[:], in1=Y1[:], op=ALU.add), r=[pyb, Y1b], w=[pyb])
                    Yc, Ycb = wk.nxt()
                    MS, MSb = sm.nxt()
                    c.op("dve", lambda: nc.vector.tensor_reduce(out=MS[:], in_=py[:].rearrange("p (h k) -> p h k", h=8), axis=AX.X, op=ALU.add),
                         r=[pyb], w=[MSb])
                    c.op("dve", lambda: nc.vector.tensor_scalar(out=MS[:], in0=MS[:], scalar1=-1.0 / 64, scalar2=None, op0=ALU.mult),
                         r=[MSb], w=[MSb])
                    c.op("dve", lambda: nc.vector.tensor_tensor(
                        out=Yc[:].rearrange("p (h k) -> p h k", h=8), in0=py[:].rearrange("p (h k) -> p h k", h=8),
                        in1=MS[:, :].unsqueeze(2).to_broadcast([128, 8, 64]), op=ALU.add), r=[pyb, MSb], w=[Ycb])
                    SQ, SQb = wk.nxt()
                    c.op("act", lambda: nc.scalar.activation(out=SQ[:], in_=Yc[:], func=AF.Square), r=[Ycb], w=[SQb])
                    VS, VSb = sm.nxt()
                    c.op("dve", lambda: nc.vector.tensor_reduce(out=VS[:], in_=SQ[:].rearrange("p (h k) -> p h k", h=8), axis=AX.X, op=ALU.add),
                         r=[SQb], w=[VSb])
                    c.op("act", lambda: nc.scalar.activation(out=VS[:], in_=VS[:], func=AF.Sqrt, scale=1.0 / 64, bias=epsg[:, 0:1]),
                         r=[VSb, epsgb], w=[VSb])
                    c.op("dve", lambda: nc.vector.reciprocal(out=VS[:], in_=VS[:]), r=[VSb], w=[VSb])
                    c.op("dve", lambda: nc.vector.tensor_tensor(
                        out=Yc[:].rearrange("p (h k) -> p h k", h=8), in0=Yc[:].rearrange("p (h k) -> p h k", h=8),
                        in1=VS[:, :].unsqueeze(2).to_broadcast([128, 8, 64]), op=ALU.mult), r=[Ycb, VSb], w=[Ycb])
                    c.op("dve", lambda: nc.vector.tensor_tensor(out=Yc[:], in0=Yc[:], in1=gng[:], op=ALU.mult), r=[Ycb, gngb], w=[Ycb])
                    c.op("pool", lambda: nc.gpsimd.tensor_tensor(out=Yc[:], in0=Yc[:], in1=gnb[:], op=ALU.add), r=[Ycb, gnbb], w=[Ycb])
                    BO, BOb = wk.nxt()
                    Gt, Gtb = wk.nxt()
                    c.dma("sp", BO[:], RA["bon"][0][g0:g0 + 128, :], r=[RA["bon"][1]], w=[BOb])
                    c.dma("sp", Gt[:], RA["g"][0][g0:g0 + 128, :], r=[RA["g"][1]], w=[Gtb])
                    c.op("dve", lambda: nc.vector.tensor_tensor(out=Yc[:], in0=Yc[:], in1=BO[:], op=ALU.add), r=[Ycb, BOb], w=[Ycb])
                    c.op("dve", lambda: nc.vector.tensor_tensor(out=Yc[:], in0=Yc[:], in1=Gt[:], op=ALU.mult), r=[Ycb, Gtb], w=[Ycb])
                    po, pob = psr.nxt()
                    for cc in range(4):
                        c.op("pe", lambda cc=cc: nc.tensor.transpose(out=po[:, cc * 128:(cc + 1) * 128], in_=Yc[:, cc * 128:(cc + 1) * 128],
                                                                     identity=ident[:]), r=[Ycb, kb], w=[pob])
                    Mo, Mob = mo.nxt()
                    c.op("act", lambda: nc.scalar.copy(out=Mo[:].rearrange("p a b -> p (a b)"), in_=po[:]), r=[pob], w=[Mob])
                    c.dma("pool", mixTv[:, 0:4, g0:g0 + 128], Mo[:], r=[Mob], w=[mixT_b])
                c.barrier()
            c.mute = m0

        if dbg == "r2":
            c.mute = False
            stage_rwkv(0, parts=("r2",))
            c.mute = True
        for l in range(DEPTH):
            stage_A(l)
            if l == 1:
                cast_weights(1, ["w_out", "ffn_up", "ffn_down"])
            stage_attn(l)
            stage_hyena(l)
            stage_rwkv(l)
            stage_C1(l)
            stage_C2(l)
        c.mute = False
        c.barrier()
    return nc, hc


_CACHE = {}


def kernel(**inputs):
    if "nc" not in _CACHE:
        _CACHE["nc"] = build()
    nc, hc = _CACHE["nc"]
    f = lambda a: np.ascontiguousarray(np.asarray(a, dtype=np.float32))
    xp = f(inputs["x_prompt"]); xs = f(inputs["x_sample"])
    ck = f(inputs["cache_k"]); cv = f(inputs["cache_v"]); stt = f(inputs["state_rwkv"])
    cc = f(inputs["c"]); cctx = f(inputs["c_ctx"])
    shared = {n: f(inputs[n]) for n in W_NAMES}
    for k, v in hc.items():
        shared["k_" + k] = v
    in_maps = []
    for i in range(8):
        m = dict(shared)
        m["xp"] = xp[2 * i:2 * i + 2].reshape(512, D)
        m["xs"] = xs[i]
        m["ck"] = ck[i].reshape(DEPTH, 256, 256)
        m["cv"] = cv[i].reshape(DEPTH, 256, 256)
        m["st"] = stt[i]
        m["cond"] = np.stack([cctx, cc[i]], 0)
        in_maps.append(m)
    res = run_bass_kernel_spmd(nc, in_maps, core_ids=list(range(8)))
    R = res.results
    yp = np.concatenate([R[i]["yp"].reshape(2, LP, D) for i in range(8)], 0)
    ys = np.stack([R[i]["ys"] for i in range(8)], 0)
    nk = np.concatenate([R[i]["nk"].reshape(2, DEPTH, 256, 4, 64) for i in range(8)], 0)
    nv = np.concatenate([R[i]["nv"].reshape(2, DEPTH, 256, 4, 64) for i in range(8)], 0)
    ns = np.concatenate([R[i]["ns"] for i in range(8)], 0)
    return (yp.astype(np.float32), ys.astype(np.float32), nk.astype(np.float32), nv.astype(np.float32), ns.astype(np.float32))
```
